# Optimizing a Trainium2 kernel written in Bass

```python
import math
import jax, jax.numpy as jnp
from jax import lax
import numpy as np


D_MODEL = 1024
BATCH = 16
SEQ = 4096
DEPTH = 4
DEC_BATCH = 16
DEC_SEQ = 32
PAST_LEN = 4096

CHUNK = 64
D_ATT = D_MODEL // 2
D_SSM = D_MODEL // 2
D_MIX = D_ATT + D_SSM
N_HEADS = D_ATT // 64
HEAD_DIM = 64
SSM_GROUP = 16
N_SSM_GROUPS = D_SSM // SSM_GROUP
STATE_DIM = 64
Q_BLOCK = 128
EPS = 1e-6
D_IN = 4 * D_ATT + 2 * D_SSM

kernel_name = 'hymba_s5_stickbreaking_stream_step'


def _rmsnorm(x, g):
    xf = x.astype(jnp.float32)
    y = xf * lax.rsqrt(jnp.mean(xf * xf, axis=-1, keepdims=True) + EPS) * g.astype(jnp.float32)
    return y.astype(x.dtype)


def _sb_block(q, k, v, q_pos, k_pos):
    z = jnp.einsum('bqhd,bkhd->bhqk', q, k).astype(jnp.float32) * (1.0 / math.sqrt(HEAD_DIM))
    mask = k_pos[None, :] < q_pos[:, None]
    log_rem = jnp.where(mask, jax.nn.log_sigmoid(-z), 0.0)
    after = lax.cumsum(log_rem, axis=3, reverse=True) - log_rem
    w = jnp.where(mask, jnp.exp(jax.nn.log_sigmoid(z) + after), 0.0)
    return jnp.einsum('bhqk,bkhd->bqhd', w.astype(v.dtype), v)


def _sb_prompt(q, k, v):
    B, T, H, Dh = q.shape
    nb = T // Q_BLOCK
    qb = q.reshape(B, nb, Q_BLOCK, H, Dh).transpose(1, 0, 2, 3, 4)
    k_pos = jnp.arange(T)

    def one(args):
        q_blk, i = args
        q_pos = i * Q_BLOCK + jnp.arange(Q_BLOCK)
        return _sb_block(q_blk, k, v, q_pos, k_pos)

    out = lax.map(one, (qb, jnp.arange(nb)))
    return out.transpose(1, 0, 2, 3, 4).reshape(B, T, H * Dh)


def _sb_sample(q, k, v, k_past, v_past):
    B, T, H, Dh = q.shape
    P = k_past.shape[1]
    k_all = jnp.concatenate([k_past.astype(k.dtype), k], axis=1)
    v_all = jnp.concatenate([v_past.astype(v.dtype), v], axis=1)
    q_pos = P + jnp.arange(T)
    k_pos = jnp.arange(P + T)
    return _sb_block(q, k_all, v_all, q_pos, k_pos).reshape(B, T, H * Dh)


def _cmul_combine(e1, e2):
    ar1, ai1, br1, bi1 = e1
    ar2, ai2, br2, bi2 = e2
    ar = ar2 * ar1 - ai2 * ai1
    ai = ar2 * ai1 + ai2 * ar1
    br = ar2 * br1 - ai2 * bi1 + br2
    bi = ar2 * bi1 + ai2 * br1 + bi2
    return (ar, ai, br, bi)


def _ssm_branch(u, h0_re, h0_im, a_re, a_im, log_dt, b_re, b_im, c_re, c_im, d_skip, w_glu, b_glu):
    B, T, _ = u.shape
    f32 = jnp.float32
    a_re = a_re.astype(f32)
    a_im = a_im.astype(f32)
    dt = jnp.exp(log_dt.astype(f32))[:, None]
    mag = jnp.exp(a_re * dt)
    ang = a_im * dt
    ab_re = mag * jnp.cos(ang)
    ab_im = mag * jnp.sin(ang)
    den = a_re * a_re + a_im * a_im
    f_re = ((ab_re - 1.0) * a_re + ab_im * a_im) / den
    f_im = (ab_im * a_re - (ab_re - 1.0) * a_im) / den
    b_re = b_re.astype(f32)
    b_im = b_im.astype(f32)
    bb_re = f_re[..., None] * b_re - f_im[..., None] * b_im
    bb_im = f_re[..., None] * b_im + f_im[..., None] * b_re
    uf = u.astype(f32)
    ug = uf.reshape(B, T, N_SSM_GROUPS, SSM_GROUP)
    bu_re = jnp.einsum('btgc,gpc->tbgp', ug, bb_re)
    bu_im = jnp.einsum('btgc,gpc->tbgp', ug, bb_im)
    if h0_re is not None:
        h0r = h0_re.astype(f32)
        h0i = h0_im.astype(f32)
        bu_re = bu_re.at[0].add(ab_re * h0r - ab_im * h0i)
        bu_im = bu_im.at[0].add(ab_re * h0i + ab_im * h0r)
    a_s_re = jnp.broadcast_to(ab_re, (T, 1, N_SSM_GROUPS, STATE_DIM))
    a_s_im = jnp.broadcast_to(ab_im, (T, 1, N_SSM_GROUPS, STATE_DIM))
    _, _, h_re, h_im = lax.associative_scan(_cmul_combine, (a_s_re, a_s_im, bu_re, bu_im), axis=0)
    y = (jnp.einsum('tbgp,gcp->btgc', h_re, c_re.astype(f32))
         - jnp.einsum('tbgp,gcp->btgc', h_im, c_im.astype(f32)))
    y = y.reshape(B, T, D_SSM) + d_skip.astype(f32) * uf
    y = jax.nn.gelu(y)
    y = y * jax.nn.sigmoid(y @ w_glu.astype(f32) + b_glu.astype(f32))
    return y.astype(u.dtype), h_re[-1], h_im[-1]


def _layer(x, k_past, v_past, h0_re, h0_im, ln_g, w_in, a_re, a_im, log_dt, b_re, b_im,
           c_re, c_im, d_skip, w_glu, b_glu, g_att, g_ssm, w_out):
    B, T, _ = x.shape
    hn = _rmsnorm(x, ln_g)
    proj = hn @ w_in
    q, k, v, ga, u, gs = jnp.split(
        proj, [D_ATT, 2 * D_ATT, 3 * D_ATT, 4 * D_ATT, 4 * D_ATT + D_SSM], axis=-1)
    q = q.reshape(B, T, N_HEADS, HEAD_DIM)
    k = k.reshape(B, T, N_HEADS, HEAD_DIM)
    v = v.reshape(B, T, N_HEADS, HEAD_DIM)
    if k_past is None:
        att = _sb_prompt(q, k, v)
    else:
        att = _sb_sample(q, k, v, k_past, v_past)
    ssm_y, h_re, h_im = _ssm_branch(u, h0_re, h0_im, a_re, a_im, log_dt, b_re, b_im,
                                     c_re, c_im, d_skip, w_glu, b_glu)
    att_o = _rmsnorm(att, g_att) * jax.nn.silu(ga)
    ssm_o = _rmsnorm(ssm_y, g_ssm) * jax.nn.silu(gs)
    out = jnp.concatenate([att_o, ssm_o], axis=-1) @ w_out
    return x + out, k, v, h_re, h_im


def setup_inputs(seed: int = 0) -> dict:
    key = jax.random.key(seed)
    ks = jax.random.split(key, 24)
    f32 = jnp.float32
    G, P, C = N_SSM_GROUPS, STATE_DIM, SSM_GROUP
    x_prompt = jax.random.normal(ks[0], (BATCH, SEQ, D_MODEL), f32)
    x_sample = jax.random.normal(ks[1], (DEC_BATCH, DEC_SEQ, D_MODEL), f32)
    cache_k = jax.random.normal(ks[2], (DEPTH, DEC_BATCH, PAST_LEN, N_HEADS, HEAD_DIM), f32)
    cache_v = jax.random.normal(ks[3], (DEPTH, DEC_BATCH, PAST_LEN, N_HEADS, HEAD_DIM), f32)
    state_ssm_re = 0.5 * jax.random.normal(ks[4], (DEPTH, DEC_BATCH, G, P), f32)
    state_ssm_im = 0.5 * jax.random.normal(ks[5], (DEPTH, DEC_BATCH, G, P), f32)
    ln_g = 1.0 + 0.02 * jax.random.normal(ks[6], (DEPTH, D_MODEL), f32)
    w_in = jax.random.normal(ks[7], (DEPTH, D_MODEL, D_IN), f32) * D_MODEL ** -0.5
    n = jnp.arange(P, dtype=f32)
    ssm_a_re = -0.5 + 0.01 * jax.random.normal(ks[8], (DEPTH, G, P), f32)
    ssm_a_im = math.pi * n + 0.01 * jax.random.normal(ks[9], (DEPTH, G, P), f32)
    ssm_log_dt = jax.random.uniform(ks[10], (DEPTH, G), f32, math.log(1e-3), math.log(1e-1))
    ssm_b_re = jax.random.normal(ks[11], (DEPTH, G, P, C), f32) * (2.0 * C) ** -0.5
    ssm_b_im = jax.random.normal(ks[12], (DEPTH, G, P, C), f32) * (2.0 * C) ** -0.5
    ssm_c_re = jax.random.normal(ks[13], (DEPTH, G, C, P), f32) * (2.0 * P) ** -0.5
    ssm_c_im = jax.random.normal(ks[14], (DEPTH, G, C, P), f32) * (2.0 * P) ** -0.5
    ssm_d = jax.random.normal(ks[15], (DEPTH, D_SSM), f32)
    w_glu = jax.random.normal(ks[16], (DEPTH, D_SSM, D_SSM), f32) * D_SSM ** -0.5
    b_glu = 0.02 * jax.random.normal(ks[17], (DEPTH, D_SSM), f32)
    g_att = 1.0 + 0.02 * jax.random.normal(ks[18], (DEPTH, D_ATT), f32)
    g_ssm = 1.0 + 0.02 * jax.random.normal(ks[19], (DEPTH, D_SSM), f32)
    w_out = jax.random.normal(ks[20], (DEPTH, D_MIX, D_MODEL), f32) * (0.5 * D_MIX ** -0.5)
    final_g = 1.0 + 0.02 * jax.random.normal(ks[21], (D_MODEL,), f32)
    return {'x_prompt': x_prompt, 'x_sample': x_sample,
            'cache_k': cache_k, 'cache_v': cache_v,
            'state_ssm_re': state_ssm_re, 'state_ssm_im': state_ssm_im,
            'ln_g': ln_g, 'w_in': w_in,
            'ssm_a_re': ssm_a_re, 'ssm_a_im': ssm_a_im, 'ssm_log_dt': ssm_log_dt,
            'ssm_b_re': ssm_b_re, 'ssm_b_im': ssm_b_im,
            'ssm_c_re': ssm_c_re, 'ssm_c_im': ssm_c_im, 'ssm_d': ssm_d,
            'w_glu': w_glu, 'b_glu': b_glu, 'g_att': g_att, 'g_ssm': g_ssm,
            'w_out': w_out, 'final_g': final_g}


def reference(x_prompt, x_sample, cache_k, cache_v, state_ssm_re, state_ssm_im,
              ln_g, w_in, ssm_a_re, ssm_a_im, ssm_log_dt, ssm_b_re, ssm_b_im,
              ssm_c_re, ssm_c_im, ssm_d, w_glu, b_glu, g_att, g_ssm, w_out, final_g):
    xp = x_prompt
    xs = x_sample
    kp, vp, hrp, hip = [], [], [], []
    ksl, vsl, hrs, his = [], [], [], []
    for l in range(DEPTH):
        w = (ln_g[l], w_in[l], ssm_a_re[l], ssm_a_im[l], ssm_log_dt[l], ssm_b_re[l], ssm_b_im[l],
             ssm_c_re[l], ssm_c_im[l], ssm_d[l], w_glu[l], b_glu[l], g_att[l], g_ssm[l], w_out[l])
        xp, k1, v1, r1, i1 = _layer(xp, None, None, None, None, *w)
        xs, k2, v2, r2, i2 = _layer(xs, cache_k[l], cache_v[l], state_ssm_re[l], state_ssm_im[l], *w)
        kp.append(k1); vp.append(v1); hrp.append(r1); hip.append(i1)
        ksl.append(k2); vsl.append(v2); hrs.append(r2); his.append(i2)
    y_prompt = _rmsnorm(xp, final_g)
    y_sample = _rmsnorm(xs, final_g)
    return (y_prompt, y_sample,
            jnp.stack(kp), jnp.stack(vp), jnp.stack(hrp), jnp.stack(hip),
            jnp.stack(ksl), jnp.stack(vsl), jnp.stack(hrs), jnp.stack(his))
```

```python
import contextlib
import math
import numpy as np
import concourse.bass as bass
import concourse.mybir as mybir
from concourse.bass_utils import run_bass_kernel_spmd

F32 = mybir.dt.float32
BF16 = mybir.dt.bfloat16
I32 = mybir.dt.int32
AF = mybir.ActivationFunctionType
ALU = mybir.AluOpType
EPS = 1e-6
TWO_PI = 2.0 * math.pi


class Sched:
    def __init__(self, nc, stack):
        self.nc = nc
        self.stack = stack
        self.eng = {"pe": nc.tensor, "act": nc.scalar, "dve": nc.vector,
                    "pool": nc.gpsimd, "sp": nc.sync}
        self.sem = {}
        self.cnt = {}
        for e in ("pe", "act", "dve", "pool"):
            self.sem[e] = stack.enter_context(nc.semaphore("s_" + e))
            self.cnt[e] = 0
        self.waited = {e: {} for e in self.eng}
        self.lastw = {}
        self.readers = {}
        self.nchan = 0
        self.stopped = False

    def chan(self, name):
        if name not in self.sem:
            self.sem[name] = self.stack.enter_context(self.nc.semaphore("d%d" % self.nchan))
            self.nchan += 1
            self.cnt[name] = 0
        return name

    def _deps(self, engine, reads, writes):
        deps = {}

        def add(d, raw):
            if d is None:
                return
            src, c = d
            if src == engine and not raw:
                return
            if src == "pe" and engine == "pe":
                return
            if deps.get(src, 0) < c:
                deps[src] = c

        for r in reads:
            add(self.lastw.get(r), True)
        for w in writes:
            add(self.lastw.get(w), True)
            for src, c in self.readers.get(w, {}).items():
                add((src, c), False)
        return deps

    def _emit_waits(self, engine, deps):
        e = self.eng[engine]
        wd = self.waited[engine]
        for src, c in deps.items():
            if wd.get(src, 0) >= c:
                continue
            e.wait_ge(self.sem[src], c)
            wd[src] = c

    def op(self, engine, fn, reads=(), writes=()):
        if self.stopped:
            return None
        deps = self._deps(engine, reads, writes)
        self._emit_waits(engine, deps)
        ins = fn(self.eng[engine])
        ins.then_inc(self.sem[engine], 1)
        self.cnt[engine] += 1
        c = self.cnt[engine]
        for r in reads:
            self.readers.setdefault(r, {})[engine] = c
        for w in writes:
            self.lastw[w] = (engine, c)
            self.readers[w] = {}
        return ins

    def dma(self, out, in_, reads=(), writes=(), chan=None, queue="sp", **kw):
        if self.stopped:
            return None
        if chan is None:
            chan = "c_" + str(writes[0] if writes else reads[0])
        self.chan(chan)
        deps = self._deps("__dma__", reads, writes)
        self._emit_waits(queue, deps)
        ins = self.eng[queue].dma_start(out=out, in_=in_, **kw)
        ins.then_inc(self.sem[chan], 16)
        self.cnt[chan] += 16
        c = self.cnt[chan]
        for r in reads:
            self.readers.setdefault(r, {})[chan] = c
        for w in writes:
            self.lastw[w] = (chan, c)
            self.readers[w] = {}
        return ins

    def barrier(self):
        if self.stopped and getattr(self, "_final", False) is False:
            return
        allsrc = [s for s in self.cnt if self.cnt[s] > 0]
        for e in ("pe", "act", "dve", "pool", "sp"):
            self._emit_waits(e, {s: self.cnt[s] for s in allsrc if s != e})


class StopBuild(Exception):
    pass


def build(T, TS, PL, DEPTH):
    import os
    KSTOP = int(os.environ.get("KSTOP", "99"))

    def chk(level):
        if KSTOP <= level:
            S.stopped = True

    nc = bass.Bass("TRN2", target_bir_lowering=False)
    D = 1024

    def din(name, shape):
        return nc.dram_tensor(name, shape, F32, kind="ExternalInput").ap()

    def dout(name, shape):
        return nc.dram_tensor(name, shape, F32, kind="ExternalOutput").ap()

    def dscr(name, shape, dt):
        return nc.dram_tensor(name, shape, dt, kind="Internal").ap()

    xp = din("xp", [2, T, D]); xs = din("xs", [2, TS, D])
    ck = din("ck", [DEPTH, 2, PL, 512]); cv = din("cv", [DEPTH, 2, PL, 512])
    sre = din("sre", [DEPTH, 2, 32, 64]); sim = din("sim", [DEPTH, 2, 32, 64])
    ln_g = din("ln_g", [DEPTH, D]); w_in = din("w_in", [DEPTH, D, 3072])
    a_re = din("a_re", [DEPTH, 32, 64]); a_im = din("a_im", [DEPTH, 32, 64])
    log_dt = din("log_dt", [DEPTH, 32])
    b_re = din("b_re", [DEPTH, 32, 64, 16]); b_im = din("b_im", [DEPTH, 32, 64, 16])
    c_re = din("c_re", [DEPTH, 32, 16, 64]); c_im = din("c_im", [DEPTH, 32, 16, 64])
    ssm_d = din("ssm_d", [DEPTH, 512]); w_glu = din("w_glu", [DEPTH, 512, 512])
    b_glu = din("b_glu", [DEPTH, 512]); g_att = din("g_att", [DEPTH, 512])
    g_ssm = din("g_ssm", [DEPTH, 512]); w_out = din("w_out", [DEPTH, D, D])
    final_g = din("final_g", [D])
    yp = dout("yp", [2, T, D]); ys = dout("ys", [2, TS, D])
    kp = dout("kp", [DEPTH, 2, T, 512]); vp = dout("vp", [DEPTH, 2, T, 512])
    hrp = dout("hrp", [DEPTH, 2, 32, 64]); hip = dout("hip", [DEPTH, 2, 32, 64])
    ks = dout("ks", [DEPTH, 2, TS, 512]); vs = dout("vs", [DEPTH, 2, TS, 512])
    hrs = dout("hrs", [DEPTH, 2, 32, 64]); his = dout("his", [DEPTH, 2, 32, 64])

    seqs = []
    for b in range(2):
        seqs.append(dict(kind="p", b=b, NT=T, xin=xp[b], yout=yp[b], past=0))
    for b in range(2):
        seqs.append(dict(kind="s", b=b, NT=TS, xin=xs[b], yout=ys[b], past=PL))
    for si, sq in enumerate(seqs):
        NT = sq["NT"]
        sq["xa"] = dscr("xa%d" % si, [NT, D], F32)
        sq["xb"] = dscr("xb%d" % si, [NT, D], F32)
        for nm in ("qT", "kT", "gaT", "soT"):
            sq[nm] = dscr("%s%d" % (nm, si), [512, NT], BF16)
        sq["vv"] = dscr("vv%d" % si, [NT, 512], BF16)
        sq["N"] = min(512, NT)

    with contextlib.ExitStack() as top:
        S = Sched(nc, top)

        def sb(stack, name, shape, dt):
            return stack.enter_context(nc.sbuf_tensor(name, shape, dt))

        def mm(out, lhsT, rhs, start, stop, r, w):
            S.op("pe", lambda e: e.matmul(out, lhsT, rhs, start=start, stop=stop), r, w)

        def tp(out, in_, ident, r, w):
            S.op("pe", lambda e: e.transpose(out, in_, ident), r, w)

        def act(out, in_, func, r, w, scale=1.0, bias=0.0):
            S.op("act", lambda e: e.activation(out=out, in_=in_, func=func, bias=bias, scale=scale), r, w)

        def tt(eng, out, in0, in1, op, r, w):
            S.op(eng, lambda e: e.tensor_tensor(out=out, in0=in0, in1=in1, op=op), r, w)

        def ts(eng, out, in0, s1, op0, r, w, s2=None, op1=None):
            if op1 is None:
                S.op(eng, lambda e: e.tensor_scalar(out=out, in0=in0, scalar1=s1, scalar2=None, op0=op0), r, w)
            else:
                S.op(eng, lambda e: e.tensor_scalar(out=out, in0=in0, scalar1=s1, scalar2=s2, op0=op0, op1=op1), r, w)

        def stt(out, in0, scalar, in1, op0, op1, r, w):
            S.op("dve", lambda e: e.scalar_tensor_tensor(out=out, in0=in0, scalar=scalar, in1=in1, op0=op0, op1=op1), r, w)

        def cp(eng, out, in_, r, w):
            S.op(eng, lambda e: e.tensor_copy(out=out, in_=in_), r, w)

        def recip(out, in_, r, w):
            S.op("dve", lambda e: e.reciprocal(out=out, in_=in_), r, w)

        def mset(eng, ap, val, w):
            S.op(eng, lambda e: e.memset(ap, val), (), w)

        identb = sb(top, "identb", [128, 128], BF16)
        identf = sb(top, "identf", [128, 128], F32)
        negU = sb(top, "negU", [128, 128], BF16)
        negones = sb(top, "negones", [128, 128], BF16)
        onesb = sb(top, "onesb", [128, 128], BF16)
        mask32 = sb(top, "mask32", [128, 128], F32)
        fgbc = sb(top, "fgbc", [128, D], F32)
        NP = seqs[0]["N"]
        ndiag = NP // 128
        masks = sb(top, "masks", [128, ndiag, NP], BF16)
        masks_s = sb(top, "masks_s", [TS, TS], BF16)
        mset("pool", identb[:], 0.0, ["identb"])
        S.op("pool", lambda e: e.affine_select(out=identb[:], in_=identb[:], pattern=[[-1, 128]], compare_op=ALU.not_equal, fill=1.0, base=0, channel_multiplier=1), ["identb"], ["identb"])
        mset("pool", identf[:], 0.0, ["identf"])
        S.op("pool", lambda e: e.affine_select(out=identf[:], in_=identf[:], pattern=[[-1, 128]], compare_op=ALU.not_equal, fill=1.0, base=0, channel_multiplier=1), ["identf"], ["identf"])
        mset("pool", negU[:], -1.0, ["negU"])
        S.op("pool", lambda e: e.affine_select(out=negU[:], in_=negU[:], pattern=[[-1, 128]], compare_op=ALU.is_ge, fill=0.0, base=0, channel_multiplier=1), ["negU"], ["negU"])
        mset("pool", negones[:], -1.0, ["negones"])
        mset("pool", onesb[:], 1.0, ["onesb"])
        mset("pool", mask32[:], 0.0, ["mask32"])
        for j in range(4):
            mset("pool", mask32[32 * j:32 * j + 32, 32 * j:32 * j + 32], 1.0, ["mask32"])
        mset("pool", masks[:], 1.0, ["masks"])
        for j in range(ndiag):
            S.op("pool", lambda e: e.affine_select(out=masks[:, j, :], in_=masks[:, j, :], pattern=[[1, NP]], compare_op=ALU.is_gt, fill=0.0, base=-128 * j, channel_multiplier=-1), ["masks"], ["masks"])
        mset("pool", masks_s[:], 1.0, ["masks_s"])
        S.op("pool", lambda e: e.affine_select(out=masks_s[:], in_=masks_s[:], pattern=[[1, TS]], compare_op=ALU.is_gt, fill=0.0, base=0, channel_multiplier=-1), ["masks_s"], ["masks_s"])
        S.dma(fgbc[:], final_g.partition_broadcast(128), writes=["fgbc"])

        try:
            chk(0)
            for l in range(DEPTH):
                last = (l == DEPTH - 1)
                with contextlib.ExitStack() as L1:
                    winb = sb(L1, "winb%d" % l, [128, 8, 3072], BF16)
                    wglub = sb(L1, "wglub%d" % l, [128, 4, 512], BF16)
                    gbc = sb(L1, "gbc%d" % l, [128, D], F32)
                    dvec = sb(L1, "dvec%d" % l, [128, 4], F32)
                    nbg = sb(L1, "nbg%d" % l, [128, 4], F32)
                    gssm = sb(L1, "gssm%d" % l, [128, 4], F32)
                    PWr = sb(L1, "PWr%d" % l, [128, 9, 16], F32)
                    PWi = sb(L1, "PWi%d" % l, [128, 9, 16], F32)
                    A1 = sb(L1, "A1%d" % l, [128, 2, 16], F32)
                    Aip = sb(L1, "Aip%d" % l, [128, 16], F32)
                    Ain = sb(L1, "Ain%d" % l, [128, 16], F32)
                    BD = sb(L1, "BD%d" % l, [128, 4, 8, 128], BF16)
                    BL = sb(L1, "BL%d" % l, [128, 4, 8, 2, 128], BF16)
                    CA = sb(L1, "CA%d" % l, [128, 8, 2, 16, 32], BF16)
                    for kc in range(8):
                        S.dma(winb[:, kc, :], w_in[l, kc * 128:(kc + 1) * 128, :], writes=["winb"], queue="pool")
                    S.dma(wglub[:], w_glu[l].rearrange("(kc p) n -> p kc n", p=128), writes=["wglub"], queue="pool")
                    S.dma(gbc[:], ln_g[l].partition_broadcast(128), writes=["gbc"])
                    for tl, src, nm in ((dvec, ssm_d, "dvec%d" % l), (nbg, b_glu, "nbg%d" % l), (gssm, g_ssm, "gssm%d" % l)):
                        S.dma(tl[:], src[l].rearrange("(c p) -> p c", p=128), writes=[nm], allow_slow_non_contiguous=True)
                    ts("dve", nbg[:], nbg[:], -1.0, ALU.mult, ["nbg%d" % l], ["nbg%d" % l])

                    with contextlib.ExitStack() as SU:
                        k_ = "su"
                        aqr = sb(SU, "aqr%d" % l, [128, 16], F32); aqi = sb(SU, "aqi%d" % l, [128, 16], F32)
                        dtq = sb(SU, "dtq%d" % l, [128, 16], F32)
                        adt = sb(SU, "adt%d" % l, [128, 16], F32); ang = sb(SU, "ang%d" % l, [128, 16], F32)
                        mag = sb(SU, "mag%d" % l, [128, 9, 16], F32)
                        ph = sb(SU, "ph%d" % l, [128, 2, 9, 16], F32)
                        pht = sb(SU, "pht%d" % l, [128, 2, 9, 16], F32)
                        phi = sb(SU, "phi%d" % l, [128, 2, 9, 16], I32)
                        sc = sb(SU, "sc%d" % l, [128, 2, 9, 16], F32)
                        t1 = sb(SU, "t1%d" % l, [128, 16], F32); t2 = sb(SU, "t2%d" % l, [128, 16], F32)
                        t3 = sb(SU, "t3%d" % l, [128, 16], F32)
                        fr = sb(SU, "fr%d" % l, [128, 16], F32); fi = sb(SU, "fi%d" % l, [128, 16], F32)
                        Bq0r = sb(SU, "Bq0r%d" % l, [128, 16, 32], F32); Bq0i = sb(SU, "Bq0i%d" % l, [128, 16, 32], F32)
                        Bbr = sb(SU, "Bbr%d" % l, [128, 16, 32], F32); Bbi = sb(SU, "Bbi%d" % l, [128, 16, 32], F32)
                        nBbi = sb(SU, "nBbi%d" % l, [128, 16, 32], F32)
                        u1 = sb(SU, "u1%d" % l, [128, 16, 32], F32); u2 = sb(SU, "u2%d" % l, [128, 16, 32], F32)
                        Yr = sb(SU, "Yr%d" % l, [128, 16, 32], F32); Yi = sb(SU, "Yi%d" % l, [128, 16, 32], F32)
                        Zr = sb(SU, "Zr%d" % l, [32, 16, 128], F32); Zi = sb(SU, "Zi%d" % l, [32, 16, 128], F32)
                        CTr = sb(SU, "CTr%d" % l, [128, 16, 32], F32); CTi = sb(SU, "CTi%d" % l, [128, 16, 32], F32)
                        Xr = sb(SU, "Xr%d" % l, [128, 9, 16, 32], F32); Xi = sb(SU, "Xi%d" % l, [128, 9, 16, 32], F32)
                        pss = SU.enter_context(nc.psum_tensor("pss%d" % l, [128, 4, 128], F32))
                        psc = SU.enter_context(nc.psum_tensor("psc%d" % l, [128, 16, 32], F32))
                        K = [k_]
                        S.dma(aqr[:], a_re[l].rearrange("(pr g2) p -> (g2 p) pr", g2=2), writes=K, allow_slow_non_contiguous=True)
                        S.dma(aqi[:], a_im[l].rearrange("(pr g2) p -> (g2 p) pr", g2=2), writes=K, allow_slow_non_contiguous=True)
                        mset("dve", Bq0r[:], 0.0, K); mset("dve", Bq0i[:], 0.0, K)
                        mset("dve", Zr[:], 0.0, K); mset("dve", Zi[:], 0.0, K)
                        for g2 in range(2):
                            S.dma(dtq[g2 * 64:(g2 + 1) * 64, :], log_dt[l].rearrange("(pr g2) -> g2 pr", g2=2)[g2].partition_broadcast(64), reads=K, writes=K, allow_slow_non_contiguous=True)
                            S.dma(Bq0r[g2 * 64:(g2 + 1) * 64, :, g2 * 16:(g2 + 1) * 16], b_re[l].rearrange("(pr g2) p c -> g2 p pr c", g2=2)[g2], reads=K, writes=K)
                            S.dma(Bq0i[g2 * 64:(g2 + 1) * 64, :, g2 * 16:(g2 + 1) * 16], b_im[l].rearrange("(pr g2) p c -> g2 p pr c", g2=2)[g2], reads=K, writes=K)
                            S.dma(Zr[g2 * 16:(g2 + 1) * 16, :, g2 * 64:(g2 + 1) * 64], c_re[l].rearrange("(pr g2) c p -> g2 c pr p", g2=2)[g2], reads=K, writes=K)
                            S.dma(Zi[g2 * 16:(g2 + 1) * 16, :, g2 * 64:(g2 + 1) * 64], c_im[l].rearrange("(pr g2) c p -> g2 c pr p", g2=2)[g2], reads=K, writes=K)
                        act(dtq[:], dtq[:], AF.Exp, K, K)
                        tt("dve", adt[:], aqr[:], dtq[:], ALU.mult, K, K)
                        tt("dve", ang[:], aqi[:], dtq[:], ALU.mult, K, K)
                        for dl in range(9):
                            act(mag[:, dl, :], adt[:], AF.Exp, K, K, scale=float(dl))
                            ts("dve", ph[:, 0, dl, :], ang[:], float(dl), ALU.mult, K, K)
                            ts("dve", ph[:, 1, dl, :], ang[:], float(dl), ALU.mult, K, K, s2=math.pi / 2, op1=ALU.add)
                        ts("dve", pht[:], ph[:], 1.0 / TWO_PI, ALU.mult, K, K)
                        cp("dve", phi[:], pht[:], K, K)
                        cp("dve", pht[:], phi[:], K, K)
                        stt(ph[:], pht[:], -TWO_PI, ph[:], ALU.mult, ALU.add, K, K)
                        ts("dve", ph[:], ph[:], -math.pi, ALU.max, K, K, s2=math.pi, op1=ALU.min)
                        act(sc[:], ph[:], AF.Sin, K, K)
                        chk(1)
                        tt("dve", PWi[:], mag[:], sc[:, 0], ALU.mult, K, ["PW"])
                        tt("dve", PWr[:], mag[:], sc[:, 1], ALU.mult, K, ["PW"])
                        cp("dve", A1[:, 0, :], PWr[:, 8, :], ["PW"], ["A8"]); cp("dve", A1[:, 1, :], PWr[:, 8, :], ["PW"], ["A8"])
                        cp("dve", Aip[:], PWi[:, 8, :], ["PW"], ["A8"])
                        ts("dve", Ain[:], PWi[:, 8, :], -1.0, ALU.mult, ["PW"], ["A8"])
                        tt("dve", t1[:], aqr[:], aqr[:], ALU.mult, K, K)
                        tt("dve", t2[:], aqi[:], aqi[:], ALU.mult, K, K)
                        tt("dve", t1[:], t1[:], t2[:], ALU.add, K, K)
                        recip(t1[:], t1[:], K, K)
                        ts("dve", t2[:], PWr[:, 1, :], -1.0, ALU.add, K + ["PW"], K)
                        tt("dve", fr[:], t2[:], aqr[:], ALU.mult, K, K)
                        tt("dve", t3[:], PWi[:, 1, :], aqi[:], ALU.mult, K + ["PW"], K)
                        tt("dve", fr[:], fr[:], t3[:], ALU.add, K, K)
                        tt("dve", fr[:], fr[:], t1[:], ALU.mult, K, K)
                        tt("dve", fi[:], PWi[:, 1, :], aqr[:], ALU.mult, K + ["PW"], K)
                        tt("dve", t3[:], t2[:], aqi[:], ALU.mult, K, K)
                        tt("dve", fi[:], fi[:], t3[:], ALU.subtract, K, K)
                        tt("dve", fi[:], fi[:], t1[:], ALU.mult, K, K)
                        bc = lambda ap: ap.unsqueeze(2).to_broadcast([128, 16, 32])
                        tt("dve", u1[:], Bq0r[:], bc(fr[:]), ALU.mult, K, K)
                        tt("dve", u2[:], Bq0i[:], bc(fi[:]), ALU.mult, K, K)
                        tt("dve", Bbr[:], u1[:], u2[:], ALU.subtract, K, K)
                        tt("dve", u1[:], Bq0i[:], bc(fr[:]), ALU.mult, K, K)
                        tt("dve", u2[:], Bq0r[:], bc(fi[:]), ALU.mult, K, K)
                        tt("dve", Bbi[:], u1[:], u2[:], ALU.add, K, K)
                        ts("dve", nBbi[:], Bbi[:], -1.0, ALU.mult, K, K)
                        for (Z, CT) in ((Zr, CTr), (Zi, CTi)):
                            for pr in range(16):
                                tp(psc[:, pr, :], Z[:, pr, :], identf[0:32, 0:32], K + ["identf"], ["psc"])
                            cp("dve", CT[:], psc[:], ["psc"], K)
                        for dl in range(9):
                            pr_ = bc(PWr[:, dl, :]); pi_ = bc(PWi[:, dl, :])
                            tt("dve", u1[:], CTr[:], pr_, ALU.mult, K + ["PW"], K)
                            tt("dve", u2[:], CTi[:], pi_, ALU.mult, K + ["PW"], K)
                            tt("dve", Xr[:, dl], u1[:], u2[:], ALU.subtract, K, K)
                            tt("dve", u1[:], CTr[:], pi_, ALU.mult, K + ["PW"], K)
                            tt("dve", u2[:], CTi[:], pr_, ALU.mult, K + ["PW"], K)
                            tt("dve", Xi[:, dl], u1[:], u2[:], ALU.add, K, K)
                        for tau in range(8):
                            cp("dve", CA[:, tau, 0, :, :], Xr[:, tau + 1], K, ["CA"])
                            ts("dve", CA[:, tau, 1, :, :], Xi[:, tau + 1], -1.0, ALU.mult, K, ["CA"])
                        for tau in range(8):
                            pr_ = bc(PWr[:, 7 - tau, :]); pi_ = bc(PWi[:, 7 - tau, :])
                            tt("dve", u1[:], Bbr[:], pr_, ALU.mult, K + ["PW"], K)
                            tt("dve", u2[:], Bbi[:], pi_, ALU.mult, K + ["PW"], K)
                            tt("dve", Yr[:], u1[:], u2[:], ALU.subtract, K, K)
                            tt("dve", u1[:], Bbr[:], pi_, ALU.mult, K + ["PW"], K)
                            tt("dve", u2[:], Bbi[:], pr_, ALU.mult, K + ["PW"], K)
                            tt("dve", Yi[:], u1[:], u2[:], ALU.add, K, K)
                            for ri, Y in ((0, Yr), (1, Yi)):
                                for ft in range(4):
                                    tp(pss[:, ft, :], Y[:, ft * 4:(ft + 1) * 4, :].rearrange("p a b -> p (a b)"), identf[:], K + ["identf"], ["pss"])
                                cp("dve", BL[:, :, tau, ri, :], pss[:], ["pss"], ["BL"])
                        for dl in range(8):
                            for ft in range(4):
                                fl = lambda t_: t_[:, ft * 4:(ft + 1) * 4, :].rearrange("p a b -> p (a b)")
                                mm(pss[:, ft, :], fl(Bbr), Xr[:, dl, ft * 4:(ft + 1) * 4, :].rearrange("p a b -> p (a b)"), True, False, K, ["pss"])
                                mm(pss[:, ft, :], fl(nBbi), Xi[:, dl, ft * 4:(ft + 1) * 4, :].rearrange("p a b -> p (a b)"), False, True, K, ["pss"])
                            tt("dve", BD[:, :, dl, :], pss[:], mask32[:].unsqueeze(1).to_broadcast([128, 4, 128]), ALU.mult, ["pss", "mask32"], ["BD"])
                    chk(2)
                    S.barrier()

                    with contextlib.ExitStack() as W1:
                        xt = [sb(W1, "xt%d_%d" % (l, i), [128, D], F32) for i in range(2)]
                        ssq = sb(W1, "ssq%d" % l, [128, 4], F32)
                        hn = sb(W1, "hn%d" % l, [128, D], BF16)
                        junk = hn
                        hnT = sb(W1, "hnT%d" % l, [128, 8, 512], BF16)
                        stg = [sb(W1, "stg%d_%d" % (l, i), [128, 512], BF16) for i in range(3)]
                        ef = [sb(W1, "ef%d_%d" % (l, i), [128, 512], F32) for i in range(2)]
                        kvst = [sb(W1, "kvst%d_%d" % (l, i), [128, 1024], F32) for i in range(2)]
                        vst = [sb(W1, "vst%d_%d" % (l, i), [128, 512], BF16) for i in range(2)]
                        uT = sb(W1, "uT%d" % l, [128, 4, 512], BF16)
                        gsT = sb(W1, "gsT%d" % l, [128, 4, 512], BF16)
                        Sall = sb(W1, "Sall%d" % l, [128, 64, 2, 16], F32)
                        Hall = sb(W1, "Hall%d" % l, [128, 65, 2, 16], F32)
                        Hb = sb(W1, "Hb%d" % l, [128, 2, 16, 64], BF16)
                        Hbn = sb(W1, "Hbn%d" % l, [128, 2, 16, 64], BF16)
                        Pt = sb(W1, "Pt%d" % l, [128, 2, 16], F32)
                        Qt = sb(W1, "Qt%d" % l, [128, 2, 16], F32)
                        ysb = sb(W1, "ysb%d" % l, [128, 512], F32)
                        tg = sb(W1, "tg%d" % l, [128, 512], F32)
                        yg = sb(W1, "yg%d" % l, [128, 4, 512], F32)
                        ygb = sb(W1, "ygb%d" % l, [128, 4, 512], BF16)
                        sy = sb(W1, "sy%d" % l, [128, 4, 512], F32)
                        sqb = sb(W1, "sqb%d" % l, [128, 4, 512], BF16)
                        rstd = sb(W1, "rstd%d" % l, [128, 512], F32)
                        sob = sqb
                        ptp = [W1.enter_context(nc.psum_tensor("ptp%d_%d" % (l, i), [128, 8, 128], BF16)) for i in range(1)]
                        ppj = [W1.enter_context(nc.psum_tensor("ppj%d_%d" % (l, i), [128, 512], F32)) for i in range(2)]
                        pS = [W1.enter_context(nc.psum_tensor("pS%d_%d" % (l, i), [128, 2, 4, 64], F32)) for i in range(4)]
                        py = [W1.enter_context(nc.psum_tensor("py%d_%d" % (l, i), [128, 8, 64], F32)) for i in range(1)]
                        cnt = dict(x=0, pj=0, stg=0, ef=0, kv=0, py=0)

                        def silu_evac(ps, pskey, out, outkey, N):
                            i = cnt["ef"] % 2; cnt["ef"] += 1
                            e_ = ef[i]; ek = "ef%d" % i
                            act(e_[:, :N], ps, AF.Exp, [pskey], [ek], scale=-1.0)
                            ts("dve", e_[:, :N], e_[:, :N], 1.0, ALU.add, [ek], [ek])
                            recip(e_[:, :N], e_[:, :N], [ek], [ek])
                            tt("dve", out, e_[:, :N], ps, ALU.mult, [ek, pskey], [outkey])

                        for si, sq in enumerate(seqs):
                            NT, N, b = sq["NT"], sq["N"], sq["b"]
                            R = min(128, N); nsub = N // R; nch = N // 8
                            xsrc = sq["xin"] if l == 0 else (sq["xa"] if l % 2 == 1 else sq["xb"])
                            kout, vout = (kp, vp) if sq["kind"] == "p" else (ks, vs)
                            if sq["kind"] == "p":
                                mset("pool", Hall[:, 0], 0.0, ["Hall"])
                            else:
                                S.dma(Hall[:, 0, 0, :], sre[l, b].rearrange("(pr g2) p -> (g2 p) pr", g2=2), writes=["Hall"], chan="hld", allow_slow_non_contiguous=True)
                                S.dma(Hall[:, 0, 1, :], sim[l, b].rearrange("(pr g2) p -> (g2 p) pr", g2=2), writes=["Hall"], chan="hld", allow_slow_non_contiguous=True)
                            for ti in range(NT // N):
                                t0 = ti * N
                                for sub in range(nsub):
                                    xi = cnt["x"] % 2; cnt["x"] += 1
                                    xk = "xt%d" % xi
                                    r0 = t0 + sub * R
                                    S.dma(xt[xi][:R, :], xsrc[r0:r0 + R, :], writes=[xk])
                                    act(junk[:R, :], xt[xi][:R, :], AF.Square, [xk], ["hn"])
                                    S.op("dve", lambda e: e.tensor_reduce(out=ssq[:R, 0:1], in_=junk[:R, :], axis=mybir.AxisListType.X, op=ALU.add), ["hn"], ["ssq"])
                                    act(ssq[:R, 1:2], ssq[:R, 0:1], AF.Ln, ["ssq"], ["ssq"], scale=1.0 / D, bias=EPS)
                                    act(ssq[:R, 2:3], ssq[:R, 1:2], AF.Exp, ["ssq"], ["ssq"], scale=-0.5)
                                    stt(hn[:R, :], xt[xi][:R, :], ssq[:R, 2:3], gbc[:R, :], ALU.mult, ALU.mult, [xk, "ssq", "gbc"], ["hn"])
                                    pi_ = 0; pk = "ptp%d" % pi_
                                    for kc in range(8):
                                        tp(ptp[pi_][:, kc, :R], hn[:R, kc * 128:(kc + 1) * 128], identb[:R, :R], ["hn", "identb"], [pk])
                                    cp("dve", hnT[:, :, sub * R:sub * R + R], ptp[pi_][:, :, :R], [pk], ["hnT"])
                                chk(3)
                                for mt in list(range(0, 8)) + list(range(12, 24)):
                                    pj = cnt["pj"] % 2; cnt["pj"] += 1
                                    pjk = "ppj%d" % pj
                                    for kc in range(8):
                                        mm(ppj[pj][:, :N], winb[:, kc, mt * 128:(mt + 1) * 128], hnT[:, kc, :N], kc == 0, kc == 7, ["winb", "hnT"], [pjk])
                                    grp, ft = mt // 4, mt % 4
                                    if grp in (0, 1):
                                        sg = cnt["stg"] % 3; cnt["stg"] += 1
                                        sk = "stg%d" % sg
                                        if grp == 0:
                                            act(stg[sg][:, :N], ppj[pj][:, :N], AF.Copy, [pjk], [sk], scale=0.125)
                                        else:
                                            cp("dve", stg[sg][:, :N], ppj[pj][:, :N], [pjk], [sk])
                                        dst = sq["qT"] if grp == 0 else sq["kT"]
                                        S.dma(dst[ft * 128:(ft + 1) * 128, t0:t0 + N], stg[sg][:, :N], reads=[sk], chan="st_" + sk)
                                    elif grp == 3:
                                        sg = cnt["stg"] % 3; cnt["stg"] += 1
                                        sk = "stg%d" % sg
                                        silu_evac(ppj[pj][:, :N], pjk, stg[sg][:, :N], sk, N)
                                        S.dma(sq["gaT"][ft * 128:(ft + 1) * 128, t0:t0 + N], stg[sg][:, :N], reads=[sk], chan="st_" + sk)
                                    elif grp == 4:
                                        cp("dve", uT[:, ft, :N], ppj[pj][:, :N], [pjk], ["uT"])
                                    else:
                                        silu_evac(ppj[pj][:, :N], pjk, gsT[:, ft, :N], "gsT", N)
                                chk(4)
                                for sub in range(nsub):
                                    ki = cnt["kv"] % 2; cnt["kv"] += 1
                                    kk = "kvst%d" % ki; vk = "vst%d" % ki
                                    r0 = t0 + sub * R
                                    for half in range(2):
                                        pj = cnt["pj"] % 2; cnt["pj"] += 1
                                        pjk = "ppj%d" % pj
                                        for kc in range(8):
                                            mm(ppj[pj][:R, :], hnT[:, kc, sub * R:sub * R + R], winb[:, kc, 512 + half * 512:1024 + half * 512], kc == 0, kc == 7, ["winb", "hnT"], [pjk])
                                        if os.environ.get("KV", "abcde").find("d") >= 0:
                                            cp("dve", kvst[ki][:R, half * 512:(half + 1) * 512], ppj[pj][:R, :], [pjk], [kk])
                                        if half == 1 and os.environ.get("KV", "abcde").find("e") >= 0:
                                            act(vst[ki][:R, :], kvst[ki][:R, 512:1024], AF.Copy, [kk], [vk])
                                    if os.environ.get("KV", "abc").find("a") >= 0:
                                        S.dma(kout[l, b, r0:r0 + R, :], kvst[ki][:R, 0:512], reads=[kk], chan="st_" + kk)
                                    if os.environ.get("KV", "abc").find("b") >= 0:
                                        S.dma(vout[l, b, r0:r0 + R, :], kvst[ki][:R, 512:1024], reads=[kk], chan="st_" + kk)
                                    if os.environ.get("KV", "abc").find("c") >= 0:
                                        S.dma(sq["vv"][r0:r0 + R, :], vst[ki][:R, :], reads=[vk], chan="st_" + vk)
                                chk(5)
                                uv = uT[:, :, :N].rearrange("p f (k t) -> p f k t", t=8)
                                for j in range(4):
                                    pk = "pS%d" % j
                                    lo = 64 if j == 3 else 32 * j
                                    for ri in range(2):
                                        for ft in range(4):
                                            for tau in range(8):
                                                mm(pS[j][:, ri, ft, :nch], BL[lo:32 * j + 32, ft, tau, ri, :], uv[lo:32 * j + 32, ft, :, tau], tau == 0, tau == 7, ["BL", "uT"], [pk])
                                    cp("dve", Sall[:, :nch, :, j::4].rearrange("p k r f -> p r f k"), pS[j][:, :, :, :nch], [pk], ["Sall"])
                                tt("pool", Sall[:, :nch, :, 3::4], Sall[:, :nch, :, 3::4], Sall[:, :nch, :, 2::4], ALU.subtract, ["Sall"], ["Sall"])
                                chk(6)
                                for k in range(nch):
                                    tt("pool", Pt[:], Hall[:, k], A1[:], ALU.mult, ["Hall", "A8"], ["Pt"])
                                    tt("pool", Qt[:, 0, :], Hall[:, k, 1, :], Ain[:], ALU.mult, ["Hall", "A8"], ["Qt"])
                                    tt("pool", Qt[:, 1, :], Hall[:, k, 0, :], Aip[:], ALU.mult, ["Hall", "A8"], ["Qt"])
                                    tt("pool", Pt[:], Pt[:], Qt[:], ALU.add, ["Pt", "Qt"], ["Pt"])
                                    tt("pool", Hall[:, k + 1], Pt[:], Sall[:, k], ALU.add, ["Pt", "Sall"], ["Hall"])
                                for ri in range(2):
                                    cp("pool", Hb[:, ri, :, :nch], Hall[:, 0:nch, ri, :].rearrange("p k r -> p r k"), ["Hall"], ["Hb"])
                                ts("pool", Hbn[:, :, :, :nch], Hb[:, :, :, :nch], -1.0, ALU.mult, ["Hb"], ["Hbn"])
                                if ti == NT // N - 1:
                                    ho_r, ho_i = (hrp, hip) if sq["kind"] == "p" else (hrs, his)
                                    S.dma(ho_r[l, b].rearrange("(pr g2) p -> (g2 p) pr", g2=2), Hall[:, nch, 0, :], reads=["Hall"], chan="hst", allow_slow_non_contiguous=True)
                                    S.dma(ho_i[l, b].rearrange("(pr g2) p -> (g2 p) pr", g2=2), Hall[:, nch, 1, :], reads=["Hall"], chan="hst", allow_slow_non_contiguous=True)
                                else:
                                    cp("pool", Hall[:, 0], Hall[:, nch], ["Hall"], ["Hall"])
                                chk(7)
                                for ft in range(4):
                                    yi = 0
                                    yk = "py%d" % yi
                                    for tau in range(8):
                                        for dl in range(tau + 1):
                                            mm(py[yi][:, tau, :nch], BD[:, ft, dl, :], uv[:, ft, :, tau - dl], dl == 0, False, ["BD", "uT"], [yk])
                                        for j in range(4):
                                            pr = ft * 4 + j
                                            for ri in range(2):
                                                if j < 3:
                                                    mm(py[yi][32 * j:32 * j + 32, tau, :nch], CA[:, tau, ri, pr, :], Hb[:, ri, pr, :nch], False, False, ["CA", "Hb"], [yk])
                                                else:
                                                    mm(py[yi][64:128, tau, :nch], CA[:, tau, ri, pr - 1:pr + 1, :].rearrange("p a b -> p (a b)"), Hb[:, ri, pr, :nch], False, False, ["CA", "Hb"], [yk])
                                                    mm(py[yi][64:96, tau, :nch], CA[:, tau, ri, pr - 1, :], Hbn[:, ri, pr, :nch], False, (ri == 1), ["CA", "Hbn"], [yk])
                                    yv = py[yi][:, :, :nch].rearrange("p t k -> p k t")
                                    ysv = ysb[:, :N].rearrange("p (k t) -> p k t", t=8)
                                    stt(ysv, uv[:, ft], dvec[:, ft:ft + 1], yv, ALU.mult, ALU.add, ["uT", "dvec%d" % l, yk], ["ysb"])
                                    tt("pool", tg[:, :N], ysb[:, :N], ysb[:, :N], ALU.mult, ["ysb"], ["tg"])
                                    ts("pool", tg[:, :N], tg[:, :N], 0.044715, ALU.mult, ["tg"], ["tg"], s2=1.0, op1=ALU.add)
                                    tt("pool", tg[:, :N], tg[:, :N], ysb[:, :N], ALU.mult, ["tg", "ysb"], ["tg"])
                                    act(tg[:, :N], tg[:, :N], AF.Exp, ["tg"], ["tg"], scale=-1.5957691216057308)
                                    ts("dve", tg[:, :N], tg[:, :N], 1.0, ALU.add, ["tg"], ["tg"])
                                    recip(tg[:, :N], tg[:, :N], ["tg"], ["tg"])
                                    tt("dve", yg[:, ft, :N], ysb[:, :N], tg[:, :N], ALU.mult, ["tg", "ysb"], ["yg"])
                                    cp("pool", ygb[:, ft, :N], yg[:, ft, :N], ["yg"], ["ygb"])
                                chk(8)
                                for mt in range(4):
                                    pj = cnt["pj"] % 2; cnt["pj"] += 1
                                    pjk = "ppj%d" % pj
                                    for kc in range(4):
                                        mm(ppj[pj][:, :N], wglub[:, kc, mt * 128:(mt + 1) * 128], ygb[:, kc, :N], kc == 0, kc == 3, ["wglub", "ygb"], [pjk])
                                    act(tg[:, :N], ppj[pj][:, :N], AF.Exp, [pjk, "nbg%d" % l], ["tg"], scale=-1.0, bias=nbg[:, mt:mt + 1])
                                    ts("dve", tg[:, :N], tg[:, :N], 1.0, ALU.add, ["tg"], ["tg"])
                                    recip(tg[:, :N], tg[:, :N], ["tg"], ["tg"])
                                    tt("dve", sy[:, mt, :N], yg[:, mt, :N], tg[:, :N], ALU.mult, ["tg", "yg"], ["sy"])
                                    tt("pool", sqb[:, mt, :N], sy[:, mt, :N], sy[:, mt, :N], ALU.mult, ["sy"], ["sqb"])
                                pj = cnt["pj"] % 2; cnt["pj"] += 1
                                pjk = "ppj%d" % pj
                                for kc in range(4):
                                    mm(ppj[pj][:, :N], onesb[:], sqb[:, kc, :N], kc == 0, kc == 3, ["onesb", "sqb"], [pjk])
                                act(rstd[:, :N], ppj[pj][:, :N], AF.Ln, [pjk], ["rstd"], scale=1.0 / 512, bias=EPS)
                                act(rstd[:, :N], rstd[:, :N], AF.Exp, ["rstd"], ["rstd"], scale=-0.5)
                                for ft in range(4):
                                    tt("dve", sy[:, ft, :N], sy[:, ft, :N], rstd[:, :N], ALU.mult, ["sy", "rstd"], ["sy"])
                                    stt(sob[:, ft, :N], sy[:, ft, :N], gssm[:, ft:ft + 1], gsT[:, ft, :N], ALU.mult, ALU.mult, ["sy", "gssm%d" % l, "gsT"], ["sqb"])
                                S.dma(sq["soT"].rearrange("(f p) t -> p f t", p=128)[:, :, t0:t0 + N], sob[:, :, :N], reads=["sqb"], chan="st_sob")
                    S.barrier()

                chk(10)
                with contextlib.ExitStack() as L2:
                    woutb = sb(L2, "woutb%d" % l, [128, 8, D], BF16)
                    gatt = sb(L2, "gatt%d" % l, [128, 4], F32)
                    NKmax = max(T, PL + TS)
                    nblk_max = (NKmax + 127) // 128
                    kTs = sb(L2, "kTs%d" % l, [128, 4, nblk_max * 128], BF16)
                    vsb = sb(L2, "vsb%d" % l, [128, nblk_max, 512], BF16)
                    qTt = sb(L2, "qTt%d" % l, [128, 4, 512], BF16)
                    gaTt = sb(L2, "gaTt%d" % l, [128, 4, 512], BF16)
                    catT = sb(L2, "catT%d" % l, [128, 8, 512], BF16)
                    ckst = [sb(L2, "ckst%d_%d" % (l, i), [128, 512], BF16) for i in range(2)]
                    e_sb = [sb(L2, "e_sb%d_%d" % (l, i), [128, 512], F32) for i in range(3)]
                    sp_sb = [sb(L2, "sp_sb%d_%d" % (l, i), [128, 512], BF16) for i in range(3)]
                    w_sb = [sb(L2, "w_sb%d_%d" % (l, i), [128, 512], BF16) for i in range(3)]
                    spsum = sb(L2, "spsum%d" % l, [128, 512], BF16)
                    att = sb(L2, "att%d" % l, [128, 4, 512], F32)
                    asq = sb(L2, "asq%d" % l, [128, 4, 512], BF16)
                    rstd2 = sb(L2, "rstd2%d" % l, [128, 512], F32)
                    xt2 = [sb(L2, "xt2%d_%d" % (l, i), [128, D], F32) for i in range(2)]
                    xn = [sb(L2, "xn%d_%d" % (l, i), [128, D], F32) for i in range(2)]
                    junk2 = sb(L2, "junk2%d" % l, [128, D], BF16)
                    ssq2 = sb(L2, "ssq2%d" % l, [128, 4], F32)
                    pA = [L2.enter_context(nc.psum_tensor("pA%d_%d" % (l, i), [128, 512], F32)) for i in range(3)]
                    pAtt = [L2.enter_context(nc.psum_tensor("pAtt%d_%d" % (l, i), [128, 512], F32)) for i in range(2)]
                    pN = L2.enter_context(nc.psum_tensor("pN%d" % l, [128, 512], F32))
                    pO = [L2.enter_context(nc.psum_tensor("pO%d_%d" % (l, i), [128, 512], F32)) for i in range(2)]
                    ptk = pO[0][:].bitcast(BF16)
                    for kc in range(8):
                        S.dma(woutb[:, kc, :], w_out[l, kc * 128:(kc + 1) * 128, :], writes=["woutb"], queue="pool")
                    S.dma(gatt[:], g_att[l].rearrange("(c p) -> p c", p=128), writes=["gatt"], allow_slow_non_contiguous=True)
                    c2 = dict(x=0, o=0, ck=0, blk=0)
                    for si, sq in enumerate(seqs):
                        NT, N, b, past = sq["NT"], sq["N"], sq["b"], sq["past"]
                        R = min(128, N); nsub = N // R
                        xsrc = sq["xin"] if l == 0 else (sq["xa"] if l % 2 == 1 else sq["xb"])
                        xdst = sq["xa"] if l % 2 == 0 else sq["xb"]
                        npast = past // 128
                        for pb in range(npast):
                            ci = c2["ck"] % 2; c2["ck"] += 1
                            ckk = "ckst%d" % ci
                            S.dma(ckst[ci][:], ck[l, b, pb * 128:(pb + 1) * 128, :], writes=[ckk], queue="pool")
                            for hp in range(4):
                                tp(ptk[:, hp * 128:(hp + 1) * 128], ckst[ci][:, hp * 128:(hp + 1) * 128], identb[:], [ckk, "identb"], ["pO0"])
                            cp("dve", kTs[:, :, pb * 128:(pb + 1) * 128], ptk[:, 0:512].rearrange("p (h s) -> p h s", h=4), ["pO0"], ["kTs"])
                        if npast:
                            S.dma(vsb[:, 0:npast, :], cv[l, b].rearrange("(n p) f -> p n f", p=128), writes=["vsb"], queue="pool")
                        S.dma(kTs[:, :, past:past + NT], sq["kT"].rearrange("(h p) t -> p h t", p=128), writes=["kTs"])
                        if NT >= 128:
                            S.dma(vsb[:, npast:npast + NT // 128, :], sq["vv"].rearrange("(n p) f -> p n f", p=128), writes=["vsb"])
                        else:
                            S.dma(vsb[:NT, npast, :], sq["vv"], writes=["vsb"])
                        for ti in range(NT // N):
                            t0 = ti * N
                            S.dma(qTt[:, :, :N], sq["qT"].rearrange("(h p) t -> p h t", p=128)[:, :, t0:t0 + N], writes=["qTt"])
                            S.dma(gaTt[:, :, :N], sq["gaT"].rearrange("(h p) t -> p h t", p=128)[:, :, t0:t0 + N], writes=["gaTt"])
                            S.dma(catT[:, 4:8, :N], sq["soT"].rearrange("(h p) t -> p h t", p=128)[:, :, t0:t0 + N], writes=["catT"])
                            chk(11)
                            blocks = []
                            if sq["kind"] == "p":
                                for j in reversed(range(nsub)):
                                    blocks.append((t0 + j * 128, 128, (t0 + j * 128) // 128, masks[:, j, :N]))
                                for kb in reversed(range(t0 // 128)):
                                    blocks.append((kb * 128, 128, kb, None))
                            else:
                                blocks.append((past, NT, npast, masks_s[:, :]))
                                for kb in reversed(range(npast)):
                                    blocks.append((kb * 128, 128, kb, None))
                            work = [(h, bi) for h in range(8) for bi in range(len(blocks))]

                            def stage1(idx):
                                h, bi = work[idx]
                                s0, Rk, vb, mk = blocks[bi]
                                hp, base = h // 2, 64 * (h % 2)
                                sl = idx % 3
                                Ak, ek, spk = "pA%d" % sl, "e_sb%d" % sl, "sp_sb%d" % sl
                                mm(pA[sl][:Rk, :N], kTs[base:base + 64, hp, s0:s0 + Rk], qTt[base:base + 64, hp, :N], True, False, ["kTs", "qTt"], [Ak])
                                act(e_sb[sl][:Rk, :N], pA[sl][:Rk, :N], AF.Exp, [Ak], [ek])
                                act(sp_sb[sl][:Rk, :N], e_sb[sl][:Rk, :N], AF.Ln, [ek], [spk], bias=1.0)
                                if mk is not None:
                                    tt("pool", sp_sb[sl][:Rk, :N], sp_sb[sl][:Rk, :N], mk[:Rk, :N] if mk.shape[0] != Rk else mk, ALU.mult, [spk, "masks"], [spk])

                            def stage2(idx):
                                h, bi = work[idx]
                                s0, Rk, vb, mk = blocks[bi]
                                hp, base = h // 2, 64 * (h % 2)
                                sl = idx % 3
                                Ak, spk, wk = "pA%d" % sl, "sp_sb%d" % sl, "w_sb%d" % sl
                                ai = hp % 2; atk = "pAtt%d" % ai
                                first, lastb = (bi == 0), (bi == len(blocks) - 1)
                                mm(pA[sl][:Rk, :N], negU[:Rk, :Rk], sp_sb[sl][:Rk, :N], False, first, [spk, "negU"], [Ak])
                                if not first:
                                    mm(pA[sl][:Rk, :N], negones[:, :Rk], spsum[:, :N], False, True, ["spsum", "negones"], [Ak])
                                act(w_sb[sl][:Rk, :N], pA[sl][:Rk, :N], AF.Exp, [Ak], [wk])
                                if mk is not None:
                                    tt("pool", w_sb[sl][:Rk, :N], w_sb[sl][:Rk, :N], mk[:Rk, :N] if mk.shape[0] != Rk else mk, ALU.mult, [wk, "masks"], [wk])
                                mm(pAtt[ai][base:base + 64, :N], vsb[:Rk, vb, h * 64:(h + 1) * 64], w_sb[sl][:Rk, :N], first, lastb, [wk, "vsb"], [atk])
                                if not lastb:
                                    if first:
                                        if Rk < 128:
                                            mset("pool", spsum[:, :N], 0.0, ["spsum"])
                                        cp("pool", spsum[:Rk, :N], sp_sb[sl][:Rk, :N], [spk], ["spsum"])
                                    else:
                                        tt("pool", spsum[:Rk, :N], spsum[:Rk, :N], sp_sb[sl][:Rk, :N], ALU.add, [spk, "spsum"], ["spsum"])
                                else:
                                    cp("dve", att[base:base + 64, hp, :N], pAtt[ai][base:base + 64, :N], [atk], ["att"])

                            LA = 2
                            for idx in range(len(work) + LA):
                                if idx < len(work):
                                    stage1(idx)
                                if idx >= LA:
                                    stage2(idx - LA)
                            chk(12)
                            for hp in range(4):
                                tt("pool", asq[:, hp, :N], att[:, hp, :N], att[:, hp, :N], ALU.mult, ["att"], ["asq"])
                            for hp in range(4):
                                mm(pN[:, :N], onesb[:], asq[:, hp, :N], hp == 0, hp == 3, ["onesb", "asq"], ["pN"])
                            act(rstd2[:, :N], pN[:, :N], AF.Ln, ["pN"], ["rstd2"], scale=1.0 / 512, bias=EPS)
                            act(rstd2[:, :N], rstd2[:, :N], AF.Exp, ["rstd2"], ["rstd2"], scale=-0.5)
                            for hp in range(4):
                                tt("dve", att[:, hp, :N], att[:, hp, :N], rstd2[:, :N], ALU.mult, ["att", "rstd2"], ["att"])
                                stt(catT[:, hp, :N], att[:, hp, :N], gatt[:, hp:hp + 1], gaTt[:, hp, :N], ALU.mult, ALU.mult, ["att", "gatt", "gaTt"], ["catT"])
                            chk(13)
                            for sub in range(nsub):
                                xi = c2["x"] % 2; c2["x"] += 1
                                xk, nk = "xt2%d" % xi, "xn%d" % xi
                                r0 = t0 + sub * R
                                S.dma(xt2[xi][:R, :], xsrc[r0:r0 + R, :], writes=[xk])
                                for half in range(2):
                                    oi = c2["o"] % 2; c2["o"] += 1
                                    ok = "pO%d" % oi
                                    for kc in range(8):
                                        mm(pO[oi][:R, :], catT[:, kc, sub * R:sub * R + R], woutb[:, kc, half * 512:(half + 1) * 512], kc == 0, kc == 7, ["catT", "woutb"], [ok])
                                    tt("dve", xn[xi][:R, half * 512:(half + 1) * 512], pO[oi][:R, :], xt2[xi][:R, half * 512:(half + 1) * 512], ALU.add, [ok, xk], [nk])
                                if not last:
                                    S.dma(xdst[r0:r0 + R, :], xn[xi][:R, :], reads=[nk], chan="st_" + nk)
                                else:
                                    act(junk2[:R, :], xn[xi][:R, :], AF.Square, [nk], ["junk2"])
                                    S.op("dve", lambda e: e.tensor_reduce(out=ssq2[:R, 0:1], in_=junk2[:R, :], axis=mybir.AxisListType.X, op=ALU.add), ["junk2"], ["ssq2"])
                                    act(ssq2[:R, 1:2], ssq2[:R, 0:1], AF.Ln, ["ssq2"], ["ssq2"], scale=1.0 / D, bias=EPS)
                                    act(ssq2[:R, 2:3], ssq2[:R, 1:2], AF.Exp, ["ssq2"], ["ssq2"], scale=-0.5)
                                    stt(xt2[xi][:R, :], xn[xi][:R, :], ssq2[:R, 2:3], fgbc[:R, :], ALU.mult, ALU.mult, [nk, "ssq2", "fgbc"], [xk])
                                    S.dma(sq["yout"][r0:r0 + R, :], xt2[xi][:R, :], reads=[xk], chan="st_" + xk)
                    S.barrier()
        except StopBuild:
            pass
        S._final = True
        S.barrier()
    return nc


_CACHE = {}


def _run(inputs, T, TS, PL, DEPTH):
    key = (T, TS, PL, DEPTH)
    if key not in _CACHE:
        _CACHE[key] = build(T, TS, PL, DEPTH)
    nc = _CACHE[key]
    f = lambda a: np.ascontiguousarray(np.asarray(a, dtype=np.float32))
    in_maps = []
    for c in range(8):
        sl = slice(2 * c, 2 * c + 2)
        m = {
            "xp": f(inputs["x_prompt"][sl]), "xs": f(inputs["x_sample"][sl]),
            "ck": f(np.asarray(inputs["cache_k"])[:, sl].reshape(DEPTH, 2, PL, 512)),
            "cv": f(np.asarray(inputs["cache_v"])[:, sl].reshape(DEPTH, 2, PL, 512)),
            "sre": f(np.asarray(inputs["state_ssm_re"])[:, sl]), "sim": f(np.asarray(inputs["state_ssm_im"])[:, sl]),
            "ln_g": f(inputs["ln_g"]), "w_in": f(inputs["w_in"]),
            "a_re": f(inputs["ssm_a_re"]), "a_im": f(inputs["ssm_a_im"]), "log_dt": f(inputs["ssm_log_dt"]),
            "b_re": f(inputs["ssm_b_re"]), "b_im": f(inputs["ssm_b_im"]),
            "c_re": f(inputs["ssm_c_re"]), "c_im": f(inputs["ssm_c_im"]),
            "ssm_d": f(inputs["ssm_d"]), "w_glu": f(inputs["w_glu"]), "b_glu": f(inputs["b_glu"]),
            "g_att": f(inputs["g_att"]), "g_ssm": f(inputs["g_ssm"]), "w_out": f(inputs["w_out"]),
            "final_g": f(inputs["final_g"]),
        }
        in_maps.append(m)
    res = run_bass_kernel_spmd(nc, in_maps, core_ids=list(range(8)))
    R = res.results
    cat0 = lambda k: np.concatenate([r[k] for r in R], axis=0)
    cat1 = lambda k: np.concatenate([r[k] for r in R], axis=1)
    y_p = cat0("yp"); y_s = cat0("ys")
    k_p = cat1("kp").reshape(DEPTH, 16, T, 8, 64); v_p = cat1("vp").reshape(DEPTH, 16, T, 8, 64)
    k_s = cat1("ks").reshape(DEPTH, 16, TS, 8, 64); v_s = cat1("vs").reshape(DEPTH, 16, TS, 8, 64)
    return (y_p, y_s, k_p, v_p, cat1("hrp"), cat1("hip"), k_s, v_s, cat1("hrs"), cat1("his"))


def kernel(**inputs):
    T = int(np.shape(inputs["x_prompt"])[1]); TS = int(np.shape(inputs["x_sample"])[1])
    PL = int(np.shape(inputs["cache_k"])[2]); DEPTH = int(np.shape(inputs["w_in"])[0])
    return _run(inputs, T, TS, PL, DEPTH)
```

```python
import contextlib
import math
import numpy as np
import concourse.bass as bass
import concourse.mybir as mybir
from concourse.bass_utils import run_bass_kernel_spmd

F32 = mybir.dt.float32
BF16 = mybir.dt.bfloat16
I32 = mybir.dt.int32
AF = mybir.ActivationFunctionType
ALU = mybir.AluOpType
EPS = 1e-6
TWO_PI = 2.0 * math.pi


class Sched:
    def __init__(self, nc, stack):
        self.nc = nc
        self.stack = stack
        self.eng = {"pe": nc.tensor, "act": nc.scalar, "dve": nc.vector,
                    "pool": nc.gpsimd, "sp": nc.sync}
        self.sem = {}
        self.cnt = {}
        for e in ("pe", "act", "dve", "pool"):
            self.sem[e] = stack.enter_context(nc.semaphore("s_" + e))
            self.cnt[e] = 0
        self.waited = {e: {} for e in self.eng}
        self.lastw = {}
        self.readers = {}
        self.nchan = 0
        self.stopped = False

    def chan(self, name):
        if name not in self.sem:
            self.sem[name] = self.stack.enter_context(self.nc.semaphore("d%d" % self.nchan))
            self.nchan += 1
            self.cnt[name] = 0
        return name

    def _deps(self, engine, reads, writes):
        deps = {}

        def add(d, raw):
            if d is None:
                return
            src, c = d
            if src == engine and not raw:
                return
            if src == "pe" and engine == "pe":
                return
            if deps.get(src, 0) < c:
                deps[src] = c

        for r in reads:
            add(self.lastw.get(r), True)
        for w in writes:
            add(self.lastw.get(w), True)
            for src, c in self.readers.get(w, {}).items():
                add((src, c), False)
        return deps

    def _emit_waits(self, engine, deps):
        e = self.eng[engine]
        wd = self.waited[engine]
        for src, c in deps.items():
            if wd.get(src, 0) >= c:
                continue
            e.wait_ge(self.sem[src], c)
            wd[src] = c

    def op(self, engine, fn, reads=(), writes=()):
        if self.stopped:
            return None
        deps = self._deps(engine, reads, writes)
        self._emit_waits(engine, deps)
        ins = fn(self.eng[engine])
        ins.then_inc(self.sem[engine], 1)
        self.cnt[engine] += 1
        c = self.cnt[engine]
        for r in reads:
            self.readers.setdefault(r, {})[engine] = c
        for w in writes:
            self.lastw[w] = (engine, c)
            self.readers[w] = {}
        return ins

    def dma(self, out, in_, reads=(), writes=(), chan=None, queue="sp", **kw):
        if self.stopped:
            return None
        if chan is None:
            chan = "c_" + str(writes[0] if writes else reads[0])
        self.chan(chan)
        deps = self._deps("__dma__", reads, writes)
        self._emit_waits(queue, deps)
        ins = self.eng[queue].dma_start(out=out, in_=in_, **kw)
        ins.then_inc(self.sem[chan], 16)
        self.cnt[chan] += 16
        c = self.cnt[chan]
        for r in reads:
            self.readers.setdefault(r, {})[chan] = c
        for w in writes:
            self.lastw[w] = (chan, c)
            self.readers[w] = {}
        return ins

    def barrier(self):
        if self.stopped and getattr(self, "_final", False) is False:
            return
        allsrc = [s for s in self.cnt if self.cnt[s] > 0]
        for e in ("pe", "act", "dve", "pool", "sp"):
            self._emit_waits(e, {s: self.cnt[s] for s in allsrc if s != e})


class StopBuild(Exception):
    pass


def build(T, TS, PL, DEPTH):
    import os
    KSTOP = int(os.environ.get("KSTOP", "99"))

    def chk(level):
        if KSTOP <= level:
            S.stopped = True

    nc = bass.Bass("TRN2", target_bir_lowering=False)
    D = 1024

    def din(name, shape):
        return nc.dram_tensor(name, shape, F32, kind="ExternalInput").ap()

    def dout(name, shape):
        return nc.dram_tensor(name, shape, F32, kind="ExternalOutput").ap()

    def dscr(name, shape, dt):
        return nc.dram_tensor(name, shape, dt, kind="Internal").ap()

    xp = din("xp", [2, T, D]); xs = din("xs", [2, TS, D])
    ck = din("ck", [DEPTH, 2, PL, 512]); cv = din("cv", [DEPTH, 2, PL, 512])
    sre = din("sre", [DEPTH, 2, 32, 64]); sim = din("sim", [DEPTH, 2, 32, 64])
    ln_g = din("ln_g", [DEPTH, D]); w_in = din("w_in", [DEPTH, D, 3072])
    a_re = din("a_re", [DEPTH, 32, 64]); a_im = din("a_im", [DEPTH, 32, 64])
    log_dt = din("log_dt", [DEPTH, 32])
    b_re = din("b_re", [DEPTH, 32, 64, 16]); b_im = din("b_im", [DEPTH, 32, 64, 16])
    c_re = din("c_re", [DEPTH, 32, 16, 64]); c_im = din("c_im", [DEPTH, 32, 16, 64])
    ssm_d = din("ssm_d", [DEPTH, 512]); w_glu = din("w_glu", [DEPTH, 512, 512])
    b_glu = din("b_glu", [DEPTH, 512]); g_att = din("g_att", [DEPTH, 512])
    g_ssm = din("g_ssm", [DEPTH, 512]); w_out = din("w_out", [DEPTH, D, D])
    final_g = din("final_g", [D])
    yp = dout("yp", [2, T, D]); ys = dout("ys", [2, TS, D])
    kp = dout("kp", [DEPTH, 2, T, 512]); vp = dout("vp", [DEPTH, 2, T, 512])
    hrp = dout("hrp", [DEPTH, 2, 32, 64]); hip = dout("hip", [DEPTH, 2, 32, 64])
    ks = dout("ks", [DEPTH, 2, TS, 512]); vs = dout("vs", [DEPTH, 2, TS, 512])
    hrs = dout("hrs", [DEPTH, 2, 32, 64]); his = dout("his", [DEPTH, 2, 32, 64])

    seqs = []
    for b in range(2):
        seqs.append(dict(kind="p", b=b, NT=T, xin=xp[b], yout=yp[b], past=0))
    for b in range(2):
        seqs.append(dict(kind="s", b=b, NT=TS, xin=xs[b], yout=ys[b], past=PL))
    for si, sq in enumerate(seqs):
        NT = sq["NT"]
        sq["xa"] = dscr("xa%d" % si, [NT, D], F32)
        sq["xb"] = dscr("xb%d" % si, [NT, D], F32)
        for nm in ("qT", "kT", "gaT", "soT"):
            sq[nm] = dscr("%s%d" % (nm, si), [512, NT], BF16)
        sq["vv"] = dscr("vv%d" % si, [NT, 512], BF16)
        sq["N"] = min(512, NT)

    with contextlib.ExitStack() as top:
        S = Sched(nc, top)

        def sb(stack, name, shape, dt):
            return stack.enter_context(nc.sbuf_tensor(name, shape, dt))

        def mm(out, lhsT, rhs, start, stop, r, w):
            S.op("pe", lambda e: e.matmul(out, lhsT, rhs, start=start, stop=stop), r, w)

        def tp(out, in_, ident, r, w):
            S.op("pe", lambda e: e.transpose(out, in_, ident), r, w)

        def act(out, in_, func, r, w, scale=1.0, bias=0.0):
            S.op("act", lambda e: e.activation(out=out, in_=in_, func=func, bias=bias, scale=scale), r, w)

        def tt(eng, out, in0, in1, op, r, w):
            S.op(eng, lambda e: e.tensor_tensor(out=out, in0=in0, in1=in1, op=op), r, w)

        def ts(eng, out, in0, s1, op0, r, w, s2=None, op1=None):
            if op1 is None:
                S.op(eng, lambda e: e.tensor_scalar(out=out, in0=in0, scalar1=s1, scalar2=None, op0=op0), r, w)
            else:
                S.op(eng, lambda e: e.tensor_scalar(out=out, in0=in0, scalar1=s1, scalar2=s2, op0=op0, op1=op1), r, w)

        def stt(out, in0, scalar, in1, op0, op1, r, w):
            S.op("dve", lambda e: e.scalar_tensor_tensor(out=out, in0=in0, scalar=scalar, in1=in1, op0=op0, op1=op1), r, w)

        def cp(eng, out, in_, r, w):
            S.op(eng, lambda e: e.tensor_copy(out=out, in_=in_), r, w)

        def recip(out, in_, r, w):
            S.op("dve", lambda e: e.reciprocal(out=out, in_=in_), r, w)

        def mset(eng, ap, val, w):
            S.op(eng, lambda e: e.memset(ap, val), (), w)

        identb = sb(top, "identb", [128, 128], BF16)
        identf = sb(top, "identf", [128, 128], F32)
        negU = sb(top, "negU", [128, 128], BF16)
        negones = sb(top, "negones", [128, 128], BF16)
        onesb = sb(top, "onesb", [128, 128], BF16)
        mask32 = sb(top, "mask32", [128, 128], F32)
        fgbc = sb(top, "fgbc", [128, D], F32)
        NP = seqs[0]["N"]
        ndiag = NP // 128
        masks = sb(top, "masks", [128, ndiag, NP], BF16)
        masks_s = sb(top, "masks_s", [TS, TS], BF16)
        mset("pool", identb[:], 0.0, ["identb"])
        S.op("pool", lambda e: e.affine_select(out=identb[:], in_=identb[:], pattern=[[-1, 128]], compare_op=ALU.not_equal, fill=1.0, base=0, channel_multiplier=1), ["identb"], ["identb"])
        mset("pool", identf[:], 0.0, ["identf"])
        S.op("pool", lambda e: e.affine_select(out=identf[:], in_=identf[:], pattern=[[-1, 128]], compare_op=ALU.not_equal, fill=1.0, base=0, channel_multiplier=1), ["identf"], ["identf"])
        mset("pool", negU[:], -1.0, ["negU"])
        S.op("pool", lambda e: e.affine_select(out=negU[:], in_=negU[:], pattern=[[-1, 128]], compare_op=ALU.is_ge, fill=0.0, base=0, channel_multiplier=1), ["negU"], ["negU"])
        mset("pool", negones[:], -1.0, ["negones"])
        mset("pool", onesb[:], 1.0, ["onesb"])
        mset("pool", mask32[:], 0.0, ["mask32"])
        for j in range(4):
            mset("pool", mask32[32 * j:32 * j + 32, 32 * j:32 * j + 32], 1.0, ["mask32"])
        mset("pool", masks[:], 1.0, ["masks"])
        for j in range(ndiag):
            S.op("pool", lambda e: e.affine_select(out=masks[:, j, :], in_=masks[:, j, :], pattern=[[1, NP]], compare_op=ALU.is_gt, fill=0.0, base=-128 * j, channel_multiplier=-1), ["masks"], ["masks"])
        mset("pool", masks_s[:], 1.0, ["masks_s"])
        S.op("pool", lambda e: e.affine_select(out=masks_s[:], in_=masks_s[:], pattern=[[1, TS]], compare_op=ALU.is_gt, fill=0.0, base=0, channel_multiplier=-1), ["masks_s"], ["masks_s"])
        S.dma(fgbc[:], final_g.partition_broadcast(128), writes=["fgbc"])

        try:
            chk(0)
            for l in range(DEPTH):
                last = (l == DEPTH - 1)
                with contextlib.ExitStack() as L1:
                    winb = sb(L1, "winb%d" % l, [128, 8, 3072], BF16)
                    wglub = sb(L1, "wglub%d" % l, [128, 4, 512], BF16)
                    gbc = sb(L1, "gbc%d" % l, [128, D], F32)
                    dvec = sb(L1, "dvec%d" % l, [128, 4], F32)
                    nbg = sb(L1, "nbg%d" % l, [128, 4], F32)
                    gssm = sb(L1, "gssm%d" % l, [128, 4], F32)
                    PWr = sb(L1, "PWr%d" % l, [128, 9, 16], F32)
                    PWi = sb(L1, "PWi%d" % l, [128, 9, 16], F32)
                    A1 = sb(L1, "A1%d" % l, [128, 2, 16], F32)
                    Aip = sb(L1, "Aip%d" % l, [128, 16], F32)
                    Ain = sb(L1, "Ain%d" % l, [128, 16], F32)
                    BD = sb(L1, "BD%d" % l, [128, 4, 8, 128], BF16)
                    BL = sb(L1, "BL%d" % l, [128, 4, 8, 2, 128], BF16)
                    CA = sb(L1, "CA%d" % l, [128, 8, 2, 16, 32], BF16)
                    for kc in range(8):
                        S.dma(winb[:, kc, :], w_in[l, kc * 128:(kc + 1) * 128, :], writes=["winb"], queue="pool")
                    S.dma(wglub[:], w_glu[l].rearrange("(kc p) n -> p kc n", p=128), writes=["wglub"], queue="pool")
                    S.dma(gbc[:], ln_g[l].partition_broadcast(128), writes=["gbc"])
                    for tl, src, nm in ((dvec, ssm_d, "dvec%d" % l), (nbg, b_glu, "nbg%d" % l), (gssm, g_ssm, "gssm%d" % l)):
                        S.dma(tl[:], src[l].rearrange("(c p) -> p c", p=128), writes=[nm], allow_slow_non_contiguous=True)
                    ts("dve", nbg[:], nbg[:], -1.0, ALU.mult, ["nbg%d" % l], ["nbg%d" % l])

                    with contextlib.ExitStack() as SU:
                        k_ = "su"
                        aqr = sb(SU, "aqr%d" % l, [128, 16], F32); aqi = sb(SU, "aqi%d" % l, [128, 16], F32)
                        dtq = sb(SU, "dtq%d" % l, [128, 16], F32)
                        adt = sb(SU, "adt%d" % l, [128, 16], F32); ang = sb(SU, "ang%d" % l, [128, 16], F32)
                        mag = sb(SU, "mag%d" % l, [128, 9, 16], F32)
                        ph = sb(SU, "ph%d" % l, [128, 2, 9, 16], F32)
                        pht = sb(SU, "pht%d" % l, [128, 2, 9, 16], F32)
                        phi = sb(SU, "phi%d" % l, [128, 2, 9, 16], I32)
                        sc = sb(SU, "sc%d" % l, [128, 2, 9, 16], F32)
                        t1 = sb(SU, "t1%d" % l, [128, 16], F32); t2 = sb(SU, "t2%d" % l, [128, 16], F32)
                        t3 = sb(SU, "t3%d" % l, [128, 16], F32)
                        fr = sb(SU, "fr%d" % l, [128, 16], F32); fi = sb(SU, "fi%d" % l, [128, 16], F32)
                        Bq0r = sb(SU, "Bq0r%d" % l, [128, 16, 32], F32); Bq0i = sb(SU, "Bq0i%d" % l, [128, 16, 32], F32)
                        Bbr = sb(SU, "Bbr%d" % l, [128, 16, 32], F32); Bbi = sb(SU, "Bbi%d" % l, [128, 16, 32], F32)
                        nBbi = sb(SU, "nBbi%d" % l, [128, 16, 32], F32)
                        u1 = sb(SU, "u1%d" % l, [128, 16, 32], F32); u2 = sb(SU, "u2%d" % l, [128, 16, 32], F32)
                        Yr = sb(SU, "Yr%d" % l, [128, 16, 32], F32); Yi = sb(SU, "Yi%d" % l, [128, 16, 32], F32)
                        Zr = sb(SU, "Zr%d" % l, [32, 16, 128], F32); Zi = sb(SU, "Zi%d" % l, [32, 16, 128], F32)
                        CTr = sb(SU, "CTr%d" % l, [128, 16, 32], F32); CTi = sb(SU, "CTi%d" % l, [128, 16, 32], F32)
                        Xr = sb(SU, "Xr%d" % l, [128, 9, 16, 32], F32); Xi = sb(SU, "Xi%d" % l, [128, 9, 16, 32], F32)
                        pss = SU.enter_context(nc.psum_tensor("pss%d" % l, [128, 4, 128], F32))
                        psc = SU.enter_context(nc.psum_tensor("psc%d" % l, [128, 16, 32], F32))
                        K = [k_]
                        S.dma(aqr[:], a_re[l].rearrange("(pr g2) p -> (g2 p) pr", g2=2), writes=K, allow_slow_non_contiguous=True)
                        S.dma(aqi[:], a_im[l].rearrange("(pr g2) p -> (g2 p) pr", g2=2), writes=K, allow_slow_non_contiguous=True)
                        mset("dve", Bq0r[:], 0.0, K); mset("dve", Bq0i[:], 0.0, K)
                        mset("dve", Zr[:], 0.0, K); mset("dve", Zi[:], 0.0, K)
                        for g2 in range(2):
                            S.dma(dtq[g2 * 64:(g2 + 1) * 64, :], log_dt[l].rearrange("(pr g2) -> g2 pr", g2=2)[g2].partition_broadcast(64), reads=K, writes=K, allow_slow_non_contiguous=True)
                            S.dma(Bq0r[g2 * 64:(g2 + 1) * 64, :, g2 * 16:(g2 + 1) * 16], b_re[l].rearrange("(pr g2) p c -> g2 p pr c", g2=2)[g2], reads=K, writes=K)
                            S.dma(Bq0i[g2 * 64:(g2 + 1) * 64, :, g2 * 16:(g2 + 1) * 16], b_im[l].rearrange("(pr g2) p c -> g2 p pr c", g2=2)[g2], reads=K, writes=K)
                            S.dma(Zr[g2 * 16:(g2 + 1) * 16, :, g2 * 64:(g2 + 1) * 64], c_re[l].rearrange("(pr g2) c p -> g2 c pr p", g2=2)[g2], reads=K, writes=K)
                            S.dma(Zi[g2 * 16:(g2 + 1) * 16, :, g2 * 64:(g2 + 1) * 64], c_im[l].rearrange("(pr g2) c p -> g2 c pr p", g2=2)[g2], reads=K, writes=K)
                        act(dtq[:], dtq[:], AF.Exp, K, K)
                        tt("dve", adt[:], aqr[:], dtq[:], ALU.mult, K, K)
                        tt("dve", ang[:], aqi[:], dtq[:], ALU.mult, K, K)
                        for dl in range(9):
                            act(mag[:, dl, :], adt[:], AF.Exp, K, K, scale=float(dl))
                            ts("dve", ph[:, 0, dl, :], ang[:], float(dl), ALU.mult, K, K)
                            ts("dve", ph[:, 1, dl, :], ang[:], float(dl), ALU.mult, K, K, s2=math.pi / 2, op1=ALU.add)
                        ts("dve", pht[:], ph[:], 1.0 / TWO_PI, ALU.mult, K, K)
                        cp("dve", phi[:], pht[:], K, K)
                        cp("dve", pht[:], phi[:], K, K)
                        stt(ph[:], pht[:], -TWO_PI, ph[:], ALU.mult, ALU.add, K, K)
                        ts("dve", ph[:], ph[:], -math.pi, ALU.max, K, K, s2=math.pi, op1=ALU.min)
                        act(sc[:], ph[:], AF.Sin, K, K)
                        chk(1)
                        tt("dve", PWi[:], mag[:], sc[:, 0], ALU.mult, K, ["PW"])
                        tt("dve", PWr[:], mag[:], sc[:, 1], ALU.mult, K, ["PW"])
                        cp("dve", A1[:, 0, :], PWr[:, 8, :], ["PW"], ["A8"]); cp("dve", A1[:, 1, :], PWr[:, 8, :], ["PW"], ["A8"])
                        cp("dve", Aip[:], PWi[:, 8, :], ["PW"], ["A8"])
                        ts("dve", Ain[:], PWi[:, 8, :], -1.0, ALU.mult, ["PW"], ["A8"])
                        tt("dve", t1[:], aqr[:], aqr[:], ALU.mult, K, K)
                        tt("dve", t2[:], aqi[:], aqi[:], ALU.mult, K, K)
                        tt("dve", t1[:], t1[:], t2[:], ALU.add, K, K)
                        recip(t1[:], t1[:], K, K)
                        ts("dve", t2[:], PWr[:, 1, :], -1.0, ALU.add, K + ["PW"], K)
                        tt("dve", fr[:], t2[:], aqr[:], ALU.mult, K, K)
                        tt("dve", t3[:], PWi[:, 1, :], aqi[:], ALU.mult, K + ["PW"], K)
                        tt("dve", fr[:], fr[:], t3[:], ALU.add, K, K)
                        tt("dve", fr[:], fr[:], t1[:], ALU.mult, K, K)
                        tt("dve", fi[:], PWi[:, 1, :], aqr[:], ALU.mult, K + ["PW"], K)
                        tt("dve", t3[:], t2[:], aqi[:], ALU.mult, K, K)
                        tt("dve", fi[:], fi[:], t3[:], ALU.subtract, K, K)
                        tt("dve", fi[:], fi[:], t1[:], ALU.mult, K, K)
                        bc = lambda ap: ap.unsqueeze(2).to_broadcast([128, 16, 32])
                        tt("dve", u1[:], Bq0r[:], bc(fr[:]), ALU.mult, K, K)
                        tt("dve", u2[:], Bq0i[:], bc(fi[:]), ALU.mult, K, K)
                        tt("dve", Bbr[:], u1[:], u2[:], ALU.subtract, K, K)
                        tt("dve", u1[:], Bq0i[:], bc(fr[:]), ALU.mult, K, K)
                        tt("dve", u2[:], Bq0r[:], bc(fi[:]), ALU.mult, K, K)
                        tt("dve", Bbi[:], u1[:], u2[:], ALU.add, K, K)
                        ts("dve", nBbi[:], Bbi[:], -1.0, ALU.mult, K, K)
                        for (Z, CT) in ((Zr, CTr), (Zi, CTi)):
                            for pr in range(16):
                                tp(psc[:, pr, :], Z[:, pr, :], identf[0:32, 0:32], K + ["identf"], ["psc"])
                            cp("dve", CT[:], psc[:], ["psc"], K)
                        for dl in range(9):
                            pr_ = bc(PWr[:, dl, :]); pi_ = bc(PWi[:, dl, :])
                            tt("dve", u1[:], CTr[:], pr_, ALU.mult, K + ["PW"], K)
                            tt("dve", u2[:], CTi[:], pi_, ALU.mult, K + ["PW"], K)
                            tt("dve", Xr[:, dl], u1[:], u2[:], ALU.subtract, K, K)
                            tt("dve", u1[:], CTr[:], pi_, ALU.mult, K + ["PW"], K)
                            tt("dve", u2[:], CTi[:], pr_, ALU.mult, K + ["PW"], K)
                            tt("dve", Xi[:, dl], u1[:], u2[:], ALU.add, K, K)
                        for tau in range(8):
                            cp("dve", CA[:, tau, 0, :, :], Xr[:, tau + 1], K, ["CA"])
                            ts("dve", CA[:, tau, 1, :, :], Xi[:, tau + 1], -1.0, ALU.mult, K, ["CA"])
                        for tau in range(8):
                            pr_ = bc(PWr[:, 7 - tau, :]); pi_ = bc(PWi[:, 7 - tau, :])
                            tt("dve", u1[:], Bbr[:], pr_, ALU.mult, K + ["PW"], K)
                            tt("dve", u2[:], Bbi[:], pi_, ALU.mult, K + ["PW"], K)
                            tt("dve", Yr[:], u1[:], u2[:], ALU.subtract, K, K)
                            tt("dve", u1[:], Bbr[:], pi_, ALU.mult, K + ["PW"], K)
                            tt("dve", u2[:], Bbi[:], pr_, ALU.mult, K + ["PW"], K)
                            tt("dve", Yi[:], u1[:], u2[:], ALU.add, K, K)
                            for ri, Y in ((0, Yr), (1, Yi)):
                                for ft in range(4):
                                    tp(pss[:, ft, :], Y[:, ft * 4:(ft + 1) * 4, :].rearrange("p a b -> p (a b)"), identf[:], K + ["identf"], ["pss"])
                                cp("dve", BL[:, :, tau, ri, :], pss[:], ["pss"], ["BL"])
                        for dl in range(8):
                            for ft in range(4):
                                fl = lambda t_: t_[:, ft * 4:(ft + 1) * 4, :].rearrange("p a b -> p (a b)")
                                mm(pss[:, ft, :], fl(Bbr), Xr[:, dl, ft * 4:(ft + 1) * 4, :].rearrange("p a b -> p (a b)"), True, False, K, ["pss"])
                                mm(pss[:, ft, :], fl(nBbi), Xi[:, dl, ft * 4:(ft + 1) * 4, :].rearrange("p a b -> p (a b)"), False, True, K, ["pss"])
                            tt("dve", BD[:, :, dl, :], pss[:], mask32[:].unsqueeze(1).to_broadcast([128, 4, 128]), ALU.mult, ["pss", "mask32"], ["BD"])
                    chk(2)
                    S.barrier()

                    with contextlib.ExitStack() as W1:
                        xt = [sb(W1, "xt%d_%d" % (l, i), [128, D], F32) for i in range(2)]
                        ssq = sb(W1, "ssq%d" % l, [128, 4], F32)
                        hn = sb(W1, "hn%d" % l, [128, D], BF16)
                        junk = hn
                        hnT = sb(W1, "hnT%d" % l, [128, 8, 512], BF16)
                        stg = [sb(W1, "stg%d_%d" % (l, i), [128, 512], BF16) for i in range(3)]
                        ef = [sb(W1, "ef%d_%d" % (l, i), [128, 512], F32) for i in range(2)]
                        kvst = [sb(W1, "kvst%d_%d" % (l, i), [128, 1024], F32) for i in range(2)]
                        vst = [sb(W1, "vst%d_%d" % (l, i), [128, 512], BF16) for i in range(2)]
                        uTs = [sb(W1, "uT%d_%d" % (l, i), [128, 4, 512], BF16) for i in range(2)]
                        gsTs = [sb(W1, "gsT%d_%d" % (l, i), [128, 4, 512], BF16) for i in range(2)]
                        Hall = sb(W1, "Hall%d" % l, [128, 65, 2, 16], F32)
                        Hbs = [sb(W1, "Hb%d_%d" % (l, i), [128, 2, 16, 64], BF16) for i in range(2)]
                        Hbns = [sb(W1, "Hbn%d_%d" % (l, i), [128, 2, 16, 64], BF16) for i in range(2)]
                        Pt = sb(W1, "Pt%d" % l, [128, 2, 16], F32)
                        Qt = sb(W1, "Qt%d" % l, [128, 2, 16], F32)
                        ysb = sb(W1, "ysb%d" % l, [128, 512], F32)
                        tg = sb(W1, "tg%d" % l, [128, 512], F32)
                        ygb = sb(W1, "ygb%d" % l, [128, 4, 512], BF16)
                        sy = sb(W1, "sy%d" % l, [128, 4, 512], F32)
                        sqb = sb(W1, "sqb%d" % l, [128, 4, 512], BF16)
                        rstd = sb(W1, "rstd%d" % l, [128, 512], F32)
                        sob = sqb
                        ptp = [W1.enter_context(nc.psum_tensor("ptp%d_%d" % (l, i), [128, 8, 128], BF16)) for i in range(1)]
                        ppj = [W1.enter_context(nc.psum_tensor("ppj%d_%d" % (l, i), [128, 512], F32)) for i in range(2)]
                        pS = [W1.enter_context(nc.psum_tensor("pS%d_%d" % (l, i), [128, 2, 4, 64], F32)) for i in range(4)]
                        py = [W1.enter_context(nc.psum_tensor("py%d_%d" % (l, i), [128, 8, 64], F32)) for i in range(1)]
                        cnt = dict(x=0, pj=0, stg=0, ef=0, kv=0, py=0)

                        def silu_evac(ps, pskey, out, outkey, N):
                            i = cnt["ef"] % 2; cnt["ef"] += 1
                            e_ = ef[i]; ek = "ef%d" % i
                            act(e_[:, :N], ps, AF.Exp, [pskey], [ek], scale=-1.0)
                            ts("dve", e_[:, :N], e_[:, :N], 1.0, ALU.add, [ek], [ek])
                            recip(e_[:, :N], e_[:, :N], [ek], [ek])
                            tt("dve", out, e_[:, :N], ps, ALU.mult, [ek, pskey], [outkey])

                        for si, sq in enumerate(seqs):
                            NT, N, b = sq["NT"], sq["N"], sq["b"]
                            R = min(128, N); nsub = N // R; nch = N // 8
                            xsrc = sq["xin"] if l == 0 else (sq["xa"] if l % 2 == 1 else sq["xb"])
                            kout, vout = (kp, vp) if sq["kind"] == "p" else (ks, vs)
                            if sq["kind"] == "p":
                                mset("pool", Hall[:, 0], 0.0, ["Hall"])
                            else:
                                S.dma(Hall[:, 0, 0, :], sre[l, b].rearrange("(pr g2) p -> (g2 p) pr", g2=2), writes=["Hall"], chan="hld", allow_slow_non_contiguous=True)
                                S.dma(Hall[:, 0, 1, :], sim[l, b].rearrange("(pr g2) p -> (g2 p) pr", g2=2), writes=["Hall"], chan="hld", allow_slow_non_contiguous=True)
                            def front(ti, bi_):
                                t0 = ti * N
                                uT = uTs[bi_]; gsT = gsTs[bi_]; Hb = Hbs[bi_]; Hbn = Hbns[bi_]
                                uk = 'uT%d' % bi_; gk = 'gsT%d' % bi_; hbk = 'Hb%d' % bi_; hbnk = 'Hbn%d' % bi_
                                for sub in range(nsub):
                                    xi = cnt["x"] % 2; cnt["x"] += 1
                                    xk = "xt%d" % xi
                                    r0 = t0 + sub * R
                                    S.dma(xt[xi][:R, :], xsrc[r0:r0 + R, :], writes=[xk])
                                    act(junk[:R, :], xt[xi][:R, :], AF.Square, [xk], ["hn"])
                                    S.op("dve", lambda e: e.tensor_reduce(out=ssq[:R, 0:1], in_=junk[:R, :], axis=mybir.AxisListType.X, op=ALU.add), ["hn"], ["ssq"])
                                    act(ssq[:R, 1:2], ssq[:R, 0:1], AF.Ln, ["ssq"], ["ssq"], scale=1.0 / D, bias=EPS)
                                    act(ssq[:R, 2:3], ssq[:R, 1:2], AF.Exp, ["ssq"], ["ssq"], scale=-0.5)
                                    stt(hn[:R, :], xt[xi][:R, :], ssq[:R, 2:3], gbc[:R, :], ALU.mult, ALU.mult, [xk, "ssq", "gbc"], ["hn"])
                                    pi_ = 0; pk = "ptp%d" % pi_
                                    for kc in range(8):
                                        tp(ptp[pi_][:, kc, :R], hn[:R, kc * 128:(kc + 1) * 128], identb[:R, :R], ["hn", "identb"], [pk])
                                    cp("dve", hnT[:, :, sub * R:sub * R + R], ptp[pi_][:, :, :R], [pk], ["hnT"])
                                chk(3)
                                for mt in list(range(0, 8)) + list(range(12, 24)):
                                    pj = cnt["pj"] % 2; cnt["pj"] += 1
                                    pjk = "ppj%d" % pj
                                    for kc in range(8):
                                        mm(ppj[pj][:, :N], winb[:, kc, mt * 128:(mt + 1) * 128], hnT[:, kc, :N], kc == 0, kc == 7, ["winb", "hnT"], [pjk])
                                    grp, ft = mt // 4, mt % 4
                                    if grp in (0, 1):
                                        sg = cnt["stg"] % 3; cnt["stg"] += 1
                                        sk = "stg%d" % sg
                                        if grp == 0:
                                            act(stg[sg][:, :N], ppj[pj][:, :N], AF.Copy, [pjk], [sk], scale=0.125)
                                        else:
                                            cp("dve", stg[sg][:, :N], ppj[pj][:, :N], [pjk], [sk])
                                        dst = sq["qT"] if grp == 0 else sq["kT"]
                                        S.dma(dst[ft * 128:(ft + 1) * 128, t0:t0 + N], stg[sg][:, :N], reads=[sk], chan="st_" + sk)
                                    elif grp == 3:
                                        sg = cnt["stg"] % 3; cnt["stg"] += 1
                                        sk = "stg%d" % sg
                                        silu_evac(ppj[pj][:, :N], pjk, stg[sg][:, :N], sk, N)
                                        S.dma(sq["gaT"][ft * 128:(ft + 1) * 128, t0:t0 + N], stg[sg][:, :N], reads=[sk], chan="st_" + sk)
                                    elif grp == 4:
                                        cp("dve", uT[:, ft, :N], ppj[pj][:, :N], [pjk], [uk])
                                    else:
                                        silu_evac(ppj[pj][:, :N], pjk, gsT[:, ft, :N], gk, N)
                                chk(4)
                                for sub in range(nsub):
                                    ki = cnt["kv"] % 2; cnt["kv"] += 1
                                    kk = "kvst%d" % ki; vk = "vst%d" % ki
                                    r0 = t0 + sub * R
                                    for half in range(2):
                                        pj = cnt["pj"] % 2; cnt["pj"] += 1
                                        pjk = "ppj%d" % pj
                                        for kc in range(8):
                                            mm(ppj[pj][:R, :], hnT[:, kc, sub * R:sub * R + R], winb[:, kc, 512 + half * 512:1024 + half * 512], kc == 0, kc == 7, ["winb", "hnT"], [pjk])
                                        if os.environ.get("KV", "abcde").find("d") >= 0:
                                            cp("dve", kvst[ki][:R, half * 512:(half + 1) * 512], ppj[pj][:R, :], [pjk], [kk])
                                        if half == 1 and os.environ.get("KV", "abcde").find("e") >= 0:
                                            act(vst[ki][:R, :], kvst[ki][:R, 512:1024], AF.Copy, [kk], [vk])
                                    if os.environ.get("KV", "abc").find("a") >= 0:
                                        S.dma(kout[l, b, r0:r0 + R, :], kvst[ki][:R, 0:512], reads=[kk], chan="st_" + kk)
                                    if os.environ.get("KV", "abc").find("b") >= 0:
                                        S.dma(vout[l, b, r0:r0 + R, :], kvst[ki][:R, 512:1024], reads=[kk], chan="st_" + kk)
                                    if os.environ.get("KV", "abc").find("c") >= 0:
                                        S.dma(sq["vv"][r0:r0 + R, :], vst[ki][:R, :], reads=[vk], chan="st_" + vk)
                                chk(5)
                                uv = uT[:, :, :N].rearrange("p f (k t) -> p f k t", t=8)
                                for j in range(4):
                                    pk = "pS%d" % j
                                    lo = 64 if j == 3 else 32 * j
                                    for ri in range(2):
                                        for ft in range(4):
                                            for tau in range(8):
                                                mm(pS[j][:, ri, ft, :nch], BL[lo:32 * j + 32, ft, tau, ri, :], uv[lo:32 * j + 32, ft, :, tau], tau == 0, tau == 7, ["BL", uk], [pk])
                                    cp("dve", Hall[:, 1:nch + 1, :, j::4].rearrange("p k r f -> p r f k"), pS[j][:, :, :, :nch], [pk], ["Hall"])
                                tt("pool", Hall[:, 1:nch + 1, :, 3::4], Hall[:, 1:nch + 1, :, 3::4], Hall[:, 1:nch + 1, :, 2::4], ALU.subtract, ["Hall"], ["Hall"])
                                chk(6)
                                for k in range(nch):
                                    tt("pool", Pt[:], Hall[:, k], A1[:], ALU.mult, ["Hall", "A8"], ["Pt"])
                                    tt("pool", Qt[:, 0, :], Hall[:, k, 1, :], Ain[:], ALU.mult, ["Hall", "A8"], ["Qt"])
                                    tt("pool", Qt[:, 1, :], Hall[:, k, 0, :], Aip[:], ALU.mult, ["Hall", "A8"], ["Qt"])
                                    tt("pool", Pt[:], Pt[:], Qt[:], ALU.add, ["Pt", "Qt"], ["Pt"])
                                    tt("pool", Hall[:, k + 1], Hall[:, k + 1], Pt[:], ALU.add, ["Pt", "Hall"], ["Hall"])
                                for ri in range(2):
                                    cp("pool", Hb[:, ri, :, :nch], Hall[:, 0:nch, ri, :].rearrange("p k r -> p r k"), ["Hall"], [hbk])
                                ts("pool", Hbn[:, :, :, :nch], Hb[:, :, :, :nch], -1.0, ALU.mult, [hbk], [hbnk])
                                if ti == NT // N - 1:
                                    ho_r, ho_i = (hrp, hip) if sq["kind"] == "p" else (hrs, his)
                                    S.dma(ho_r[l, b].rearrange("(pr g2) p -> (g2 p) pr", g2=2), Hall[:, nch, 0, :], reads=["Hall"], chan="hst", allow_slow_non_contiguous=True)
                                    S.dma(ho_i[l, b].rearrange("(pr g2) p -> (g2 p) pr", g2=2), Hall[:, nch, 1, :], reads=["Hall"], chan="hst", allow_slow_non_contiguous=True)
                                else:
                                    cp("pool", Hall[:, 0], Hall[:, nch], ["Hall"], ["Hall"])

                            def back(ti, bi_):
                                t0 = ti * N
                                uT = uTs[bi_]; gsT = gsTs[bi_]; Hb = Hbs[bi_]; Hbn = Hbns[bi_]
                                uk = 'uT%d' % bi_; gk = 'gsT%d' % bi_; hbk = 'Hb%d' % bi_; hbnk = 'Hbn%d' % bi_
                                uv = uT[:, :, :N].rearrange("p f (k t) -> p f k t", t=8)
                                chk(7)
                                for ft in range(4):
                                    yi = 0
                                    yk = "py%d" % yi
                                    for tau in range(8):
                                        for dl in range(tau + 1):
                                            mm(py[yi][:, tau, :nch], BD[:, ft, dl, :], uv[:, ft, :, tau - dl], dl == 0, False, ["BD", uk], [yk])
                                        for j in range(4):
                                            pr = ft * 4 + j
                                            for ri in range(2):
                                                if j < 3:
                                                    mm(py[yi][32 * j:32 * j + 32, tau, :nch], CA[:, tau, ri, pr, :], Hb[:, ri, pr, :nch], False, False, ["CA", hbk], [yk])
                                                else:
                                                    mm(py[yi][64:128, tau, :nch], CA[:, tau, ri, pr - 1:pr + 1, :].rearrange("p a b -> p (a b)"), Hb[:, ri, pr, :nch], False, False, ["CA", hbk], [yk])
                                                    mm(py[yi][64:96, tau, :nch], CA[:, tau, ri, pr - 1, :], Hbn[:, ri, pr, :nch], False, (ri == 1), ["CA", hbnk], [yk])
                                    yv = py[yi][:, :, :nch].rearrange("p t k -> p k t")
                                    ysv = ysb[:, :N].rearrange("p (k t) -> p k t", t=8)
                                    stt(ysv, uv[:, ft], dvec[:, ft:ft + 1], yv, ALU.mult, ALU.add, [uk, "dvec%d" % l, yk], ["ysb"])
                                    tt("dve", tg[:, :N], ysb[:, :N], ysb[:, :N], ALU.mult, ["ysb"], ["tg"])
                                    ts("dve", tg[:, :N], tg[:, :N], 0.044715, ALU.mult, ["tg"], ["tg"], s2=1.0, op1=ALU.add)
                                    tt("dve", tg[:, :N], tg[:, :N], ysb[:, :N], ALU.mult, ["tg", "ysb"], ["tg"])
                                    act(tg[:, :N], tg[:, :N], AF.Exp, ["tg"], ["tg"], scale=-1.5957691216057308)
                                    ts("dve", tg[:, :N], tg[:, :N], 1.0, ALU.add, ["tg"], ["tg"])
                                    recip(tg[:, :N], tg[:, :N], ["tg"], ["tg"])
                                    tt("dve", ygb[:, ft, :N], ysb[:, :N], tg[:, :N], ALU.mult, ["tg", "ysb"], ["ygb"])
                                chk(8)
                                for mt in range(4):
                                    pj = cnt["pj"] % 2; cnt["pj"] += 1
                                    pjk = "ppj%d" % pj
                                    for kc in range(4):
                                        mm(ppj[pj][:, :N], wglub[:, kc, mt * 128:(mt + 1) * 128], ygb[:, kc, :N], kc == 0, kc == 3, ["wglub", "ygb"], [pjk])
                                    act(tg[:, :N], ppj[pj][:, :N], AF.Exp, [pjk, "nbg%d" % l], ["tg"], scale=-1.0, bias=nbg[:, mt:mt + 1])
                                    ts("dve", tg[:, :N], tg[:, :N], 1.0, ALU.add, ["tg"], ["tg"])
                                    recip(tg[:, :N], tg[:, :N], ["tg"], ["tg"])
                                    tt("dve", sy[:, mt, :N], ygb[:, mt, :N], tg[:, :N], ALU.mult, ["tg", "ygb"], ["sy"])
                                    tt("pool", sqb[:, mt, :N], sy[:, mt, :N], sy[:, mt, :N], ALU.mult, ["sy"], ["sqb"])
                                pj = cnt["pj"] % 2; cnt["pj"] += 1
                                pjk = "ppj%d" % pj
                                for kc in range(4):
                                    mm(ppj[pj][:, :N], onesb[:], sqb[:, kc, :N], kc == 0, kc == 3, ["onesb", "sqb"], [pjk])
                                act(rstd[:, :N], ppj[pj][:, :N], AF.Ln, [pjk], ["rstd"], scale=1.0 / 512, bias=EPS)
                                act(rstd[:, :N], rstd[:, :N], AF.Exp, ["rstd"], ["rstd"], scale=-0.5)
                                for ft in range(4):
                                    tt("dve", sy[:, ft, :N], sy[:, ft, :N], rstd[:, :N], ALU.mult, ["sy", "rstd"], ["sy"])
                                    stt(sob[:, ft, :N], sy[:, ft, :N], gssm[:, ft:ft + 1], gsT[:, ft, :N], ALU.mult, ALU.mult, ["sy", "gssm%d" % l, gk], ["sqb"])
                                S.dma(sq["soT"].rearrange("(f p) t -> p f t", p=128)[:, :, t0:t0 + N], sob[:, :, :N], reads=["sqb"], chan="st_sob")

                            ntile = NT // N
                            front(0, 0)
                            for ti in range(ntile):
                                if ti + 1 < ntile:
                                    front(ti + 1, (ti + 1) % 2)
                                back(ti, ti % 2)
                    S.barrier()

                chk(10)
                with contextlib.ExitStack() as L2:
                    woutb = sb(L2, "woutb%d" % l, [128, 8, D], BF16)
                    gatt = sb(L2, "gatt%d" % l, [128, 4], F32)
                    NKmax = max(T, PL + TS)
                    nblk_max = (NKmax + 127) // 128
                    kTs = sb(L2, "kTs%d" % l, [128, 4, nblk_max * 128], BF16)
                    vsb = sb(L2, "vsb%d" % l, [128, nblk_max, 512], BF16)
                    qTt = sb(L2, "qTt%d" % l, [128, 4, 512], BF16)
                    gaTt = sb(L2, "gaTt%d" % l, [128, 4, 512], BF16)
                    catT = sb(L2, "catT%d" % l, [128, 8, 512], BF16)
                    ckst = [sb(L2, "ckst%d_%d" % (l, i), [128, 512], BF16) for i in range(2)]
                    e_sb = [sb(L2, "e_sb%d_%d" % (l, i), [128, 512], F32) for i in range(4)]
                    sp_sb = [sb(L2, "sp_sb%d_%d" % (l, i), [128, 512], BF16) for i in range(4)]
                    w_sb = [sb(L2, "w_sb%d_%d" % (l, i), [128, 512], BF16) for i in range(3)]
                    spsum = [sb(L2, "spsum%d_%d" % (l, i), [128, 512], BF16) for i in range(4)]
                    att = sb(L2, "att%d" % l, [128, 4, 512], F32)
                    asq = sb(L2, "asq%d" % l, [128, 4, 512], BF16)
                    rstd2 = sb(L2, "rstd2%d" % l, [128, 512], F32)
                    xt2 = [sb(L2, "xt2%d_%d" % (l, i), [128, D], F32) for i in range(2)]
                    xn = [sb(L2, "xn%d_%d" % (l, i), [128, D], F32) for i in range(2)]
                    junk2 = sb(L2, "junk2%d" % l, [128, D], BF16)
                    ssq2 = sb(L2, "ssq2%d" % l, [128, 4], F32)
                    pA = [L2.enter_context(nc.psum_tensor("pA%d_%d" % (l, i), [128, 512], F32)) for i in range(4)]
                    pAtt = [L2.enter_context(nc.psum_tensor("pAtt%d_%d" % (l, i), [128, 512], F32)) for i in range(2)]
                    pO = [L2.enter_context(nc.psum_tensor("pO%d_%d" % (l, i), [128, 512], F32)) for i in range(2)]
                    ptk = pO[0][:].bitcast(BF16)
                    for kc in range(8):
                        S.dma(woutb[:, kc, :], w_out[l, kc * 128:(kc + 1) * 128, :], writes=["woutb"], queue="pool")
                    S.dma(gatt[:], g_att[l].rearrange("(c p) -> p c", p=128), writes=["gatt"], allow_slow_non_contiguous=True)
                    c2 = dict(x=0, o=0, ck=0, blk=0)
                    for si, sq in enumerate(seqs):
                        NT, N, b, past = sq["NT"], sq["N"], sq["b"], sq["past"]
                        R = min(128, N); nsub = N // R
                        xsrc = sq["xin"] if l == 0 else (sq["xa"] if l % 2 == 1 else sq["xb"])
                        xdst = sq["xa"] if l % 2 == 0 else sq["xb"]
                        npast = past // 128
                        for pb in range(npast):
                            ci = c2["ck"] % 2; c2["ck"] += 1
                            ckk = "ckst%d" % ci
                            S.dma(ckst[ci][:], ck[l, b, pb * 128:(pb + 1) * 128, :], writes=[ckk], queue="pool")
                            for hp in range(4):
                                tp(ptk[:, hp * 128:(hp + 1) * 128], ckst[ci][:, hp * 128:(hp + 1) * 128], identb[:], [ckk, "identb"], ["pO0"])
                            cp("dve", kTs[:, :, pb * 128:(pb + 1) * 128], ptk[:, 0:512].rearrange("p (h s) -> p h s", h=4), ["pO0"], ["kTs"])
                        if npast:
                            S.dma(vsb[:, 0:npast, :], cv[l, b].rearrange("(n p) f -> p n f", p=128), writes=["vsb"], queue="pool")
                        S.dma(kTs[:, :, past:past + NT], sq["kT"].rearrange("(h p) t -> p h t", p=128), writes=["kTs"])
                        if NT >= 128:
                            S.dma(vsb[:, npast:npast + NT // 128, :], sq["vv"].rearrange("(n p) f -> p n f", p=128), writes=["vsb"])
                        else:
                            S.dma(vsb[:NT, npast, :], sq["vv"], writes=["vsb"])
                        for ti in range(NT // N):
                            t0 = ti * N
                            S.dma(qTt[:, :, :N], sq["qT"].rearrange("(h p) t -> p h t", p=128)[:, :, t0:t0 + N], writes=["qTt"])
                            S.dma(gaTt[:, :, :N], sq["gaT"].rearrange("(h p) t -> p h t", p=128)[:, :, t0:t0 + N], writes=["gaTt"])
                            S.dma(catT[:, 4:8, :N], sq["soT"].rearrange("(h p) t -> p h t", p=128)[:, :, t0:t0 + N], writes=["catT"])
                            chk(11)
                            blocks = []
                            if sq["kind"] == "p":
                                for j in reversed(range(nsub)):
                                    blocks.append((t0 + j * 128, 128, (t0 + j * 128) // 128, masks[:, j, :N]))
                                for kb in reversed(range(t0 // 128)):
                                    blocks.append((kb * 128, 128, kb, None))
                            else:
                                blocks.append((past, NT, npast, masks_s[:, :]))
                                for kb in reversed(range(npast)):
                                    blocks.append((kb * 128, 128, kb, None))
                            work = [(h, bi) for h in range(8) for bi in range(len(blocks))]

                            def stage1a(idx):
                                h, bi = work[idx]
                                s0, Rk, vb, mk = blocks[bi]
                                hp, base = h // 2, 64 * (h % 2)
                                sl = idx % 4
                                Ak, ek = "pA%d" % sl, "e_sb%d" % sl
                                mm(pA[sl][:Rk, :N], kTs[base:base + 64, hp, s0:s0 + Rk], qTt[base:base + 64, hp, :N], True, False, ["kTs", "qTt"], [Ak])
                                act(e_sb[sl][:Rk, :N], pA[sl][:Rk, :N], AF.Exp, [Ak], [ek])

                            def stage1b(idx):
                                h, bi = work[idx]
                                s0, Rk, vb, mk = blocks[bi]
                                sl = idx % 4
                                ek, spk = "e_sb%d" % sl, "sp_sb%d" % sl
                                act(sp_sb[sl][:Rk, :N], e_sb[sl][:Rk, :N], AF.Ln, [ek], [spk], bias=1.0)
                                if mk is not None:
                                    tt("pool", sp_sb[sl][:Rk, :N], sp_sb[sl][:Rk, :N], mk, ALU.mult, [spk, "masks"], [spk])
                                lastb = (bi == len(blocks) - 1)
                                if not lastb:
                                    cur, prv = "spsum%d" % (idx % 4), "spsum%d" % ((idx - 1) % 4)
                                    if bi == 0:
                                        if Rk < 128:
                                            mset("dve", spsum[idx % 4][:, :N], 0.0, [cur])
                                        cp("dve", spsum[idx % 4][:Rk, :N], sp_sb[sl][:Rk, :N], [spk], [cur])
                                    else:
                                        tt("dve", spsum[idx % 4][:, :N], spsum[(idx - 1) % 4][:, :N], sp_sb[sl][:, :N], ALU.add, [spk, prv], [cur])

                            def stage2(idx):
                                h, bi = work[idx]
                                s0, Rk, vb, mk = blocks[bi]
                                hp, base = h // 2, 64 * (h % 2)
                                sl = idx % 4
                                Ak, spk, wk = "pA%d" % sl, "sp_sb%d" % sl, "w_sb%d" % (idx % 3)
                                wt = w_sb[idx % 3]
                                ai = hp % 2; atk = "pAtt%d" % ai
                                first, lastb = (bi == 0), (bi == len(blocks) - 1)
                                mm(pA[sl][:Rk, :N], negU[:Rk, :Rk], sp_sb[sl][:Rk, :N], False, first, [spk, "negU"], [Ak])
                                if not first:
                                    prv = "spsum%d" % ((idx - 1) % 4)
                                    mm(pA[sl][:Rk, :N], negones[:, :Rk], spsum[(idx - 1) % 4][:, :N], False, True, [prv, "negones"], [Ak])
                                act(wt[:Rk, :N], pA[sl][:Rk, :N], AF.Exp, [Ak], [wk])
                                if mk is not None:
                                    tt("pool", wt[:Rk, :N], wt[:Rk, :N], mk, ALU.mult, [wk, "masks"], [wk])
                                mm(pAtt[ai][base:base + 64, :N], vsb[:Rk, vb, h * 64:(h + 1) * 64], wt[:Rk, :N], first, lastb, [wk, "vsb"], [atk])
                                if lastb:
                                    cp("dve", att[base:base + 64, hp, :N], pAtt[ai][base:base + 64, :N], [atk], ["att"])

                            nw = len(work)
                            for it in range(nw + 3):
                                if it < nw:
                                    stage1a(it)
                                if 1 <= it <= nw:
                                    stage1b(it - 1)
                                if it >= 3:
                                    stage2(it - 3)
                            chk(12)
                            for hp in range(4):
                                tt("pool", asq[:, hp, :N], att[:, hp, :N], att[:, hp, :N], ALU.mult, ["att"], ["asq"])
                            for hp in range(4):
                                mm(pO[1][:, :N], onesb[:], asq[:, hp, :N], hp == 0, hp == 3, ["onesb", "asq"], ["pO1"])
                            act(rstd2[:, :N], pO[1][:, :N], AF.Ln, ["pO1"], ["rstd2"], scale=1.0 / 512, bias=EPS)
                            act(rstd2[:, :N], rstd2[:, :N], AF.Exp, ["rstd2"], ["rstd2"], scale=-0.5)
                            for hp in range(4):
                                tt("dve", att[:, hp, :N], att[:, hp, :N], rstd2[:, :N], ALU.mult, ["att", "rstd2"], ["att"])
                                stt(catT[:, hp, :N], att[:, hp, :N], gatt[:, hp:hp + 1], gaTt[:, hp, :N], ALU.mult, ALU.mult, ["att", "gatt", "gaTt"], ["catT"])
                            chk(13)
                            for sub in range(nsub):
                                xi = c2["x"] % 2; c2["x"] += 1
                                xk, nk = "xt2%d" % xi, "xn%d" % xi
                                r0 = t0 + sub * R
                                S.dma(xt2[xi][:R, :], xsrc[r0:r0 + R, :], writes=[xk])
                                for half in range(2):
                                    oi = c2["o"] % 2; c2["o"] += 1
                                    ok = "pO%d" % oi
                                    for kc in range(8):
                                        mm(pO[oi][:R, :], catT[:, kc, sub * R:sub * R + R], woutb[:, kc, half * 512:(half + 1) * 512], kc == 0, kc == 7, ["catT", "woutb"], [ok])
                                    tt("dve", xn[xi][:R, half * 512:(half + 1) * 512], pO[oi][:R, :], xt2[xi][:R, half * 512:(half + 1) * 512], ALU.add, [ok, xk], [nk])
                                if not last:
                                    S.dma(xdst[r0:r0 + R, :], xn[xi][:R, :], reads=[nk], chan="st_" + nk)
                                else:
                                    act(junk2[:R, :], xn[xi][:R, :], AF.Square, [nk], ["junk2"])
                                    S.op("dve", lambda e: e.tensor_reduce(out=ssq2[:R, 0:1], in_=junk2[:R, :], axis=mybir.AxisListType.X, op=ALU.add), ["junk2"], ["ssq2"])
                                    act(ssq2[:R, 1:2], ssq2[:R, 0:1], AF.Ln, ["ssq2"], ["ssq2"], scale=1.0 / D, bias=EPS)
                                    act(ssq2[:R, 2:3], ssq2[:R, 1:2], AF.Exp, ["ssq2"], ["ssq2"], scale=-0.5)
                                    stt(xt2[xi][:R, :], xn[xi][:R, :], ssq2[:R, 2:3], fgbc[:R, :], ALU.mult, ALU.mult, [nk, "ssq2", "fgbc"], [xk])
                                    S.dma(sq["yout"][r0:r0 + R, :], xt2[xi][:R, :], reads=[xk], chan="st_" + xk)
                    S.barrier()
        except StopBuild:
            pass
        S._final = True
        S.barrier()
    return nc


_CACHE = {}


def _run(inputs, T, TS, PL, DEPTH):
    key = (T, TS, PL, DEPTH)
    if key not in _CACHE:
        _CACHE[key] = build(T, TS, PL, DEPTH)
    nc = _CACHE[key]
    f = lambda a: np.ascontiguousarray(np.asarray(a, dtype=np.float32))
    in_maps = []
    for c in range(8):
        sl = slice(2 * c, 2 * c + 2)
        m = {
            "xp": f(inputs["x_prompt"][sl]), "xs": f(inputs["x_sample"][sl]),
            "ck": f(np.asarray(inputs["cache_k"])[:, sl].reshape(DEPTH, 2, PL, 512)),
            "cv": f(np.asarray(inputs["cache_v"])[:, sl].reshape(DEPTH, 2, PL, 512)),
            "sre": f(np.asarray(inputs["state_ssm_re"])[:, sl]), "sim": f(np.asarray(inputs["state_ssm_im"])[:, sl]),
            "ln_g": f(inputs["ln_g"]), "w_in": f(inputs["w_in"]),
            "a_re": f(inputs["ssm_a_re"]), "a_im": f(inputs["ssm_a_im"]), "log_dt": f(inputs["ssm_log_dt"]),
            "b_re": f(inputs["ssm_b_re"]), "b_im": f(inputs["ssm_b_im"]),
            "c_re": f(inputs["ssm_c_re"]), "c_im": f(inputs["ssm_c_im"]),
            "ssm_d": f(inputs["ssm_d"]), "w_glu": f(inputs["w_glu"]), "b_glu": f(inputs["b_glu"]),
            "g_att": f(inputs["g_att"]), "g_ssm": f(inputs["g_ssm"]), "w_out": f(inputs["w_out"]),
            "final_g": f(inputs["final_g"]),
        }
        in_maps.append(m)
    res = run_bass_kernel_spmd(nc, in_maps, core_ids=list(range(8)))
    R = res.results
    cat0 = lambda k: np.concatenate([r[k] for r in R], axis=0)
    cat1 = lambda k: np.concatenate([r[k] for r in R], axis=1)
    y_p = cat0("yp"); y_s = cat0("ys")
    k_p = cat1("kp").reshape(DEPTH, 16, T, 8, 64); v_p = cat1("vp").reshape(DEPTH, 16, T, 8, 64)
    k_s = cat1("ks").reshape(DEPTH, 16, TS, 8, 64); v_s = cat1("vs").reshape(DEPTH, 16, TS, 8, 64)
    return (y_p, y_s, k_p, v_p, cat1("hrp"), cat1("hip"), k_s, v_s, cat1("hrs"), cat1("his"))


def kernel(**inputs):
    T = int(np.shape(inputs["x_prompt"])[1]); TS = int(np.shape(inputs["x_sample"])[1])
    PL = int(np.shape(inputs["cache_k"])[2]); DEPTH = int(np.shape(inputs["w_in"])[0])
    return _run(inputs, T, TS, PL, DEPTH)
```

```python
import contextlib
import math
import numpy as np
import concourse.bass as bass
import concourse.mybir as mybir
from concourse.bass_utils import run_bass_kernel_spmd

F32 = mybir.dt.float32
BF16 = mybir.dt.bfloat16
I32 = mybir.dt.int32
AF = mybir.ActivationFunctionType
ALU = mybir.AluOpType
EPS = 1e-6
TWO_PI = 2.0 * math.pi


class Sched:
    def __init__(self, nc, stack):
        self.nc = nc
        self.stack = stack
        self.eng = {"pe": nc.tensor, "act": nc.scalar, "dve": nc.vector,
                    "pool": nc.gpsimd, "sp": nc.sync}
        self.sem = {}
        self.cnt = {}
        for e in ("pe", "act", "dve", "pool"):
            self.sem[e] = stack.enter_context(nc.semaphore("s_" + e))
            self.cnt[e] = 0
        self.waited = {e: {} for e in self.eng}
        self.lastw = {}
        self.readers = {}
        self.nchan = 0
        self.stopped = False

    def chan(self, name):
        if name not in self.sem:
            self.sem[name] = self.stack.enter_context(self.nc.semaphore("d%d" % self.nchan))
            self.nchan += 1
            self.cnt[name] = 0
        return name

    def _deps(self, engine, reads, writes):
        deps = {}

        def add(d, raw):
            if d is None:
                return
            src, c = d
            if src == engine and not raw:
                return
            if src == "pe" and engine == "pe":
                return
            if deps.get(src, 0) < c:
                deps[src] = c

        for r in reads:
            add(self.lastw.get(r), True)
        for w in writes:
            add(self.lastw.get(w), True)
            for src, c in self.readers.get(w, {}).items():
                add((src, c), False)
        return deps

    def _emit_waits(self, engine, deps):
        e = self.eng[engine]
        wd = self.waited[engine]
        for src, c in deps.items():
            if wd.get(src, 0) >= c:
                continue
            e.wait_ge(self.sem[src], c)
            wd[src] = c

    def op(self, engine, fn, reads=(), writes=()):
        if self.stopped:
            return None
        deps = self._deps(engine, reads, writes)
        self._emit_waits(engine, deps)
        ins = fn(self.eng[engine])
        ins.then_inc(self.sem[engine], 1)
        self.cnt[engine] += 1
        c = self.cnt[engine]
        for r in reads:
            self.readers.setdefault(r, {})[engine] = c
        for w in writes:
            self.lastw[w] = (engine, c)
            self.readers[w] = {}
        return ins

    def dma(self, out, in_, reads=(), writes=(), chan=None, queue="sp", **kw):
        if self.stopped:
            return None
        if chan is None:
            chan = "c_" + str(writes[0] if writes else reads[0])
        self.chan(chan)
        deps = self._deps("__dma__", reads, writes)
        self._emit_waits(queue, deps)
        ins = self.eng[queue].dma_start(out=out, in_=in_, **kw)
        ins.then_inc(self.sem[chan], 16)
        self.cnt[chan] += 16
        c = self.cnt[chan]
        for r in reads:
            self.readers.setdefault(r, {})[chan] = c
        for w in writes:
            self.lastw[w] = (chan, c)
            self.readers[w] = {}
        return ins

    def barrier(self):
        if self.stopped and getattr(self, "_final", False) is False:
            return
        allsrc = [s for s in self.cnt if self.cnt[s] > 0]
        for e in ("pe", "act", "dve", "pool", "sp"):
            self._emit_waits(e, {s: self.cnt[s] for s in allsrc if s != e})


class StopBuild(Exception):
    pass


def build(T, TS, PL, DEPTH):
    import os
    KSTOP = int(os.environ.get("KSTOP", "99"))

    def chk(level):
        if KSTOP <= level:
            S.stopped = True

    nc = bass.Bass("TRN2", target_bir_lowering=False)
    D = 1024

    def din(name, shape):
        return nc.dram_tensor(name, shape, F32, kind="ExternalInput").ap()

    def dout(name, shape):
        return nc.dram_tensor(name, shape, F32, kind="ExternalOutput").ap()

    def dscr(name, shape, dt):
        return nc.dram_tensor(name, shape, dt, kind="Internal").ap()

    xp = din("xp", [2, T, D]); xs = din("xs", [2, TS, D])
    ck = din("ck", [DEPTH, 2, PL, 512]); cv = din("cv", [DEPTH, 2, PL, 512])
    sre = din("sre", [DEPTH, 2, 32, 64]); sim = din("sim", [DEPTH, 2, 32, 64])
    ln_g = din("ln_g", [DEPTH, D]); w_in = din("w_in", [DEPTH, D, 3072])
    a_re = din("a_re", [DEPTH, 32, 64]); a_im = din("a_im", [DEPTH, 32, 64])
    log_dt = din("log_dt", [DEPTH, 32])
    b_re = din("b_re", [DEPTH, 32, 64, 16]); b_im = din("b_im", [DEPTH, 32, 64, 16])
    c_re = din("c_re", [DEPTH, 32, 16, 64]); c_im = din("c_im", [DEPTH, 32, 16, 64])
    ssm_d = din("ssm_d", [DEPTH, 512]); w_glu = din("w_glu", [DEPTH, 512, 512])
    b_glu = din("b_glu", [DEPTH, 512]); g_att = din("g_att", [DEPTH, 512])
    g_ssm = din("g_ssm", [DEPTH, 512]); w_out = din("w_out", [DEPTH, D, D])
    final_g = din("final_g", [D])
    yp = dout("yp", [2, T, D]); ys = dout("ys", [2, TS, D])
    kp = dout("kp", [DEPTH, 2, T, 512]); vp = dout("vp", [DEPTH, 2, T, 512])
    hrp = dout("hrp", [DEPTH, 2, 32, 64]); hip = dout("hip", [DEPTH, 2, 32, 64])
    ks = dout("ks", [DEPTH, 2, TS, 512]); vs = dout("vs", [DEPTH, 2, TS, 512])
    hrs = dout("hrs", [DEPTH, 2, 32, 64]); his = dout("his", [DEPTH, 2, 32, 64])

    seqs = []
    for b in range(2):
        seqs.append(dict(kind="p", b=b, NT=T, xin=xp[b], yout=yp[b], past=0))
    for b in range(2):
        seqs.append(dict(kind="s", b=b, NT=TS, xin=xs[b], yout=ys[b], past=PL))
    for si, sq in enumerate(seqs):
        NT = sq["NT"]
        sq["xa"] = dscr("xa%d" % si, [NT, D], F32)
        sq["xb"] = dscr("xb%d" % si, [NT, D], F32)
        for nm in ("qT", "kT", "gaT", "soT"):
            sq[nm] = dscr("%s%d" % (nm, si), [512, NT], BF16)
        sq["vv"] = dscr("vv%d" % si, [NT, 512], BF16)
        sq["N"] = min(512, NT)

    with contextlib.ExitStack() as top:
        S = Sched(nc, top)

        def sb(stack, name, shape, dt):
            return stack.enter_context(nc.sbuf_tensor(name, shape, dt))

        def mm(out, lhsT, rhs, start, stop, r, w):
            S.op("pe", lambda e: e.matmul(out, lhsT, rhs, start=start, stop=stop), r, w)

        def tp(out, in_, ident, r, w):
            S.op("pe", lambda e: e.transpose(out, in_, ident), r, w)

        def act(out, in_, func, r, w, scale=1.0, bias=0.0):
            S.op("act", lambda e: e.activation(out=out, in_=in_, func=func, bias=bias, scale=scale), r, w)

        def tt(eng, out, in0, in1, op, r, w):
            S.op(eng, lambda e: e.tensor_tensor(out=out, in0=in0, in1=in1, op=op), r, w)

        def ts(eng, out, in0, s1, op0, r, w, s2=None, op1=None):
            if op1 is None:
                S.op(eng, lambda e: e.tensor_scalar(out=out, in0=in0, scalar1=s1, scalar2=None, op0=op0), r, w)
            else:
                S.op(eng, lambda e: e.tensor_scalar(out=out, in0=in0, scalar1=s1, scalar2=s2, op0=op0, op1=op1), r, w)

        def stt(out, in0, scalar, in1, op0, op1, r, w):
            S.op("dve", lambda e: e.scalar_tensor_tensor(out=out, in0=in0, scalar=scalar, in1=in1, op0=op0, op1=op1), r, w)

        def cp(eng, out, in_, r, w):
            S.op(eng, lambda e: e.tensor_copy(out=out, in_=in_), r, w)

        def recip(out, in_, r, w):
            S.op("dve", lambda e: e.reciprocal(out=out, in_=in_), r, w)

        def mset(eng, ap, val, w):
            S.op(eng, lambda e: e.memset(ap, val), (), w)

        identb = sb(top, "identb", [128, 128], BF16)
        identf = sb(top, "identf", [128, 128], F32)
        negU = sb(top, "negU", [128, 128], BF16)
        negones = sb(top, "negones", [128, 128], BF16)
        onesb = sb(top, "onesb", [128, 128], BF16)
        mask32 = sb(top, "mask32", [128, 128], F32)
        fgbc = sb(top, "fgbc", [128, D], F32)
        NP = seqs[0]["N"]
        ndiag = NP // 128
        masks = sb(top, "masks", [128, ndiag, NP], BF16)
        masks_s = sb(top, "masks_s", [TS, TS], BF16)
        mset("pool", identb[:], 0.0, ["identb"])
        S.op("pool", lambda e: e.affine_select(out=identb[:], in_=identb[:], pattern=[[-1, 128]], compare_op=ALU.not_equal, fill=1.0, base=0, channel_multiplier=1), ["identb"], ["identb"])
        mset("pool", identf[:], 0.0, ["identf"])
        S.op("pool", lambda e: e.affine_select(out=identf[:], in_=identf[:], pattern=[[-1, 128]], compare_op=ALU.not_equal, fill=1.0, base=0, channel_multiplier=1), ["identf"], ["identf"])
        mset("pool", negU[:], -1.0, ["negU"])
        S.op("pool", lambda e: e.affine_select(out=negU[:], in_=negU[:], pattern=[[-1, 128]], compare_op=ALU.is_ge, fill=0.0, base=0, channel_multiplier=1), ["negU"], ["negU"])
        mset("pool", negones[:], -1.0, ["negones"])
        mset("pool", onesb[:], 1.0, ["onesb"])
        mset("pool", mask32[:], 0.0, ["mask32"])
        for j in range(4):
            mset("pool", mask32[32 * j:32 * j + 32, 32 * j:32 * j + 32], 1.0, ["mask32"])
        mset("pool", masks[:], 1.0, ["masks"])
        for j in range(ndiag):
            S.op("pool", lambda e: e.affine_select(out=masks[:, j, :], in_=masks[:, j, :], pattern=[[1, NP]], compare_op=ALU.is_gt, fill=0.0, base=-128 * j, channel_multiplier=-1), ["masks"], ["masks"])
        mset("pool", masks_s[:], 1.0, ["masks_s"])
        S.op("pool", lambda e: e.affine_select(out=masks_s[:], in_=masks_s[:], pattern=[[1, TS]], compare_op=ALU.is_gt, fill=0.0, base=0, channel_multiplier=-1), ["masks_s"], ["masks_s"])
        S.dma(fgbc[:], final_g.partition_broadcast(128), writes=["fgbc"])

        try:
            chk(0)
            for l in range(DEPTH):
                last = (l == DEPTH - 1)
                with contextlib.ExitStack() as L1:
                    winb = sb(L1, "winb%d" % l, [128, 8, 3072], BF16)
                    wglub = sb(L1, "wglub%d" % l, [128, 4, 512], BF16)
                    gbc = sb(L1, "gbc%d" % l, [128, D], F32)
                    dvec = sb(L1, "dvec%d" % l, [128, 4], F32)
                    nbg = sb(L1, "nbg%d" % l, [128, 4], F32)
                    gssm = sb(L1, "gssm%d" % l, [128, 4], F32)
                    PWr = sb(L1, "PWr%d" % l, [128, 16, 16], F32)
                    PWi = sb(L1, "PWi%d" % l, [128, 16, 16], F32)
                    A1 = sb(L1, "A1%d" % l, [128, 2, 16], F32)
                    Aip = sb(L1, "Aip%d" % l, [128, 16], F32)
                    Ain = sb(L1, "Ain%d" % l, [128, 16], F32)
                    AJ1 = sb(L1, "AJ1%d" % l, [128, 8, 2, 16], F32)
                    AJn = sb(L1, "AJn%d" % l, [128, 8, 16], F32)
                    BD = sb(L1, "BD%d" % l, [128, 4, 8, 128], BF16)
                    BL = sb(L1, "BL%d" % l, [128, 4, 8, 2, 128], BF16)
                    CA = sb(L1, "CA%d" % l, [128, 8, 2, 16, 32], BF16)
                    for kc in range(8):
                        S.dma(winb[:, kc, :], w_in[l, kc * 128:(kc + 1) * 128, :], writes=["winb"], queue="pool")
                    S.dma(wglub[:], w_glu[l].rearrange("(kc p) n -> p kc n", p=128), writes=["wglub"], queue="pool")
                    S.dma(gbc[:], ln_g[l].partition_broadcast(128), writes=["gbc"])
                    for tl, src, nm in ((dvec, ssm_d, "dvec%d" % l), (nbg, b_glu, "nbg%d" % l), (gssm, g_ssm, "gssm%d" % l)):
                        S.dma(tl[:], src[l].rearrange("(c p) -> p c", p=128), writes=[nm], allow_slow_non_contiguous=True)
                    ts("dve", nbg[:], nbg[:], -1.0, ALU.mult, ["nbg%d" % l], ["nbg%d" % l])

                    with contextlib.ExitStack() as SU:
                        k_ = "su"
                        aqr = sb(SU, "aqr%d" % l, [128, 16], F32); aqi = sb(SU, "aqi%d" % l, [128, 16], F32)
                        dtq = sb(SU, "dtq%d" % l, [128, 16], F32)
                        adt = sb(SU, "adt%d" % l, [128, 16], F32); ang = sb(SU, "ang%d" % l, [128, 16], F32)
                        mag = sb(SU, "mag%d" % l, [128, 16, 16], F32)
                        ph = sb(SU, "ph%d" % l, [128, 2, 16, 16], F32)
                        pht = sb(SU, "pht%d" % l, [128, 2, 16, 16], F32)
                        phi = sb(SU, "phi%d" % l, [128, 2, 16, 16], I32)
                        sc = sb(SU, "sc%d" % l, [128, 2, 16, 16], F32)
                        t1 = sb(SU, "t1%d" % l, [128, 16], F32); t2 = sb(SU, "t2%d" % l, [128, 16], F32)
                        t3 = sb(SU, "t3%d" % l, [128, 16], F32)
                        fr = sb(SU, "fr%d" % l, [128, 16], F32); fi = sb(SU, "fi%d" % l, [128, 16], F32)
                        Bq0r = sb(SU, "Bq0r%d" % l, [128, 16, 32], F32); Bq0i = sb(SU, "Bq0i%d" % l, [128, 16, 32], F32)
                        Bbr = sb(SU, "Bbr%d" % l, [128, 16, 32], F32); Bbi = sb(SU, "Bbi%d" % l, [128, 16, 32], F32)
                        nBbi = sb(SU, "nBbi%d" % l, [128, 16, 32], F32)
                        u1 = sb(SU, "u1%d" % l, [128, 16, 32], F32); u2 = sb(SU, "u2%d" % l, [128, 16, 32], F32)
                        Yr = sb(SU, "Yr%d" % l, [128, 16, 32], F32); Yi = sb(SU, "Yi%d" % l, [128, 16, 32], F32)
                        Zr = sb(SU, "Zr%d" % l, [32, 16, 128], F32); Zi = sb(SU, "Zi%d" % l, [32, 16, 128], F32)
                        CTr = sb(SU, "CTr%d" % l, [128, 16, 32], F32); CTi = sb(SU, "CTi%d" % l, [128, 16, 32], F32)
                        Xr = sb(SU, "Xr%d" % l, [128, 9, 16, 32], F32); Xi = sb(SU, "Xi%d" % l, [128, 9, 16, 32], F32)
                        pss = SU.enter_context(nc.psum_tensor("pss%d" % l, [128, 4, 128], F32))
                        psc = SU.enter_context(nc.psum_tensor("psc%d" % l, [128, 16, 32], F32))
                        K = [k_]
                        S.dma(aqr[:], a_re[l].rearrange("(pr g2) p -> (g2 p) pr", g2=2), writes=K, allow_slow_non_contiguous=True)
                        S.dma(aqi[:], a_im[l].rearrange("(pr g2) p -> (g2 p) pr", g2=2), writes=K, allow_slow_non_contiguous=True)
                        mset("dve", Bq0r[:], 0.0, K); mset("dve", Bq0i[:], 0.0, K)
                        mset("dve", Zr[:], 0.0, K); mset("dve", Zi[:], 0.0, K)
                        for g2 in range(2):
                            S.dma(dtq[g2 * 64:(g2 + 1) * 64, :], log_dt[l].rearrange("(pr g2) -> g2 pr", g2=2)[g2].partition_broadcast(64), reads=K, writes=K, allow_slow_non_contiguous=True)
                            S.dma(Bq0r[g2 * 64:(g2 + 1) * 64, :, g2 * 16:(g2 + 1) * 16], b_re[l].rearrange("(pr g2) p c -> g2 p pr c", g2=2)[g2], reads=K, writes=K)
                            S.dma(Bq0i[g2 * 64:(g2 + 1) * 64, :, g2 * 16:(g2 + 1) * 16], b_im[l].rearrange("(pr g2) p c -> g2 p pr c", g2=2)[g2], reads=K, writes=K)
                            S.dma(Zr[g2 * 16:(g2 + 1) * 16, :, g2 * 64:(g2 + 1) * 64], c_re[l].rearrange("(pr g2) c p -> g2 c pr p", g2=2)[g2], reads=K, writes=K)
                            S.dma(Zi[g2 * 16:(g2 + 1) * 16, :, g2 * 64:(g2 + 1) * 64], c_im[l].rearrange("(pr g2) c p -> g2 c pr p", g2=2)[g2], reads=K, writes=K)
                        act(dtq[:], dtq[:], AF.Exp, K, K)
                        tt("dve", adt[:], aqr[:], dtq[:], ALU.mult, K, K)
                        tt("dve", ang[:], aqi[:], dtq[:], ALU.mult, K, K)
                        for di, dl in enumerate(list(range(9)) + [16, 24, 32, 40, 48, 56, 64]):
                            act(mag[:, di, :], adt[:], AF.Exp, K, K, scale=float(dl))
                            ts("dve", ph[:, 0, di, :], ang[:], float(dl), ALU.mult, K, K)
                            ts("dve", ph[:, 1, di, :], ang[:], float(dl), ALU.mult, K, K, s2=math.pi / 2, op1=ALU.add)
                        ts("dve", pht[:], ph[:], 1.0 / TWO_PI, ALU.mult, K, K)
                        cp("dve", phi[:], pht[:], K, K)
                        cp("dve", pht[:], phi[:], K, K)
                        stt(ph[:], pht[:], -TWO_PI, ph[:], ALU.mult, ALU.add, K, K)
                        ts("dve", ph[:], ph[:], -math.pi, ALU.max, K, K, s2=math.pi, op1=ALU.min)
                        act(sc[:], ph[:], AF.Sin, K, K)
                        chk(1)
                        tt("dve", PWi[:], mag[:], sc[:, 0], ALU.mult, K, ["PW"])
                        tt("dve", PWr[:], mag[:], sc[:, 1], ALU.mult, K, ["PW"])
                        cp("dve", A1[:, 0, :], PWr[:, 8, :], ["PW"], ["A8"]); cp("dve", A1[:, 1, :], PWr[:, 8, :], ["PW"], ["A8"])
                        cp("dve", Aip[:], PWi[:, 8, :], ["PW"], ["A8"])
                        ts("dve", Ain[:], PWi[:, 8, :], -1.0, ALU.mult, ["PW"], ["A8"])
                        cp("dve", AJ1[:, :, 0, :], PWr[:, 8:16, :], ["PW"], ["A8"]); cp("dve", AJ1[:, :, 1, :], PWr[:, 8:16, :], ["PW"], ["A8"])
                        ts("dve", AJn[:], PWi[:, 8:16, :], -1.0, ALU.mult, ["PW"], ["A8"])
                        tt("dve", t1[:], aqr[:], aqr[:], ALU.mult, K, K)
                        tt("dve", t2[:], aqi[:], aqi[:], ALU.mult, K, K)
                        tt("dve", t1[:], t1[:], t2[:], ALU.add, K, K)
                        recip(t1[:], t1[:], K, K)
                        ts("dve", t2[:], PWr[:, 1, :], -1.0, ALU.add, K + ["PW"], K)
                        tt("dve", fr[:], t2[:], aqr[:], ALU.mult, K, K)
                        tt("dve", t3[:], PWi[:, 1, :], aqi[:], ALU.mult, K + ["PW"], K)
                        tt("dve", fr[:], fr[:], t3[:], ALU.add, K, K)
                        tt("dve", fr[:], fr[:], t1[:], ALU.mult, K, K)
                        tt("dve", fi[:], PWi[:, 1, :], aqr[:], ALU.mult, K + ["PW"], K)
                        tt("dve", t3[:], t2[:], aqi[:], ALU.mult, K, K)
                        tt("dve", fi[:], fi[:], t3[:], ALU.subtract, K, K)
                        tt("dve", fi[:], fi[:], t1[:], ALU.mult, K, K)
                        bc = lambda ap: ap.unsqueeze(2).to_broadcast([128, 16, 32])
                        tt("dve", u1[:], Bq0r[:], bc(fr[:]), ALU.mult, K, K)
                        tt("dve", u2[:], Bq0i[:], bc(fi[:]), ALU.mult, K, K)
                        tt("dve", Bbr[:], u1[:], u2[:], ALU.subtract, K, K)
                        tt("dve", u1[:], Bq0i[:], bc(fr[:]), ALU.mult, K, K)
                        tt("dve", u2[:], Bq0r[:], bc(fi[:]), ALU.mult, K, K)
                        tt("dve", Bbi[:], u1[:], u2[:], ALU.add, K, K)
                        ts("dve", nBbi[:], Bbi[:], -1.0, ALU.mult, K, K)
                        for (Z, CT) in ((Zr, CTr), (Zi, CTi)):
                            for pr in range(16):
                                tp(psc[:, pr, :], Z[:, pr, :], identf[0:32, 0:32], K + ["identf"], ["psc"])
                            cp("dve", CT[:], psc[:], ["psc"], K)
                        for dl in range(9):
                            pr_ = bc(PWr[:, dl, :]); pi_ = bc(PWi[:, dl, :])
                            tt("dve", u1[:], CTr[:], pr_, ALU.mult, K + ["PW"], K)
                            tt("dve", u2[:], CTi[:], pi_, ALU.mult, K + ["PW"], K)
                            tt("dve", Xr[:, dl], u1[:], u2[:], ALU.subtract, K, K)
                            tt("dve", u1[:], CTr[:], pi_, ALU.mult, K + ["PW"], K)
                            tt("dve", u2[:], CTi[:], pr_, ALU.mult, K + ["PW"], K)
                            tt("dve", Xi[:, dl], u1[:], u2[:], ALU.add, K, K)
                        for tau in range(8):
                            cp("dve", CA[:, tau, 0, :, :], Xr[:, tau + 1], K, ["CA"])
                            ts("dve", CA[:, tau, 1, :, :], Xi[:, tau + 1], -1.0, ALU.mult, K, ["CA"])
                        for tau in range(8):
                            pr_ = bc(PWr[:, 7 - tau, :]); pi_ = bc(PWi[:, 7 - tau, :])
                            tt("dve", u1[:], Bbr[:], pr_, ALU.mult, K + ["PW"], K)
                            tt("dve", u2[:], Bbi[:], pi_, ALU.mult, K + ["PW"], K)
                            tt("dve", Yr[:], u1[:], u2[:], ALU.subtract, K, K)
                            tt("dve", u1[:], Bbr[:], pi_, ALU.mult, K + ["PW"], K)
                            tt("dve", u2[:], Bbi[:], pr_, ALU.mult, K + ["PW"], K)
                            tt("dve", Yi[:], u1[:], u2[:], ALU.add, K, K)
                            for ri, Y in ((0, Yr), (1, Yi)):
                                for ft in range(4):
                                    tp(pss[:, ft, :], Y[:, ft * 4:(ft + 1) * 4, :].rearrange("p a b -> p (a b)"), identf[:], K + ["identf"], ["pss"])
                                cp("dve", BL[:, :, tau, ri, :], pss[:], ["pss"], ["BL"])
                        for dl in range(8):
                            for ft in range(4):
                                fl = lambda t_: t_[:, ft * 4:(ft + 1) * 4, :].rearrange("p a b -> p (a b)")
                                mm(pss[:, ft, :], fl(Bbr), Xr[:, dl, ft * 4:(ft + 1) * 4, :].rearrange("p a b -> p (a b)"), True, False, K, ["pss"])
                                mm(pss[:, ft, :], fl(nBbi), Xi[:, dl, ft * 4:(ft + 1) * 4, :].rearrange("p a b -> p (a b)"), False, True, K, ["pss"])
                            tt("dve", BD[:, :, dl, :], pss[:], mask32[:].unsqueeze(1).to_broadcast([128, 4, 128]), ALU.mult, ["pss", "mask32"], ["BD"])
                    chk(2)
                    S.barrier()

                    with contextlib.ExitStack() as W1:
                        xt = [sb(W1, "xt%d_%d" % (l, i), [128, D], F32) for i in range(2)]
                        ssq = sb(W1, "ssq%d" % l, [128, 4], F32)
                        hn = sb(W1, "hn%d" % l, [128, D], BF16)
                        junk = hn
                        hnT = sb(W1, "hnT%d" % l, [128, 8, 512], BF16)
                        stg = [sb(W1, "stg%d_%d" % (l, i), [128, 512], BF16) for i in range(3)]
                        ef = [sb(W1, "ef%d_%d" % (l, i), [128, 512], F32) for i in range(2)]
                        kvst = [sb(W1, "kvst%d_%d" % (l, i), [128, 1024], F32) for i in range(2)]
                        vst = [sb(W1, "vst%d_%d" % (l, i), [128, 512], BF16) for i in range(2)]
                        uTs = [sb(W1, "uT%d_%d" % (l, i), [128, 4, 512], BF16) for i in range(2)]
                        gsTs = [sb(W1, "gsT%d_%d" % (l, i), [128, 4, 512], BF16) for i in range(2)]
                        Hall = sb(W1, "Hall%d" % l, [128, 65, 2, 16], F32)
                        Hbs = [sb(W1, "Hb%d_%d" % (l, i), [128, 2, 16, 64], BF16) for i in range(2)]
                        Hbns = [sb(W1, "Hbn%d_%d" % (l, i), [128, 2, 16, 64], BF16) for i in range(2)]
                        Pg = sb(W1, "Pg%d" % l, [128, 8, 2, 16], F32)
                        Qg = sb(W1, "Qg%d" % l, [128, 8, 2, 16], F32)
                        ysb = sb(W1, "ysb%d" % l, [128, 512], F32)
                        tg = sb(W1, "tg%d" % l, [128, 512], F32)
                        ygb = sb(W1, "ygb%d" % l, [128, 4, 512], BF16)
                        sy = sb(W1, "sy%d" % l, [128, 4, 512], F32)
                        sqb = sb(W1, "sqb%d" % l, [128, 4, 512], BF16)
                        rstd = ysb
                        sob = sqb
                        ptp = [W1.enter_context(nc.psum_tensor("ptp%d_%d" % (l, i), [128, 8, 128], BF16)) for i in range(1)]
                        ppj = [W1.enter_context(nc.psum_tensor("ppj%d_%d" % (l, i), [128, 512], F32)) for i in range(2)]
                        pS = [W1.enter_context(nc.psum_tensor("pS%d_%d" % (l, i), [128, 2, 4, 64], F32)) for i in range(4)]
                        py = [W1.enter_context(nc.psum_tensor("py%d_%d" % (l, i), [128, 8, 64], F32)) for i in range(1)]
                        cnt = dict(x=0, pj=0, stg=0, ef=0, kv=0, py=0)

                        def silu_evac(ps, pskey, out, outkey, N):
                            i = cnt["ef"] % 2; cnt["ef"] += 1
                            e_ = ef[i]; ek = "ef%d" % i
                            act(e_[:, :N], ps, AF.Exp, [pskey], [ek], scale=-1.0)
                            ts("dve", e_[:, :N], e_[:, :N], 1.0, ALU.add, [ek], [ek])
                            recip(e_[:, :N], e_[:, :N], [ek], [ek])
                            tt("dve", out, e_[:, :N], ps, ALU.mult, [ek, pskey], [outkey])

                        for si, sq in enumerate(seqs):
                            NT, N, b = sq["NT"], sq["N"], sq["b"]
                            R = min(128, N); nsub = N // R; nch = N // 8
                            xsrc = sq["xin"] if l == 0 else (sq["xa"] if l % 2 == 1 else sq["xb"])
                            kout, vout = (kp, vp) if sq["kind"] == "p" else (ks, vs)
                            if sq["kind"] == "p":
                                mset("pool", Hall[:, 0], 0.0, ["Hall"])
                            else:
                                S.dma(Hall[:, 0, 0, :], sre[l, b].rearrange("(pr g2) p -> (g2 p) pr", g2=2), writes=["Hall"], chan="hld", allow_slow_non_contiguous=True)
                                S.dma(Hall[:, 0, 1, :], sim[l, b].rearrange("(pr g2) p -> (g2 p) pr", g2=2), writes=["Hall"], chan="hld", allow_slow_non_contiguous=True)
                            def front(ti, bi_):
                                t0 = ti * N
                                uT = uTs[bi_]; gsT = gsTs[bi_]; Hb = Hbs[bi_]; Hbn = Hbns[bi_]
                                uk = 'uT%d' % bi_; gk = 'gsT%d' % bi_; hbk = 'Hb%d' % bi_; hbnk = 'Hbn%d' % bi_
                                for sub in range(nsub):
                                    xi = cnt["x"] % 2; cnt["x"] += 1
                                    xk = "xt%d" % xi
                                    r0 = t0 + sub * R
                                    S.dma(xt[xi][:R, :], xsrc[r0:r0 + R, :], writes=[xk])
                                    act(junk[:R, :], xt[xi][:R, :], AF.Square, [xk], ["hn"])
                                    S.op("dve", lambda e: e.tensor_reduce(out=ssq[:R, 0:1], in_=junk[:R, :], axis=mybir.AxisListType.X, op=ALU.add), ["hn"], ["ssq"])
                                    act(ssq[:R, 1:2], ssq[:R, 0:1], AF.Ln, ["ssq"], ["ssq"], scale=1.0 / D, bias=EPS)
                                    act(ssq[:R, 2:3], ssq[:R, 1:2], AF.Exp, ["ssq"], ["ssq"], scale=-0.5)
                                    stt(hn[:R, :], xt[xi][:R, :], ssq[:R, 2:3], gbc[:R, :], ALU.mult, ALU.mult, [xk, "ssq", "gbc"], ["hn"])
                                    pi_ = 0; pk = "ptp%d" % pi_
                                    for kc in range(8):
                                        tp(ptp[pi_][:, kc, :R], hn[:R, kc * 128:(kc + 1) * 128], identb[:R, :R], ["hn", "identb"], [pk])
                                    cp("dve", hnT[:, :, sub * R:sub * R + R], ptp[pi_][:, :, :R], [pk], ["hnT"])
                                chk(3)
                                for mt in list(range(0, 8)) + list(range(12, 24)):
                                    pj = cnt["pj"] % 2; cnt["pj"] += 1
                                    pjk = "ppj%d" % pj
                                    for kc in range(8):
                                        mm(ppj[pj][:, :N], winb[:, kc, mt * 128:(mt + 1) * 128], hnT[:, kc, :N], kc == 0, kc == 7, ["winb", "hnT"], [pjk])
                                    grp, ft = mt // 4, mt % 4
                                    if grp in (0, 1):
                                        sg = cnt["stg"] % 3; cnt["stg"] += 1
                                        sk = "stg%d" % sg
                                        if grp == 0:
                                            act(stg[sg][:, :N], ppj[pj][:, :N], AF.Copy, [pjk], [sk], scale=0.125)
                                        else:
                                            cp("dve", stg[sg][:, :N], ppj[pj][:, :N], [pjk], [sk])
                                        dst = sq["qT"] if grp == 0 else sq["kT"]
                                        S.dma(dst[ft * 128:(ft + 1) * 128, t0:t0 + N], stg[sg][:, :N], reads=[sk], chan="st_" + sk)
                                    elif grp == 3:
                                        sg = cnt["stg"] % 3; cnt["stg"] += 1
                                        sk = "stg%d" % sg
                                        silu_evac(ppj[pj][:, :N], pjk, stg[sg][:, :N], sk, N)
                                        S.dma(sq["gaT"][ft * 128:(ft + 1) * 128, t0:t0 + N], stg[sg][:, :N], reads=[sk], chan="st_" + sk)
                                    elif grp == 4:
                                        cp("dve", uT[:, ft, :N], ppj[pj][:, :N], [pjk], [uk])
                                    else:
                                        silu_evac(ppj[pj][:, :N], pjk, gsT[:, ft, :N], gk, N)
                                chk(4)
                                for sub in range(nsub):
                                    ki = cnt["kv"] % 2; cnt["kv"] += 1
                                    kk = "kvst%d" % ki; vk = "vst%d" % ki
                                    r0 = t0 + sub * R
                                    for half in range(2):
                                        pj = cnt["pj"] % 2; cnt["pj"] += 1
                                        pjk = "ppj%d" % pj
                                        for kc in range(8):
                                            mm(ppj[pj][:R, :], hnT[:, kc, sub * R:sub * R + R], winb[:, kc, 512 + half * 512:1024 + half * 512], kc == 0, kc == 7, ["winb", "hnT"], [pjk])
                                        if os.environ.get("KV", "abcde").find("d") >= 0:
                                            cp("dve", kvst[ki][:R, half * 512:(half + 1) * 512], ppj[pj][:R, :], [pjk], [kk])
                                        if half == 1 and os.environ.get("KV", "abcde").find("e") >= 0:
                                            act(vst[ki][:R, :], kvst[ki][:R, 512:1024], AF.Copy, [kk], [vk])
                                    if os.environ.get("KV", "abc").find("a") >= 0:
                                        S.dma(kout[l, b, r0:r0 + R, :], kvst[ki][:R, 0:512], reads=[kk], chan="st_" + kk)
                                    if os.environ.get("KV", "abc").find("b") >= 0:
                                        S.dma(vout[l, b, r0:r0 + R, :], kvst[ki][:R, 512:1024], reads=[kk], chan="st_" + kk)
                                    if os.environ.get("KV", "abc").find("c") >= 0:
                                        S.dma(sq["vv"][r0:r0 + R, :], vst[ki][:R, :], reads=[vk], chan="st_" + vk)
                                chk(5)
                                uv = uT[:, :, :N].rearrange("p f (k t) -> p f k t", t=8)
                                for j in range(4):
                                    pk = "pS%d" % j
                                    lo = 64 if j == 3 else 32 * j
                                    for ri in range(2):
                                        for ft in range(4):
                                            for tau in range(8):
                                                mm(pS[j][:, ri, ft, :nch], BL[lo:32 * j + 32, ft, tau, ri, :], uv[lo:32 * j + 32, ft, :, tau], tau == 0, tau == 7, ["BL", uk], [pk])
                                    cp("dve", Hall[:, 1:nch + 1, :, j::4].rearrange("p k r f -> p r f k"), pS[j][:, :, :, :nch], [pk], ["Hall"])
                                tt("pool", Hall[:, 1:nch + 1, :, 3::4], Hall[:, 1:nch + 1, :, 3::4], Hall[:, 1:nch + 1, :, 2::4], ALU.subtract, ["Hall"], ["Hall"])
                                chk(6)
                                def cmacc(eng, dst, X, a1, an, ap_, g):
                                    if g:
                                        P_, Q_ = Pg[:, :g], Qg[:, :g]
                                        Q0, Q1, X0, X1 = Qg[:, :g, 0, :], Qg[:, :g, 1, :], X[:, :, 0, :], X[:, :, 1, :]
                                    else:
                                        P_, Q_ = Pg[:, 0], Qg[:, 0]
                                        Q0, Q1, X0, X1 = Qg[:, 0, 0, :], Qg[:, 0, 1, :], X[:, 0, :], X[:, 1, :]
                                    tt(eng, P_, X, a1, ALU.mult, ["Hall", "A8"], ["Pg"])
                                    tt(eng, Q0, X1, an, ALU.mult, ["Hall", "A8"], ["Qg"])
                                    tt(eng, Q1, X0, ap_, ALU.mult, ["Hall", "A8"], ["Qg"])
                                    tt(eng, P_, P_, Q_, ALU.add, ["Pg", "Qg"], ["Pg"])
                                    tt(eng, dst, dst, P_, ALU.add, ["Pg", "Hall"], ["Hall"])

                                if nch == 64:
                                    G = 8
                                    Hv = Hall[:, 1:65].rearrange("p (g m) r c -> p g m r c", m=8)
                                    Cv = Hall[:, 0:64].rearrange("p (g m) r c -> p g m r c", m=8)[:, :, 0]
                                    b4 = lambda ap: ap.unsqueeze(1).to_broadcast([128, G, 2, 16])
                                    b3 = lambda ap: ap.unsqueeze(1).to_broadcast([128, G, 16])
                                    for j in range(1, 8):
                                        cmacc("dve", Hv[:, :, j], Hv[:, :, j - 1], b4(A1[:]), b3(Ain[:]), b3(Aip[:]), G)
                                    for g in range(G):
                                        cmacc("dve", Hall[:, 8 * (g + 1)], Hall[:, 8 * g], AJ1[:, 7], AJn[:, 7], PWi[:, 15, :], 0)
                                    for j in range(7):
                                        cmacc("dve", Hv[:, :, j], Cv, b4(AJ1[:, j]), b3(AJn[:, j]), b3(PWi[:, 8 + j, :]), G)
                                else:
                                    for k in range(nch):
                                        cmacc("pool", Hall[:, k + 1], Hall[:, k], A1[:], Ain[:], Aip[:], 0)
                                for ri in range(2):
                                    cp("pool", Hb[:, ri, :, :nch], Hall[:, 0:nch, ri, :].rearrange("p k r -> p r k"), ["Hall"], [hbk])
                                ts("pool", Hbn[:, :, :, :nch], Hb[:, :, :, :nch], -1.0, ALU.mult, [hbk], [hbnk])
                                if ti == NT // N - 1:
                                    ho_r, ho_i = (hrp, hip) if sq["kind"] == "p" else (hrs, his)
                                    S.dma(ho_r[l, b].rearrange("(pr g2) p -> (g2 p) pr", g2=2), Hall[:, nch, 0, :], reads=["Hall"], chan="hst", allow_slow_non_contiguous=True)
                                    S.dma(ho_i[l, b].rearrange("(pr g2) p -> (g2 p) pr", g2=2), Hall[:, nch, 1, :], reads=["Hall"], chan="hst", allow_slow_non_contiguous=True)
                                else:
                                    cp("pool", Hall[:, 0], Hall[:, nch], ["Hall"], ["Hall"])

                            def back(ti, bi_):
                                t0 = ti * N
                                uT = uTs[bi_]; gsT = gsTs[bi_]; Hb = Hbs[bi_]; Hbn = Hbns[bi_]
                                uk = 'uT%d' % bi_; gk = 'gsT%d' % bi_; hbk = 'Hb%d' % bi_; hbnk = 'Hbn%d' % bi_
                                uv = uT[:, :, :N].rearrange("p f (k t) -> p f k t", t=8)
                                chk(7)
                                for ft in range(4):
                                    yi = 0
                                    yk = "py%d" % yi
                                    for tau in range(8):
                                        for dl in range(tau + 1):
                                            mm(py[yi][:, tau, :nch], BD[:, ft, dl, :], uv[:, ft, :, tau - dl], dl == 0, False, ["BD", uk], [yk])
                                        for j in range(4):
                                            pr = ft * 4 + j
                                            for ri in range(2):
                                                if j < 3:
                                                    mm(py[yi][32 * j:32 * j + 32, tau, :nch], CA[:, tau, ri, pr, :], Hb[:, ri, pr, :nch], False, False, ["CA", hbk], [yk])
                                                else:
                                                    mm(py[yi][64:128, tau, :nch], CA[:, tau, ri, pr - 1:pr + 1, :].rearrange("p a b -> p (a b)"), Hb[:, ri, pr, :nch], False, False, ["CA", hbk], [yk])
                                                    mm(py[yi][64:96, tau, :nch], CA[:, tau, ri, pr - 1, :], Hbn[:, ri, pr, :nch], False, (ri == 1), ["CA", hbnk], [yk])
                                    yv = py[yi][:, :, :nch].rearrange("p t k -> p k t")
                                    ysv = ysb[:, :N].rearrange("p (k t) -> p k t", t=8)
                                    stt(ysv, uv[:, ft], dvec[:, ft:ft + 1], yv, ALU.mult, ALU.add, [uk, "dvec%d" % l, yk], ["ysb"])
                                    tt("pool", tg[:, :N], ysb[:, :N], ysb[:, :N], ALU.mult, ["ysb"], ["tg"])
                                    ts("pool", tg[:, :N], tg[:, :N], 0.044715, ALU.mult, ["tg"], ["tg"], s2=1.0, op1=ALU.add)
                                    tt("pool", tg[:, :N], tg[:, :N], ysb[:, :N], ALU.mult, ["tg", "ysb"], ["tg"])
                                    act(tg[:, :N], tg[:, :N], AF.Exp, ["tg"], ["tg"], scale=-1.5957691216057308)
                                    ts("dve", tg[:, :N], tg[:, :N], 1.0, ALU.add, ["tg"], ["tg"])
                                    recip(tg[:, :N], tg[:, :N], ["tg"], ["tg"])
                                    tt("dve", ygb[:, ft, :N], ysb[:, :N], tg[:, :N], ALU.mult, ["tg", "ysb"], ["ygb"])
                                chk(8)
                                for mt in range(4):
                                    pj = cnt["pj"] % 2; cnt["pj"] += 1
                                    pjk = "ppj%d" % pj
                                    for kc in range(4):
                                        mm(ppj[pj][:, :N], wglub[:, kc, mt * 128:(mt + 1) * 128], ygb[:, kc, :N], kc == 0, kc == 3, ["wglub", "ygb"], [pjk])
                                    act(tg[:, :N], ppj[pj][:, :N], AF.Exp, [pjk, "nbg%d" % l], ["tg"], scale=-1.0, bias=nbg[:, mt:mt + 1])
                                    ts("dve", tg[:, :N], tg[:, :N], 1.0, ALU.add, ["tg"], ["tg"])
                                    recip(tg[:, :N], tg[:, :N], ["tg"], ["tg"])
                                    tt("dve", sy[:, mt, :N], ygb[:, mt, :N], tg[:, :N], ALU.mult, ["tg", "ygb"], ["sy"])
                                    tt("pool", sqb[:, mt, :N], sy[:, mt, :N], sy[:, mt, :N], ALU.mult, ["sy"], ["sqb"])
                                pj = cnt["pj"] % 2; cnt["pj"] += 1
                                pjk = "ppj%d" % pj
                                for kc in range(4):
                                    mm(ppj[pj][:, :N], onesb[:], sqb[:, kc, :N], kc == 0, kc == 3, ["onesb", "sqb"], [pjk])
                                act(rstd[:, :N], ppj[pj][:, :N], AF.Ln, [pjk], ["ysb"], scale=1.0 / 512, bias=EPS)
                                act(rstd[:, :N], rstd[:, :N], AF.Exp, ["ysb"], ["ysb"], scale=-0.5)
                                for ft in range(4):
                                    tt("dve", sy[:, ft, :N], sy[:, ft, :N], rstd[:, :N], ALU.mult, ["sy", "ysb"], ["sy"])
                                    stt(sob[:, ft, :N], sy[:, ft, :N], gssm[:, ft:ft + 1], gsT[:, ft, :N], ALU.mult, ALU.mult, ["sy", "gssm%d" % l, gk], ["sqb"])
                                S.dma(sq["soT"].rearrange("(f p) t -> p f t", p=128)[:, :, t0:t0 + N], sob[:, :, :N], reads=["sqb"], chan="st_sob")

                            ntile = NT // N
                            front(0, 0)
                            for ti in range(ntile):
                                if ti + 1 < ntile:
                                    front(ti + 1, (ti + 1) % 2)
                                back(ti, ti % 2)
                    S.barrier()

                chk(10)
                with contextlib.ExitStack() as L2:
                    woutb = sb(L2, "woutb%d" % l, [128, 8, D], BF16)
                    gatt = sb(L2, "gatt%d" % l, [128, 4], F32)
                    NKmax = max(T, PL + TS)
                    nblk_max = (NKmax + 127) // 128
                    kTs = sb(L2, "kTs%d" % l, [128, 4, nblk_max * 128], BF16)
                    vsb = sb(L2, "vsb%d" % l, [128, nblk_max, 512], BF16)
                    qTt = sb(L2, "qTt%d" % l, [128, 4, 512], BF16)
                    gaTt = sb(L2, "gaTt%d" % l, [128, 4, 512], BF16)
                    catT = sb(L2, "catT%d" % l, [128, 8, 512], BF16)
                    ckst = [sb(L2, "ckst%d_%d" % (l, i), [128, 512], BF16) for i in range(2)]
                    e_sb = [sb(L2, "e_sb%d_%d" % (l, i), [128, 512], F32) for i in range(3)]
                    sp_sb = [sb(L2, "sp_sb%d_%d" % (l, i), [128, 512], BF16) for i in range(3)]
                    w_sb = [sb(L2, "w_sb%d_%d" % (l, i), [128, 512], BF16) for i in range(3)]
                    spsum = [sb(L2, "spsum%d_%d" % (l, i), [128, 512], BF16) for i in range(4)]
                    att = sb(L2, "att%d" % l, [128, 4, 512], F32)
                    asq = sb(L2, "asq%d" % l, [128, 4, 512], BF16)
                    rstd2 = sb(L2, "rstd2%d" % l, [128, 512], F32)
                    xt2 = [sb(L2, "xt2%d_%d" % (l, i), [128, D], F32) for i in range(2)]
                    xn = [sb(L2, "xn%d_%d" % (l, i), [128, D], F32) for i in range(2)]
                    junk2 = sb(L2, "junk2%d" % l, [128, D], BF16)
                    ssq2 = sb(L2, "ssq2%d" % l, [128, 4], F32)
                    pA = [L2.enter_context(nc.psum_tensor("pA%d_%d" % (l, i), [128, 512], F32)) for i in range(5)]
                    pAtt = [L2.enter_context(nc.psum_tensor("pAtt%d_%d" % (l, i), [128, 512], F32)) for i in range(1)]
                    pO = [L2.enter_context(nc.psum_tensor("pO%d_%d" % (l, i), [128, 512], F32)) for i in range(2)]
                    ptk = pO[0][:].bitcast(BF16)
                    for kc in range(8):
                        S.dma(woutb[:, kc, :], w_out[l, kc * 128:(kc + 1) * 128, :], writes=["woutb"], queue="pool")
                    S.dma(gatt[:], g_att[l].rearrange("(c p) -> p c", p=128), writes=["gatt"], allow_slow_non_contiguous=True)
                    c2 = dict(x=0, o=0, ck=0, blk=0)
                    for si, sq in enumerate(seqs):
                        NT, N, b, past = sq["NT"], sq["N"], sq["b"], sq["past"]
                        R = min(128, N); nsub = N // R
                        xsrc = sq["xin"] if l == 0 else (sq["xa"] if l % 2 == 1 else sq["xb"])
                        xdst = sq["xa"] if l % 2 == 0 else sq["xb"]
                        npast = past // 128
                        for pb in range(npast):
                            ci = c2["ck"] % 2; c2["ck"] += 1
                            ckk = "ckst%d" % ci
                            S.dma(ckst[ci][:], ck[l, b, pb * 128:(pb + 1) * 128, :], writes=[ckk], queue="pool")
                            for hp in range(4):
                                tp(ptk[:, hp * 128:(hp + 1) * 128], ckst[ci][:, hp * 128:(hp + 1) * 128], identb[:], [ckk, "identb"], ["pO0"])
                            cp("dve", kTs[:, :, pb * 128:(pb + 1) * 128], ptk[:, 0:512].rearrange("p (h s) -> p h s", h=4), ["pO0"], ["kTs"])
                        if npast:
                            S.dma(vsb[:, 0:npast, :], cv[l, b].rearrange("(n p) f -> p n f", p=128), writes=["vsb"], queue="pool")
                        S.dma(kTs[:, :, past:past + NT], sq["kT"].rearrange("(h p) t -> p h t", p=128), writes=["kTs"])
                        if NT >= 128:
                            S.dma(vsb[:, npast:npast + NT // 128, :], sq["vv"].rearrange("(n p) f -> p n f", p=128), writes=["vsb"])
                        else:
                            S.dma(vsb[:NT, npast, :], sq["vv"], writes=["vsb"])
                        for ti in range(NT // N):
                            t0 = ti * N
                            S.dma(qTt[:, :, :N], sq["qT"].rearrange("(h p) t -> p h t", p=128)[:, :, t0:t0 + N], writes=["qTt"])
                            S.dma(gaTt[:, :, :N], sq["gaT"].rearrange("(h p) t -> p h t", p=128)[:, :, t0:t0 + N], writes=["gaTt"])
                            S.dma(catT[:, 4:8, :N], sq["soT"].rearrange("(h p) t -> p h t", p=128)[:, :, t0:t0 + N], writes=["catT"])
                            chk(11)
                            blocks = []
                            if sq["kind"] == "p":
                                for j in reversed(range(nsub)):
                                    blocks.append((t0 + j * 128, 128, (t0 + j * 128) // 128, masks[:, j, :N]))
                                for kb in reversed(range(t0 // 128)):
                                    blocks.append((kb * 128, 128, kb, None))
                            else:
                                blocks.append((past, NT, npast, masks_s[:, :]))
                                for kb in reversed(range(npast)):
                                    blocks.append((kb * 128, 128, kb, None))
                            work = [(h, bi) for h in range(8) for bi in range(len(blocks))]

                            def info(idx):
                                h, bi = work[idx]
                                s0, Rk, vb, mk = blocks[bi]
                                return h, bi, s0, Rk, vb, mk, h // 2, 64 * (h % 2)

                            def sA(idx):
                                h, bi, s0, Rk, vb, mk, hp, base = info(idx)
                                sl = idx % 5
                                mm(pA[sl][:Rk, :N], kTs[base:base + 64, hp, s0:s0 + Rk], qTt[base:base + 64, hp, :N], True, False, ["kTs", "qTt"], ["pA%d" % sl])

                            def sB(idx):
                                h, bi, s0, Rk, vb, mk, hp, base = info(idx)
                                sl = idx % 5
                                act(e_sb[idx % 3][:Rk, :N], pA[sl][:Rk, :N], AF.Exp, ["pA%d" % sl], ["e_sb%d" % (idx % 3)])

                            def sC(idx):
                                h, bi, s0, Rk, vb, mk, hp, base = info(idx)
                                spk = "sp_sb%d" % (idx % 3)
                                act(sp_sb[idx % 3][:Rk, :N], e_sb[idx % 3][:Rk, :N], AF.Ln, ["e_sb%d" % (idx % 3)], [spk], bias=1.0)
                                if mk is not None:
                                    tt("pool", sp_sb[idx % 3][:Rk, :N], sp_sb[idx % 3][:Rk, :N], mk, ALU.mult, [spk, "masks"], [spk])

                            def sD(idx):
                                h, bi, s0, Rk, vb, mk, hp, base = info(idx)
                                sl = idx % 5
                                Ak, spk = "pA%d" % sl, "sp_sb%d" % (idx % 3)
                                spt = sp_sb[idx % 3]
                                first, lastb = (bi == 0), (bi == len(blocks) - 1)
                                mm(pA[sl][:Rk, :N], negU[:Rk, :Rk], spt[:Rk, :N], False, first, [spk, "negU"], [Ak])
                                if not first:
                                    prv = "spsum%d" % ((idx - 1) % 4)
                                    mm(pA[sl][:Rk, :N], negones[:, :Rk], spsum[(idx - 1) % 4][:, :N], False, True, [prv, "negones"], [Ak])
                                if not lastb:
                                    cur, prv = "spsum%d" % (idx % 4), "spsum%d" % ((idx - 1) % 4)
                                    if first:
                                        if Rk < 128:
                                            mset("dve", spsum[idx % 4][:, :N], 0.0, [cur])
                                        cp("dve", spsum[idx % 4][:Rk, :N], spt[:Rk, :N], [spk], [cur])
                                    else:
                                        tt("dve", spsum[idx % 4][:, :N], spsum[(idx - 1) % 4][:, :N], spt[:, :N], ALU.add, [spk, prv], [cur])

                            def sE(idx):
                                h, bi, s0, Rk, vb, mk, hp, base = info(idx)
                                sl = idx % 5
                                wk = "w_sb%d" % (idx % 3)
                                act(w_sb[idx % 3][:Rk, :N], pA[sl][:Rk, :N], AF.Exp, ["pA%d" % sl], [wk])
                                if mk is not None:
                                    tt("pool", w_sb[idx % 3][:Rk, :N], w_sb[idx % 3][:Rk, :N], mk, ALU.mult, [wk, "masks"], [wk])

                            def sF(idx):
                                h, bi, s0, Rk, vb, mk, hp, base = info(idx)
                                wk = "w_sb%d" % (idx % 3)
                                atk = "pAtt_%d" % (h % 2)
                                first, lastb = (bi == 0), (bi == len(blocks) - 1)
                                mm(pAtt[0][base:base + 64, :N], vsb[:Rk, vb, h * 64:(h + 1) * 64], w_sb[idx % 3][:Rk, :N], first, lastb, [wk, "vsb"], [atk])
                                if lastb:
                                    cp("dve", att[base:base + 64, hp, :N], pAtt[0][base:base + 64, :N], [atk], ["att"])

                            nw = len(work)
                            stages = [sA, sB, sC, sD, sE, sF]
                            for it in range(nw + 5):
                                for d, fn in enumerate(stages):
                                    if 0 <= it - d < nw:
                                        fn(it - d)
                            chk(12)
                            for hp in range(4):
                                tt("pool", asq[:, hp, :N], att[:, hp, :N], att[:, hp, :N], ALU.mult, ["att"], ["asq"])
                            for hp in range(4):
                                mm(pO[1][:, :N], onesb[:], asq[:, hp, :N], hp == 0, hp == 3, ["onesb", "asq"], ["pO1"])
                            act(rstd2[:, :N], pO[1][:, :N], AF.Ln, ["pO1"], ["rstd2"], scale=1.0 / 512, bias=EPS)
                            act(rstd2[:, :N], rstd2[:, :N], AF.Exp, ["rstd2"], ["rstd2"], scale=-0.5)
                            for hp in range(4):
                                tt("dve", att[:, hp, :N], att[:, hp, :N], rstd2[:, :N], ALU.mult, ["att", "rstd2"], ["att"])
                                stt(catT[:, hp, :N], att[:, hp, :N], gatt[:, hp:hp + 1], gaTt[:, hp, :N], ALU.mult, ALU.mult, ["att", "gatt", "gaTt"], ["catT"])
                            chk(13)
                            for sub in range(nsub):
                                xi = c2["x"] % 2; c2["x"] += 1
                                xk, nk = "xt2%d" % xi, "xn%d" % xi
                                r0 = t0 + sub * R
                                S.dma(xt2[xi][:R, :], xsrc[r0:r0 + R, :], writes=[xk])
                                for half in range(2):
                                    oi = c2["o"] % 2; c2["o"] += 1
                                    ok = "pO%d" % oi
                                    for kc in range(8):
                                        mm(pO[oi][:R, :], catT[:, kc, sub * R:sub * R + R], woutb[:, kc, half * 512:(half + 1) * 512], kc == 0, kc == 7, ["catT", "woutb"], [ok])
                                    tt("dve", xn[xi][:R, half * 512:(half + 1) * 512], pO[oi][:R, :], xt2[xi][:R, half * 512:(half + 1) * 512], ALU.add, [ok, xk], [nk])
                                if not last:
                                    S.dma(xdst[r0:r0 + R, :], xn[xi][:R, :], reads=[nk], chan="st_" + nk)
                                else:
                                    act(junk2[:R, :], xn[xi][:R, :], AF.Square, [nk], ["junk2"])
                                    S.op("dve", lambda e: e.tensor_reduce(out=ssq2[:R, 0:1], in_=junk2[:R, :], axis=mybir.AxisListType.X, op=ALU.add), ["junk2"], ["ssq2"])
                                    act(ssq2[:R, 1:2], ssq2[:R, 0:1], AF.Ln, ["ssq2"], ["ssq2"], scale=1.0 / D, bias=EPS)
                                    act(ssq2[:R, 2:3], ssq2[:R, 1:2], AF.Exp, ["ssq2"], ["ssq2"], scale=-0.5)
                                    stt(xt2[xi][:R, :], xn[xi][:R, :], ssq2[:R, 2:3], fgbc[:R, :], ALU.mult, ALU.mult, [nk, "ssq2", "fgbc"], [xk])
                                    S.dma(sq["yout"][r0:r0 + R, :], xt2[xi][:R, :], reads=[xk], chan="st_" + xk)
                    S.barrier()
        except StopBuild:
            pass
        S._final = True
        S.barrier()
    return nc


_CACHE = {}


def _run(inputs, T, TS, PL, DEPTH):
    key = (T, TS, PL, DEPTH)
    if key not in _CACHE:
        _CACHE[key] = build(T, TS, PL, DEPTH)
    nc = _CACHE[key]
    f = lambda a: np.ascontiguousarray(np.asarray(a, dtype=np.float32))
    in_maps = []
    for c in range(8):
        sl = slice(2 * c, 2 * c + 2)
        m = {
            "xp": f(inputs["x_prompt"][sl]), "xs": f(inputs["x_sample"][sl]),
            "ck": f(np.asarray(inputs["cache_k"])[:, sl].reshape(DEPTH, 2, PL, 512)),
            "cv": f(np.asarray(inputs["cache_v"])[:, sl].reshape(DEPTH, 2, PL, 512)),
            "sre": f(np.asarray(inputs["state_ssm_re"])[:, sl]), "sim": f(np.asarray(inputs["state_ssm_im"])[:, sl]),
            "ln_g": f(inputs["ln_g"]), "w_in": f(inputs["w_in"]),
            "a_re": f(inputs["ssm_a_re"]), "a_im": f(inputs["ssm_a_im"]), "log_dt": f(inputs["ssm_log_dt"]),
            "b_re": f(inputs["ssm_b_re"]), "b_im": f(inputs["ssm_b_im"]),
            "c_re": f(inputs["ssm_c_re"]), "c_im": f(inputs["ssm_c_im"]),
            "ssm_d": f(inputs["ssm_d"]), "w_glu": f(inputs["w_glu"]), "b_glu": f(inputs["b_glu"]),
            "g_att": f(inputs["g_att"]), "g_ssm": f(inputs["g_ssm"]), "w_out": f(inputs["w_out"]),
            "final_g": f(inputs["final_g"]),
        }
        in_maps.append(m)
    res = run_bass_kernel_spmd(nc, in_maps, core_ids=list(range(8)))
    R = res.results
    cat0 = lambda k: np.concatenate([r[k] for r in R], axis=0)
    cat1 = lambda k: np.concatenate([r[k] for r in R], axis=1)
    y_p = cat0("yp"); y_s = cat0("ys")
    k_p = cat1("kp").reshape(DEPTH, 16, T, 8, 64); v_p = cat1("vp").reshape(DEPTH, 16, T, 8, 64)
    k_s = cat1("ks").reshape(DEPTH, 16, TS, 8, 64); v_s = cat1("vs").reshape(DEPTH, 16, TS, 8, 64)
    return (y_p, y_s, k_p, v_p, cat1("hrp"), cat1("hip"), k_s, v_s, cat1("hrs"), cat1("his"))


def kernel(**inputs):
    T = int(np.shape(inputs["x_prompt"])[1]); TS = int(np.shape(inputs["x_sample"])[1])
    PL = int(np.shape(inputs["cache_k"])[2]); DEPTH = int(np.shape(inputs["w_in"])[0])
    return _run(inputs, T, TS, PL, DEPTH)
```

```python
import contextlib
import math
import numpy as np
import concourse.bass as bass
import concourse.mybir as mybir
from concourse.bass_utils import run_bass_kernel_spmd

F32 = mybir.dt.float32
BF16 = mybir.dt.bfloat16
I32 = mybir.dt.int32
AF = mybir.ActivationFunctionType
ALU = mybir.AluOpType
EPS = 1e-6
TWO_PI = 2.0 * math.pi


class Sched:
    def __init__(self, nc, stack):
        self.nc = nc
        self.stack = stack
        self.eng = {"pe": nc.tensor, "act": nc.scalar, "dve": nc.vector,
                    "pool": nc.gpsimd, "sp": nc.sync}
        self.sem = {}
        self.cnt = {}
        for e in ("pe", "act", "dve", "pool"):
            self.sem[e] = stack.enter_context(nc.semaphore("s_" + e))
            self.cnt[e] = 0
        self.waited = {e: {} for e in self.eng}
        self.lastw = {}
        self.readers = {}
        self.nchan = 0
        self.stopped = False

    def chan(self, name):
        if name not in self.sem:
            self.sem[name] = self.stack.enter_context(self.nc.semaphore("d%d" % self.nchan))
            self.nchan += 1
            self.cnt[name] = 0
        return name

    def _deps(self, engine, reads, writes):
        deps = {}

        def add(d, raw):
            if d is None:
                return
            src, c = d
            if src == engine and not raw:
                return
            if src == "pe" and engine == "pe":
                return
            if deps.get(src, 0) < c:
                deps[src] = c

        for r in reads:
            add(self.lastw.get(r), True)
        for w in writes:
            add(self.lastw.get(w), True)
            for src, c in self.readers.get(w, {}).items():
                add((src, c), False)
        return deps

    def _emit_waits(self, engine, deps):
        e = self.eng[engine]
        wd = self.waited[engine]
        for src, c in deps.items():
            if wd.get(src, 0) >= c:
                continue
            e.wait_ge(self.sem[src], c)
            wd[src] = c

    def op(self, engine, fn, reads=(), writes=()):
        if self.stopped:
            return None
        deps = self._deps(engine, reads, writes)
        self._emit_waits(engine, deps)
        ins = fn(self.eng[engine])
        ins.then_inc(self.sem[engine], 1)
        self.cnt[engine] += 1
        c = self.cnt[engine]
        for r in reads:
            self.readers.setdefault(r, {})[engine] = c
        for w in writes:
            self.lastw[w] = (engine, c)
            self.readers[w] = {}
        return ins

    def dma(self, out, in_, reads=(), writes=(), chan=None, queue="sp", **kw):
        if self.stopped:
            return None
        if chan is None:
            chan = "c_" + str(writes[0] if writes else reads[0])
        self.chan(chan)
        deps = self._deps("__dma__", reads, writes)
        self._emit_waits(queue, deps)
        ins = self.eng[queue].dma_start(out=out, in_=in_, **kw)
        ins.then_inc(self.sem[chan], 16)
        self.cnt[chan] += 16
        c = self.cnt[chan]
        for r in reads:
            self.readers.setdefault(r, {})[chan] = c
        for w in writes:
            self.lastw[w] = (chan, c)
            self.readers[w] = {}
        return ins

    def barrier(self):
        if self.stopped and getattr(self, "_final", False) is False:
            return
        allsrc = [s for s in self.cnt if self.cnt[s] > 0]
        for e in ("pe", "act", "dve", "pool", "sp"):
            self._emit_waits(e, {s: self.cnt[s] for s in allsrc if s != e})


class StopBuild(Exception):
    pass


def build(T, TS, PL, DEPTH):
    import os
    KSTOP = int(os.environ.get("KSTOP", "99"))

    def chk(level):
        if KSTOP <= level:
            S.stopped = True

    nc = bass.Bass("TRN2", target_bir_lowering=False)
    D = 1024

    def din(name, shape):
        return nc.dram_tensor(name, shape, F32, kind="ExternalInput").ap()

    def dout(name, shape):
        return nc.dram_tensor(name, shape, F32, kind="ExternalOutput").ap()

    def dscr(name, shape, dt):
        return nc.dram_tensor(name, shape, dt, kind="Internal").ap()

    xp = din("xp", [2, T, D]); xs = din("xs", [2, TS, D])
    ck = din("ck", [DEPTH, 2, PL, 512]); cv = din("cv", [DEPTH, 2, PL, 512])
    sre = din("sre", [DEPTH, 2, 32, 64]); sim = din("sim", [DEPTH, 2, 32, 64])
    ln_g = din("ln_g", [DEPTH, D]); w_in = din("w_in", [DEPTH, D, 3072])
    a_re = din("a_re", [DEPTH, 32, 64]); a_im = din("a_im", [DEPTH, 32, 64])
    log_dt = din("log_dt", [DEPTH, 32])
    b_re = din("b_re", [DEPTH, 32, 64, 16]); b_im = din("b_im", [DEPTH, 32, 64, 16])
    c_re = din("c_re", [DEPTH, 32, 16, 64]); c_im = din("c_im", [DEPTH, 32, 16, 64])
    ssm_d = din("ssm_d", [DEPTH, 512]); w_glu = din("w_glu", [DEPTH, 512, 512])
    b_glu = din("b_glu", [DEPTH, 512]); g_att = din("g_att", [DEPTH, 512])
    g_ssm = din("g_ssm", [DEPTH, 512]); w_out = din("w_out", [DEPTH, D, D])
    final_g = din("final_g", [D])
    yp = dout("yp", [2, T, D]); ys = dout("ys", [2, TS, D])
    kp = dout("kp", [DEPTH, 2, T, 512]); vp = dout("vp", [DEPTH, 2, T, 512])
    hrp = dout("hrp", [DEPTH, 2, 32, 64]); hip = dout("hip", [DEPTH, 2, 32, 64])
    ks = dout("ks", [DEPTH, 2, TS, 512]); vs = dout("vs", [DEPTH, 2, TS, 512])
    hrs = dout("hrs", [DEPTH, 2, 32, 64]); his = dout("his", [DEPTH, 2, 32, 64])

    seqs = []
    for b in range(2):
        seqs.append(dict(kind="p", b=b, NT=T, xin=xp[b], yout=yp[b], past=0))
    for b in range(2):
        seqs.append(dict(kind="s", b=b, NT=TS, xin=xs[b], yout=ys[b], past=PL))
    for si, sq in enumerate(seqs):
        NT = sq["NT"]
        sq["xa"] = dscr("xa%d" % si, [NT, D], F32)
        sq["xb"] = dscr("xb%d" % si, [NT, D], F32)
        for nm in ("qT", "kT", "gaT", "soT"):
            sq[nm] = dscr("%s%d" % (nm, si), [512, NT], BF16)
        sq["vv"] = dscr("vv%d" % si, [NT, 512], BF16)
        sq["N"] = min(512, NT)

    with contextlib.ExitStack() as top:
        S = Sched(nc, top)

        def sb(stack, name, shape, dt):
            return stack.enter_context(nc.sbuf_tensor(name, shape, dt))

        def mm(out, lhsT, rhs, start, stop, r, w):
            S.op("pe", lambda e: e.matmul(out, lhsT, rhs, start=start, stop=stop), r, w)

        def tp(out, in_, ident, r, w):
            S.op("pe", lambda e: e.transpose(out, in_, ident), r, w)

        def act(out, in_, func, r, w, scale=1.0, bias=0.0, accum=None):
            if accum is None:
                S.op("act", lambda e: e.activation(out=out, in_=in_, func=func, bias=bias, scale=scale), r, w)
            else:
                S.op("act", lambda e: e.activation(out=out, in_=in_, func=func, bias=bias, scale=scale, accum_out=accum), r, w)

        def tt(eng, out, in0, in1, op, r, w):
            S.op(eng, lambda e: e.tensor_tensor(out=out, in0=in0, in1=in1, op=op), r, w)

        def ts(eng, out, in0, s1, op0, r, w, s2=None, op1=None):
            if op1 is None:
                S.op(eng, lambda e: e.tensor_scalar(out=out, in0=in0, scalar1=s1, scalar2=None, op0=op0), r, w)
            else:
                S.op(eng, lambda e: e.tensor_scalar(out=out, in0=in0, scalar1=s1, scalar2=s2, op0=op0, op1=op1), r, w)

        def stt(out, in0, scalar, in1, op0, op1, r, w):
            S.op("dve", lambda e: e.scalar_tensor_tensor(out=out, in0=in0, scalar=scalar, in1=in1, op0=op0, op1=op1), r, w)

        def cp(eng, out, in_, r, w):
            S.op(eng, lambda e: e.tensor_copy(out=out, in_=in_), r, w)

        def recip(out, in_, r, w):
            S.op("dve", lambda e: e.reciprocal(out=out, in_=in_), r, w)

        def mset(eng, ap, val, w):
            S.op(eng, lambda e: e.memset(ap, val), (), w)

        identb = sb(top, "identb", [128, 128], BF16)
        identf = sb(top, "identf", [128, 128], F32)
        negU = sb(top, "negU", [128, 128], BF16)
        negones = sb(top, "negones", [128, 128], BF16)
        onesb = sb(top, "onesb", [128, 128], BF16)
        mask32 = sb(top, "mask32", [128, 128], F32)
        fgbc = sb(top, "fgbc", [128, D], F32)
        NP = seqs[0]["N"]
        ndiag = NP // 128
        masks = sb(top, "masks", [128, ndiag, NP], BF16)
        masks_s = sb(top, "masks_s", [TS, TS], BF16)
        mset("pool", identb[:], 0.0, ["identb"])
        S.op("pool", lambda e: e.affine_select(out=identb[:], in_=identb[:], pattern=[[-1, 128]], compare_op=ALU.not_equal, fill=1.0, base=0, channel_multiplier=1), ["identb"], ["identb"])
        mset("pool", identf[:], 0.0, ["identf"])
        S.op("pool", lambda e: e.affine_select(out=identf[:], in_=identf[:], pattern=[[-1, 128]], compare_op=ALU.not_equal, fill=1.0, base=0, channel_multiplier=1), ["identf"], ["identf"])
        mset("pool", negU[:], -1.0, ["negU"])
        S.op("pool", lambda e: e.affine_select(out=negU[:], in_=negU[:], pattern=[[-1, 128]], compare_op=ALU.is_ge, fill=0.0, base=0, channel_multiplier=1), ["negU"], ["negU"])
        mset("pool", negones[:], -1.0, ["negones"])
        mset("pool", onesb[:], 1.0, ["onesb"])
        mset("pool", mask32[:], 0.0, ["mask32"])
        for j in range(4):
            mset("pool", mask32[32 * j:32 * j + 32, 32 * j:32 * j + 32], 1.0, ["mask32"])
        mset("pool", masks[:], 1.0, ["masks"])
        for j in range(ndiag):
            S.op("pool", lambda e: e.affine_select(out=masks[:, j, :], in_=masks[:, j, :], pattern=[[1, NP]], compare_op=ALU.is_gt, fill=0.0, base=-128 * j, channel_multiplier=-1), ["masks"], ["masks"])
        mset("pool", masks_s[:], 1.0, ["masks_s"])
        S.op("pool", lambda e: e.affine_select(out=masks_s[:], in_=masks_s[:], pattern=[[1, TS]], compare_op=ALU.is_gt, fill=0.0, base=0, channel_multiplier=-1), ["masks_s"], ["masks_s"])
        S.dma(fgbc[:], final_g.partition_broadcast(128), writes=["fgbc"])

        try:
            chk(0)
            for l in range(DEPTH):
                last = (l == DEPTH - 1)
                with contextlib.ExitStack() as L1:
                    winb = sb(L1, "winb%d" % l, [128, 8, 3072], BF16)
                    wglub = sb(L1, "wglub%d" % l, [128, 4, 512], BF16)
                    gbc = sb(L1, "gbc%d" % l, [128, D], F32)
                    dvec = sb(L1, "dvec%d" % l, [128, 4], F32)
                    nbg = sb(L1, "nbg%d" % l, [128, 4], F32)
                    gssm = sb(L1, "gssm%d" % l, [128, 4], F32)
                    PWr = sb(L1, "PWr%d" % l, [128, 16, 16], F32)
                    PWi = sb(L1, "PWi%d" % l, [128, 16, 16], F32)
                    A1 = sb(L1, "A1%d" % l, [128, 2, 16], F32)
                    Aip = sb(L1, "Aip%d" % l, [128, 16], F32)
                    Ain = sb(L1, "Ain%d" % l, [128, 16], F32)
                    AJ1 = sb(L1, "AJ1%d" % l, [128, 8, 2, 16], F32)
                    AJn = sb(L1, "AJn%d" % l, [128, 8, 16], F32)
                    BD = sb(L1, "BD%d" % l, [128, 4, 8, 128], BF16)
                    BL = sb(L1, "BL%d" % l, [128, 4, 8, 2, 128], BF16)
                    CA = sb(L1, "CA%d" % l, [128, 8, 2, 16, 32], BF16)
                    for kc in range(8):
                        S.dma(winb[:, kc, :], w_in[l, kc * 128:(kc + 1) * 128, :], writes=["winb"], queue="pool")
                    S.dma(wglub[:], w_glu[l].rearrange("(kc p) n -> p kc n", p=128), writes=["wglub"], queue="pool")
                    S.dma(gbc[:], ln_g[l].partition_broadcast(128), writes=["gbc"])
                    for tl, src, nm in ((dvec, ssm_d, "dvec%d" % l), (nbg, b_glu, "nbg%d" % l), (gssm, g_ssm, "gssm%d" % l)):
                        S.dma(tl[:], src[l].rearrange("(c p) -> p c", p=128), writes=[nm], allow_slow_non_contiguous=True)
                    ts("dve", nbg[:], nbg[:], -1.0, ALU.mult, ["nbg%d" % l], ["nbg%d" % l])

                    with contextlib.ExitStack() as SU:
                        k_ = "su"
                        aqr = sb(SU, "aqr%d" % l, [128, 16], F32); aqi = sb(SU, "aqi%d" % l, [128, 16], F32)
                        dtq = sb(SU, "dtq%d" % l, [128, 16], F32)
                        adt = sb(SU, "adt%d" % l, [128, 16], F32); ang = sb(SU, "ang%d" % l, [128, 16], F32)
                        mag = sb(SU, "mag%d" % l, [128, 16, 16], F32)
                        ph = sb(SU, "ph%d" % l, [128, 2, 16, 16], F32)
                        pht = sb(SU, "pht%d" % l, [128, 2, 16, 16], F32)
                        phi = sb(SU, "phi%d" % l, [128, 2, 16, 16], I32)
                        sc = sb(SU, "sc%d" % l, [128, 2, 16, 16], F32)
                        t1 = sb(SU, "t1%d" % l, [128, 16], F32); t2 = sb(SU, "t2%d" % l, [128, 16], F32)
                        t3 = sb(SU, "t3%d" % l, [128, 16], F32)
                        fr = sb(SU, "fr%d" % l, [128, 16], F32); fi = sb(SU, "fi%d" % l, [128, 16], F32)
                        Bq0r = sb(SU, "Bq0r%d" % l, [128, 16, 32], F32); Bq0i = sb(SU, "Bq0i%d" % l, [128, 16, 32], F32)
                        Bbr = sb(SU, "Bbr%d" % l, [128, 16, 32], F32); Bbi = sb(SU, "Bbi%d" % l, [128, 16, 32], F32)
                        nBbi = sb(SU, "nBbi%d" % l, [128, 16, 32], F32)
                        u1 = sb(SU, "u1%d" % l, [128, 16, 32], F32); u2 = sb(SU, "u2%d" % l, [128, 16, 32], F32)
                        Yr = sb(SU, "Yr%d" % l, [128, 16, 32], F32); Yi = sb(SU, "Yi%d" % l, [128, 16, 32], F32)
                        Zr = sb(SU, "Zr%d" % l, [32, 16, 128], F32); Zi = sb(SU, "Zi%d" % l, [32, 16, 128], F32)
                        CTr = sb(SU, "CTr%d" % l, [128, 16, 32], F32); CTi = sb(SU, "CTi%d" % l, [128, 16, 32], F32)
                        Xr = sb(SU, "Xr%d" % l, [128, 9, 16, 32], F32); Xi = sb(SU, "Xi%d" % l, [128, 9, 16, 32], F32)
                        pss = SU.enter_context(nc.psum_tensor("pss%d" % l, [128, 4, 128], F32))
                        psc = SU.enter_context(nc.psum_tensor("psc%d" % l, [128, 16, 32], F32))
                        K = [k_]
                        S.dma(aqr[:], a_re[l].rearrange("(pr g2) p -> (g2 p) pr", g2=2), writes=K, allow_slow_non_contiguous=True)
                        S.dma(aqi[:], a_im[l].rearrange("(pr g2) p -> (g2 p) pr", g2=2), writes=K, allow_slow_non_contiguous=True)
                        mset("dve", Bq0r[:], 0.0, K); mset("dve", Bq0i[:], 0.0, K)
                        mset("dve", Zr[:], 0.0, K); mset("dve", Zi[:], 0.0, K)
                        for g2 in range(2):
                            S.dma(dtq[g2 * 64:(g2 + 1) * 64, :], log_dt[l].rearrange("(pr g2) -> g2 pr", g2=2)[g2].partition_broadcast(64), reads=K, writes=K, allow_slow_non_contiguous=True)
                            S.dma(Bq0r[g2 * 64:(g2 + 1) * 64, :, g2 * 16:(g2 + 1) * 16], b_re[l].rearrange("(pr g2) p c -> g2 p pr c", g2=2)[g2], reads=K, writes=K)
                            S.dma(Bq0i[g2 * 64:(g2 + 1) * 64, :, g2 * 16:(g2 + 1) * 16], b_im[l].rearrange("(pr g2) p c -> g2 p pr c", g2=2)[g2], reads=K, writes=K)
                            S.dma(Zr[g2 * 16:(g2 + 1) * 16, :, g2 * 64:(g2 + 1) * 64], c_re[l].rearrange("(pr g2) c p -> g2 c pr p", g2=2)[g2], reads=K, writes=K)
                            S.dma(Zi[g2 * 16:(g2 + 1) * 16, :, g2 * 64:(g2 + 1) * 64], c_im[l].rearrange("(pr g2) c p -> g2 c pr p", g2=2)[g2], reads=K, writes=K)
                        act(dtq[:], dtq[:], AF.Exp, K, K)
                        tt("dve", adt[:], aqr[:], dtq[:], ALU.mult, K, K)
                        tt("dve", ang[:], aqi[:], dtq[:], ALU.mult, K, K)
                        for di, dl in enumerate(list(range(9)) + [16, 24, 32, 40, 48, 56, 64]):
                            act(mag[:, di, :], adt[:], AF.Exp, K, K, scale=float(dl))
                            ts("dve", ph[:, 0, di, :], ang[:], float(dl), ALU.mult, K, K)
                            ts("dve", ph[:, 1, di, :], ang[:], float(dl), ALU.mult, K, K, s2=math.pi / 2, op1=ALU.add)
                        ts("dve", pht[:], ph[:], 1.0 / TWO_PI, ALU.mult, K, K)
                        cp("dve", phi[:], pht[:], K, K)
                        cp("dve", pht[:], phi[:], K, K)
                        stt(ph[:], pht[:], -TWO_PI, ph[:], ALU.mult, ALU.add, K, K)
                        ts("dve", ph[:], ph[:], -math.pi, ALU.max, K, K, s2=math.pi, op1=ALU.min)
                        act(sc[:], ph[:], AF.Sin, K, K)
                        chk(1)
                        tt("dve", PWi[:], mag[:], sc[:, 0], ALU.mult, K, ["PW"])
                        tt("dve", PWr[:], mag[:], sc[:, 1], ALU.mult, K, ["PW"])
                        cp("dve", A1[:, 0, :], PWr[:, 8, :], ["PW"], ["A8"]); cp("dve", A1[:, 1, :], PWr[:, 8, :], ["PW"], ["A8"])
                        cp("dve", Aip[:], PWi[:, 8, :], ["PW"], ["A8"])
                        ts("dve", Ain[:], PWi[:, 8, :], -1.0, ALU.mult, ["PW"], ["A8"])
                        cp("dve", AJ1[:, :, 0, :], PWr[:, 8:16, :], ["PW"], ["A8"]); cp("dve", AJ1[:, :, 1, :], PWr[:, 8:16, :], ["PW"], ["A8"])
                        ts("dve", AJn[:], PWi[:, 8:16, :], -1.0, ALU.mult, ["PW"], ["A8"])
                        tt("dve", t1[:], aqr[:], aqr[:], ALU.mult, K, K)
                        tt("dve", t2[:], aqi[:], aqi[:], ALU.mult, K, K)
                        tt("dve", t1[:], t1[:], t2[:], ALU.add, K, K)
                        recip(t1[:], t1[:], K, K)
                        ts("dve", t2[:], PWr[:, 1, :], -1.0, ALU.add, K + ["PW"], K)
                        tt("dve", fr[:], t2[:], aqr[:], ALU.mult, K, K)
                        tt("dve", t3[:], PWi[:, 1, :], aqi[:], ALU.mult, K + ["PW"], K)
                        tt("dve", fr[:], fr[:], t3[:], ALU.add, K, K)
                        tt("dve", fr[:], fr[:], t1[:], ALU.mult, K, K)
                        tt("dve", fi[:], PWi[:, 1, :], aqr[:], ALU.mult, K + ["PW"], K)
                        tt("dve", t3[:], t2[:], aqi[:], ALU.mult, K, K)
                        tt("dve", fi[:], fi[:], t3[:], ALU.subtract, K, K)
                        tt("dve", fi[:], fi[:], t1[:], ALU.mult, K, K)
                        bc = lambda ap: ap.unsqueeze(2).to_broadcast([128, 16, 32])
                        tt("dve", u1[:], Bq0r[:], bc(fr[:]), ALU.mult, K, K)
                        tt("dve", u2[:], Bq0i[:], bc(fi[:]), ALU.mult, K, K)
                        tt("dve", Bbr[:], u1[:], u2[:], ALU.subtract, K, K)
                        tt("dve", u1[:], Bq0i[:], bc(fr[:]), ALU.mult, K, K)
                        tt("dve", u2[:], Bq0r[:], bc(fi[:]), ALU.mult, K, K)
                        tt("dve", Bbi[:], u1[:], u2[:], ALU.add, K, K)
                        ts("dve", nBbi[:], Bbi[:], -1.0, ALU.mult, K, K)
                        for (Z, CT) in ((Zr, CTr), (Zi, CTi)):
                            for pr in range(16):
                                tp(psc[:, pr, :], Z[:, pr, :], identf[0:32, 0:32], K + ["identf"], ["psc"])
                            cp("dve", CT[:], psc[:], ["psc"], K)
                        for dl in range(9):
                            pr_ = bc(PWr[:, dl, :]); pi_ = bc(PWi[:, dl, :])
                            tt("dve", u1[:], CTr[:], pr_, ALU.mult, K + ["PW"], K)
                            tt("dve", u2[:], CTi[:], pi_, ALU.mult, K + ["PW"], K)
                            tt("dve", Xr[:, dl], u1[:], u2[:], ALU.subtract, K, K)
                            tt("dve", u1[:], CTr[:], pi_, ALU.mult, K + ["PW"], K)
                            tt("dve", u2[:], CTi[:], pr_, ALU.mult, K + ["PW"], K)
                            tt("dve", Xi[:, dl], u1[:], u2[:], ALU.add, K, K)
                        for tau in range(8):
                            cp("dve", CA[:, tau, 0, :, :], Xr[:, tau + 1], K, ["CA"])
                            ts("dve", CA[:, tau, 1, :, :], Xi[:, tau + 1], -1.0, ALU.mult, K, ["CA"])
                        for tau in range(8):
                            pr_ = bc(PWr[:, 7 - tau, :]); pi_ = bc(PWi[:, 7 - tau, :])
                            tt("dve", u1[:], Bbr[:], pr_, ALU.mult, K + ["PW"], K)
                            tt("dve", u2[:], Bbi[:], pi_, ALU.mult, K + ["PW"], K)
                            tt("dve", Yr[:], u1[:], u2[:], ALU.subtract, K, K)
                            tt("dve", u1[:], Bbr[:], pi_, ALU.mult, K + ["PW"], K)
                            tt("dve", u2[:], Bbi[:], pr_, ALU.mult, K + ["PW"], K)
                            tt("dve", Yi[:], u1[:], u2[:], ALU.add, K, K)
                            for ri, Y in ((0, Yr), (1, Yi)):
                                for ft in range(4):
                                    tp(pss[:, ft, :], Y[:, ft * 4:(ft + 1) * 4, :].rearrange("p a b -> p (a b)"), identf[:], K + ["identf"], ["pss"])
                                cp("dve", BL[:, :, tau, ri, :], pss[:], ["pss"], ["BL"])
                        for dl in range(8):
                            for ft in range(4):
                                fl = lambda t_: t_[:, ft * 4:(ft + 1) * 4, :].rearrange("p a b -> p (a b)")
                                mm(pss[:, ft, :], fl(Bbr), Xr[:, dl, ft * 4:(ft + 1) * 4, :].rearrange("p a b -> p (a b)"), True, False, K, ["pss"])
                                mm(pss[:, ft, :], fl(nBbi), Xi[:, dl, ft * 4:(ft + 1) * 4, :].rearrange("p a b -> p (a b)"), False, True, K, ["pss"])
                            tt("dve", BD[:, :, dl, :], pss[:], mask32[:].unsqueeze(1).to_broadcast([128, 4, 128]), ALU.mult, ["pss", "mask32"], ["BD"])
                    chk(2)
                    S.barrier()

                    with contextlib.ExitStack() as W1:
                        xt = [sb(W1, "xt%d_%d" % (l, i), [128, D], F32) for i in range(2)]
                        ssq = sb(W1, "ssq%d" % l, [128, 4], F32)
                        hn = sb(W1, "hn%d" % l, [128, D], BF16)
                        junk = hn
                        hnT = sb(W1, "hnT%d" % l, [128, 8, 512], BF16)
                        stg = [sb(W1, "stg%d_%d" % (l, i), [128, 512], BF16) for i in range(3)]
                        ef = [sb(W1, "ef%d_%d" % (l, i), [128, 512], F32) for i in range(2)]
                        kvst = [sb(W1, "kvst%d_%d" % (l, i), [128, 1024], F32) for i in range(2)]
                        vst = [sb(W1, "vst%d_%d" % (l, i), [128, 512], BF16) for i in range(2)]
                        uTs = [sb(W1, "uT%d_%d" % (l, i), [128, 4, 512], BF16) for i in range(2)]
                        gsTs = [sb(W1, "gsT%d_%d" % (l, i), [128, 4, 512], BF16) for i in range(2)]
                        Hall = sb(W1, "Hall%d" % l, [128, 65, 2, 16], F32)
                        Hbs = [sb(W1, "Hb%d_%d" % (l, i), [128, 2, 16, 64], BF16) for i in range(2)]
                        Hbns = [sb(W1, "Hbn%d_%d" % (l, i), [128, 2, 16, 64], BF16) for i in range(2)]
                        Pg = sb(W1, "Pg%d" % l, [128, 8, 2, 16], F32)
                        Qg = sb(W1, "Qg%d" % l, [128, 8, 2, 16], F32)
                        ysb = sb(W1, "ysb%d" % l, [128, 512], F32)
                        ygb = sb(W1, "ygb%d" % l, [128, 4, 512], BF16)
                        sy = sb(W1, "sy%d" % l, [128, 4, 512], F32)
                        sqb = sb(W1, "sqb%d" % l, [128, 4, 512], BF16)
                        rstd = ysb
                        sob = sqb
                        ptp = [W1.enter_context(nc.psum_tensor("ptp%d_%d" % (l, i), [128, 8, 128], BF16)) for i in range(1)]
                        ppj = [W1.enter_context(nc.psum_tensor("ppj%d_%d" % (l, i), [128, 512], F32)) for i in range(2)]
                        pS = [W1.enter_context(nc.psum_tensor("pS%d_%d" % (l, i), [128, 2, 4, 64], F32)) for i in range(4)]
                        py = [W1.enter_context(nc.psum_tensor("py%d_%d" % (l, i), [128, 8, 64], F32)) for i in range(1)]
                        cnt = dict(x=0, pj=0, stg=0, ef=0, kv=0, py=0)

                        def silu_evac(ps, pskey, out, outkey, N):
                            i = cnt["ef"] % 2; cnt["ef"] += 1
                            e_ = ef[i]; ek = "ef%d" % i
                            act(e_[:, :N], ps, AF.Exp, [pskey], [ek], scale=-1.0)
                            ts("dve", e_[:, :N], e_[:, :N], 1.0, ALU.add, [ek], [ek])
                            recip(e_[:, :N], e_[:, :N], [ek], [ek])
                            tt("dve", out, e_[:, :N], ps, ALU.mult, [ek, pskey], [outkey])

                        for si, sq in enumerate(seqs):
                            NT, N, b = sq["NT"], sq["N"], sq["b"]
                            R = min(128, N); nsub = N // R; nch = N // 8
                            xsrc = sq["xin"] if l == 0 else (sq["xa"] if l % 2 == 1 else sq["xb"])
                            kout, vout = (kp, vp) if sq["kind"] == "p" else (ks, vs)
                            if sq["kind"] == "p":
                                mset("pool", Hall[:, 0], 0.0, ["Hall"])
                            else:
                                S.dma(Hall[:, 0, 0, :], sre[l, b].rearrange("(pr g2) p -> (g2 p) pr", g2=2), writes=["Hall"], chan="hld", allow_slow_non_contiguous=True)
                                S.dma(Hall[:, 0, 1, :], sim[l, b].rearrange("(pr g2) p -> (g2 p) pr", g2=2), writes=["Hall"], chan="hld", allow_slow_non_contiguous=True)
                            def front(ti, bi_):
                                t0 = ti * N
                                uT = uTs[bi_]; gsT = gsTs[bi_]; Hb = Hbs[bi_]; Hbn = Hbns[bi_]
                                uk = 'uT%d' % bi_; gk = 'gsT%d' % bi_; hbk = 'Hb%d' % bi_; hbnk = 'Hbn%d' % bi_
                                for sub in range(nsub):
                                    xi = cnt["x"] % 2; cnt["x"] += 1
                                    xk = "xt%d" % xi
                                    r0 = t0 + sub * R
                                    S.dma(xt[xi][:R, :], xsrc[r0:r0 + R, :], writes=[xk])
                                    act(junk[:R, :], xt[xi][:R, :], AF.Square, [xk], ["hn", "ssq"], accum=ssq[:R, 0:1])
                                    act(ssq[:R, 1:2], ssq[:R, 0:1], AF.Ln, ["ssq"], ["ssq"], scale=1.0 / D, bias=EPS)
                                    act(ssq[:R, 2:3], ssq[:R, 1:2], AF.Exp, ["ssq"], ["ssq"], scale=-0.5)
                                    stt(hn[:R, :], xt[xi][:R, :], ssq[:R, 2:3], gbc[:R, :], ALU.mult, ALU.mult, [xk, "ssq", "gbc"], ["hn"])
                                    pi_ = 0; pk = "ptp%d" % pi_
                                    for kc in range(8):
                                        tp(ptp[pi_][:, kc, :R], hn[:R, kc * 128:(kc + 1) * 128], identb[:R, :R], ["hn", "identb"], [pk])
                                    cp("dve", hnT[:, :, sub * R:sub * R + R], ptp[pi_][:, :, :R], [pk], ["hnT"])
                                chk(3)
                                for mt in list(range(0, 8)) + list(range(12, 24)):
                                    pj = cnt["pj"] % 2; cnt["pj"] += 1
                                    pjk = "ppj%d" % pj
                                    for kc in range(8):
                                        mm(ppj[pj][:, :N], winb[:, kc, mt * 128:(mt + 1) * 128], hnT[:, kc, :N], kc == 0, kc == 7, ["winb", "hnT"], [pjk])
                                    grp, ft = mt // 4, mt % 4
                                    if grp in (0, 1):
                                        sg = cnt["stg"] % 3; cnt["stg"] += 1
                                        sk = "stg%d" % sg
                                        if grp == 0:
                                            act(stg[sg][:, :N], ppj[pj][:, :N], AF.Copy, [pjk], [sk], scale=0.125)
                                        else:
                                            act(stg[sg][:, :N], ppj[pj][:, :N], AF.Copy, [pjk], [sk])
                                        dst = sq["qT"] if grp == 0 else sq["kT"]
                                        S.dma(dst[ft * 128:(ft + 1) * 128, t0:t0 + N], stg[sg][:, :N], reads=[sk], chan="st_" + sk)
                                    elif grp == 3:
                                        sg = cnt["stg"] % 3; cnt["stg"] += 1
                                        sk = "stg%d" % sg
                                        silu_evac(ppj[pj][:, :N], pjk, stg[sg][:, :N], sk, N)
                                        S.dma(sq["gaT"][ft * 128:(ft + 1) * 128, t0:t0 + N], stg[sg][:, :N], reads=[sk], chan="st_" + sk)
                                    elif grp == 4:
                                        act(uT[:, ft, :N], ppj[pj][:, :N], AF.Copy, [pjk], [uk])
                                    else:
                                        silu_evac(ppj[pj][:, :N], pjk, gsT[:, ft, :N], gk, N)
                                chk(4)
                                for sub in range(nsub):
                                    ki = cnt["kv"] % 2; cnt["kv"] += 1
                                    kk = "kvst%d" % ki; vk = "vst%d" % ki
                                    r0 = t0 + sub * R
                                    for half in range(2):
                                        pj = cnt["pj"] % 2; cnt["pj"] += 1
                                        pjk = "ppj%d" % pj
                                        for kc in range(8):
                                            mm(ppj[pj][:R, :], hnT[:, kc, sub * R:sub * R + R], winb[:, kc, 512 + half * 512:1024 + half * 512], kc == 0, kc == 7, ["winb", "hnT"], [pjk])
                                        if os.environ.get("KV", "abcde").find("d") >= 0:
                                            cp("dve", kvst[ki][:R, half * 512:(half + 1) * 512], ppj[pj][:R, :], [pjk], [kk])
                                        if half == 1 and os.environ.get("KV", "abcde").find("e") >= 0:
                                            act(vst[ki][:R, :], kvst[ki][:R, 512:1024], AF.Copy, [kk], [vk])
                                    if os.environ.get("KV", "abc").find("a") >= 0:
                                        S.dma(kout[l, b, r0:r0 + R, :], kvst[ki][:R, 0:512], reads=[kk], chan="st_" + kk)
                                    if os.environ.get("KV", "abc").find("b") >= 0:
                                        S.dma(vout[l, b, r0:r0 + R, :], kvst[ki][:R, 512:1024], reads=[kk], chan="st_" + kk)
                                    if os.environ.get("KV", "abc").find("c") >= 0:
                                        S.dma(sq["vv"][r0:r0 + R, :], vst[ki][:R, :], reads=[vk], chan="st_" + vk)
                                chk(5)
                                uv = uT[:, :, :N].rearrange("p f (k t) -> p f k t", t=8)
                                for j in range(4):
                                    pk = "pS%d" % j
                                    lo = 64 if j == 3 else 32 * j
                                    for ri in range(2):
                                        for ft in range(4):
                                            for tau in range(8):
                                                mm(pS[j][:, ri, ft, :nch], BL[lo:32 * j + 32, ft, tau, ri, :], uv[lo:32 * j + 32, ft, :, tau], tau == 0, tau == 7, ["BL", uk], [pk])
                                    cp("dve", Hall[:, 1:nch + 1, :, j::4].rearrange("p k r f -> p r f k"), pS[j][:, :, :, :nch], [pk], ["Hall"])
                                tt("pool", Hall[:, 1:nch + 1, :, 3::4], Hall[:, 1:nch + 1, :, 3::4], Hall[:, 1:nch + 1, :, 2::4], ALU.subtract, ["Hall"], ["Hall"])
                                chk(6)
                                def cmacc(eng, dst, X, a1, an, ap_, g):
                                    if g:
                                        P_, Q_ = Pg[:, :g], Qg[:, :g]
                                        Q0, Q1, X0, X1 = Qg[:, :g, 0, :], Qg[:, :g, 1, :], X[:, :, 0, :], X[:, :, 1, :]
                                    else:
                                        P_, Q_ = Pg[:, 0], Qg[:, 0]
                                        Q0, Q1, X0, X1 = Qg[:, 0, 0, :], Qg[:, 0, 1, :], X[:, 0, :], X[:, 1, :]
                                    tt(eng, P_, X, a1, ALU.mult, ["Hall", "A8"], ["Pg"])
                                    tt(eng, Q0, X1, an, ALU.mult, ["Hall", "A8"], ["Qg"])
                                    tt(eng, Q1, X0, ap_, ALU.mult, ["Hall", "A8"], ["Qg"])
                                    tt(eng, P_, P_, Q_, ALU.add, ["Pg", "Qg"], ["Pg"])
                                    tt(eng, dst, dst, P_, ALU.add, ["Pg", "Hall"], ["Hall"])

                                if nch == 64:
                                    G = 8
                                    Hv = Hall[:, 1:65].rearrange("p (g m) r c -> p g m r c", m=8)
                                    Cv = Hall[:, 0:64].rearrange("p (g m) r c -> p g m r c", m=8)[:, :, 0]
                                    b4 = lambda ap: ap.unsqueeze(1).to_broadcast([128, G, 2, 16])
                                    b3 = lambda ap: ap.unsqueeze(1).to_broadcast([128, G, 16])
                                    for j in range(1, 8):
                                        cmacc("pool", Hv[:, :, j], Hv[:, :, j - 1], b4(A1[:]), b3(Ain[:]), b3(Aip[:]), G)
                                    for g in range(G):
                                        cmacc("pool", Hall[:, 8 * (g + 1)], Hall[:, 8 * g], AJ1[:, 7], AJn[:, 7], PWi[:, 15, :], 0)
                                    for j in range(7):
                                        cmacc("pool", Hv[:, :, j], Cv, b4(AJ1[:, j]), b3(AJn[:, j]), b3(PWi[:, 8 + j, :]), G)
                                else:
                                    for k in range(nch):
                                        cmacc("pool", Hall[:, k + 1], Hall[:, k], A1[:], Ain[:], Aip[:], 0)
                                for ri in range(2):
                                    cp("pool", Hb[:, ri, :, :nch], Hall[:, 0:nch, ri, :].rearrange("p k r -> p r k"), ["Hall"], [hbk])
                                ts("pool", Hbn[:, :, :, :nch], Hb[:, :, :, :nch], -1.0, ALU.mult, [hbk], [hbnk])
                                if ti == NT // N - 1:
                                    ho_r, ho_i = (hrp, hip) if sq["kind"] == "p" else (hrs, his)
                                    S.dma(ho_r[l, b].rearrange("(pr g2) p -> (g2 p) pr", g2=2), Hall[:, nch, 0, :], reads=["Hall"], chan="hst", allow_slow_non_contiguous=True)
                                    S.dma(ho_i[l, b].rearrange("(pr g2) p -> (g2 p) pr", g2=2), Hall[:, nch, 1, :], reads=["Hall"], chan="hst", allow_slow_non_contiguous=True)
                                else:
                                    cp("pool", Hall[:, 0], Hall[:, nch], ["Hall"], ["Hall"])

                            def back(ti, bi_):
                                t0 = ti * N
                                uT = uTs[bi_]; gsT = gsTs[bi_]; Hb = Hbs[bi_]; Hbn = Hbns[bi_]
                                uk = 'uT%d' % bi_; gk = 'gsT%d' % bi_; hbk = 'Hb%d' % bi_; hbnk = 'Hbn%d' % bi_
                                uv = uT[:, :, :N].rearrange("p f (k t) -> p f k t", t=8)
                                chk(7)
                                for ft in range(4):
                                    yi = 0
                                    yk = "py%d" % yi
                                    for tau in range(8):
                                        for dl in range(tau + 1):
                                            mm(py[yi][:, tau, :nch], BD[:, ft, dl, :], uv[:, ft, :, tau - dl], dl == 0, False, ["BD", uk], [yk])
                                        for j in range(4):
                                            pr = ft * 4 + j
                                            for ri in range(2):
                                                if j < 3:
                                                    mm(py[yi][32 * j:32 * j + 32, tau, :nch], CA[:, tau, ri, pr, :], Hb[:, ri, pr, :nch], False, False, ["CA", hbk], [yk])
                                                else:
                                                    mm(py[yi][64:128, tau, :nch], CA[:, tau, ri, pr - 1:pr + 1, :].rearrange("p a b -> p (a b)"), Hb[:, ri, pr, :nch], False, False, ["CA", hbk], [yk])
                                                    mm(py[yi][64:96, tau, :nch], CA[:, tau, ri, pr - 1, :], Hbn[:, ri, pr, :nch], False, (ri == 1), ["CA", hbnk], [yk])
                                    yv = py[yi][:, :, :nch].rearrange("p t k -> p k t")
                                    ysk = "ys%d" % ft
                                    ysf = sy[:, ft, :N]
                                    ysv = ysf.rearrange("p (k t) -> p k t", t=8)
                                    gi = cnt["ef"] % 2; cnt["ef"] += 1
                                    tg_ = ef[gi]; tk = "ef%d" % gi
                                    stt(ysv, uv[:, ft], dvec[:, ft:ft + 1], yv, ALU.mult, ALU.add, [uk, "dvec%d" % l, yk], [ysk])
                                    tt("dve", tg_[:, :N], ysf, ysf, ALU.mult, [ysk], [tk])
                                    ts("dve", tg_[:, :N], tg_[:, :N], 0.044715, ALU.mult, [tk], [tk], s2=1.0, op1=ALU.add)
                                    tt("dve", tg_[:, :N], tg_[:, :N], ysf, ALU.mult, [tk, ysk], [tk])
                                    act(tg_[:, :N], tg_[:, :N], AF.Exp, [tk], [tk], scale=-1.5957691216057308)
                                    ts("dve", tg_[:, :N], tg_[:, :N], 1.0, ALU.add, [tk], [tk])
                                    recip(tg_[:, :N], tg_[:, :N], [tk], [tk])
                                    tt("dve", ygb[:, ft, :N], ysf, tg_[:, :N], ALU.mult, [tk, ysk], ["ygb"])
                                chk(8)
                                for mt in range(4):
                                    pj = cnt["pj"] % 2; cnt["pj"] += 1
                                    pjk = "ppj%d" % pj
                                    for kc in range(4):
                                        mm(ppj[pj][:, :N], wglub[:, kc, mt * 128:(mt + 1) * 128], ygb[:, kc, :N], kc == 0, kc == 3, ["wglub", "ygb"], [pjk])
                                    gi = cnt["ef"] % 2; cnt["ef"] += 1
                                    tg_ = ef[gi]; tk = "ef%d" % gi
                                    act(tg_[:, :N], ppj[pj][:, :N], AF.Exp, [pjk, "nbg%d" % l], [tk], scale=-1.0, bias=nbg[:, mt:mt + 1])
                                    ts("dve", tg_[:, :N], tg_[:, :N], 1.0, ALU.add, [tk], [tk])
                                    recip(tg_[:, :N], tg_[:, :N], [tk], [tk])
                                    tt("dve", sy[:, mt, :N], ygb[:, mt, :N], tg_[:, :N], ALU.mult, [tk, "ygb", "ys%d" % mt], ["sy"])
                                    tt("pool", sqb[:, mt, :N], sy[:, mt, :N], sy[:, mt, :N], ALU.mult, ["sy"], ["sqb"])
                                pj = cnt["pj"] % 2; cnt["pj"] += 1
                                pjk = "ppj%d" % pj
                                for kc in range(4):
                                    mm(ppj[pj][:, :N], onesb[:], sqb[:, kc, :N], kc == 0, kc == 3, ["onesb", "sqb"], [pjk])
                                act(rstd[:, :N], ppj[pj][:, :N], AF.Ln, [pjk], ["ysb"], scale=1.0 / 512, bias=EPS)
                                act(rstd[:, :N], rstd[:, :N], AF.Exp, ["ysb"], ["ysb"], scale=-0.5)
                                for ft in range(4):
                                    tt("dve", sy[:, ft, :N], sy[:, ft, :N], rstd[:, :N], ALU.mult, ["sy", "ysb"], ["sy"])
                                    stt(sob[:, ft, :N], sy[:, ft, :N], gssm[:, ft:ft + 1], gsT[:, ft, :N], ALU.mult, ALU.mult, ["sy", "gssm%d" % l, gk], ["sqb"])
                                S.dma(sq["soT"].rearrange("(f p) t -> p f t", p=128)[:, :, t0:t0 + N], sob[:, :, :N], reads=["sqb"], chan="st_sob")

                            ntile = NT // N
                            front(0, 0)
                            for ti in range(ntile):
                                if ti + 1 < ntile:
                                    front(ti + 1, (ti + 1) % 2)
                                back(ti, ti % 2)
                    S.barrier()

                chk(10)
                with contextlib.ExitStack() as L2:
                    woutb = sb(L2, "woutb%d" % l, [128, 8, D], BF16)
                    gatt = sb(L2, "gatt%d" % l, [128, 4], F32)
                    NKmax = max(T, PL + TS)
                    nblk_max = (NKmax + 127) // 128
                    kTs = sb(L2, "kTs%d" % l, [128, 4, nblk_max * 128], BF16)
                    vsb = sb(L2, "vsb%d" % l, [128, nblk_max, 512], BF16)
                    qTt = sb(L2, "qTt%d" % l, [128, 4, 512], BF16)
                    gaTt = sb(L2, "gaTt%d" % l, [128, 4, 512], BF16)
                    catT = sb(L2, "catT%d" % l, [128, 8, 512], BF16)
                    ckst = [sb(L2, "ckst%d_%d" % (l, i), [128, 512], BF16) for i in range(2)]
                    e_sb = [sb(L2, "e_sb%d_%d" % (l, i), [128, 512], F32) for i in range(3)]
                    sp_sb = [sb(L2, "sp_sb%d_%d" % (l, i), [128, 512], BF16) for i in range(3)]
                    w_sb = [sb(L2, "w_sb%d_%d" % (l, i), [128, 512], BF16) for i in range(3)]
                    spsum = [sb(L2, "spsum%d_%d" % (l, i), [128, 512], BF16) for i in range(4)]
                    att = sb(L2, "att%d" % l, [128, 4, 512], F32)
                    asq = sb(L2, "asq%d" % l, [128, 4, 512], BF16)
                    rstd2 = sb(L2, "rstd2%d" % l, [128, 512], F32)
                    xt2 = [sb(L2, "xt2%d_%d" % (l, i), [128, D], F32) for i in range(2)]
                    xn = [sb(L2, "xn%d_%d" % (l, i), [128, D], F32) for i in range(2)]
                    junk2 = sb(L2, "junk2%d" % l, [128, D], BF16)
                    ssq2 = sb(L2, "ssq2%d" % l, [128, 4], F32)
                    pA = [L2.enter_context(nc.psum_tensor("pA%d_%d" % (l, i), [128, 512], F32)) for i in range(5)]
                    pAtt = [L2.enter_context(nc.psum_tensor("pAtt%d_%d" % (l, i), [128, 512], F32)) for i in range(1)]
                    pO = [L2.enter_context(nc.psum_tensor("pO%d_%d" % (l, i), [128, 512], F32)) for i in range(2)]
                    ptk = pO[0][:].bitcast(BF16)
                    for kc in range(8):
                        S.dma(woutb[:, kc, :], w_out[l, kc * 128:(kc + 1) * 128, :], writes=["woutb"], queue="pool")
                    S.dma(gatt[:], g_att[l].rearrange("(c p) -> p c", p=128), writes=["gatt"], allow_slow_non_contiguous=True)
                    c2 = dict(x=0, o=0, ck=0, blk=0)
                    for si, sq in enumerate(seqs):
                        NT, N, b, past = sq["NT"], sq["N"], sq["b"], sq["past"]
                        R = min(128, N); nsub = N // R
                        xsrc = sq["xin"] if l == 0 else (sq["xa"] if l % 2 == 1 else sq["xb"])
                        xdst = sq["xa"] if l % 2 == 0 else sq["xb"]
                        npast = past // 128
                        for pb in range(npast):
                            ci = c2["ck"] % 2; c2["ck"] += 1
                            ckk = "ckst%d" % ci
                            S.dma(ckst[ci][:], ck[l, b, pb * 128:(pb + 1) * 128, :], writes=[ckk], queue="pool")
                            for hp in range(4):
                                tp(ptk[:, hp * 128:(hp + 1) * 128], ckst[ci][:, hp * 128:(hp + 1) * 128], identb[:], [ckk, "identb"], ["pO0"])
                            cp("dve", kTs[:, :, pb * 128:(pb + 1) * 128], ptk[:, 0:512].rearrange("p (h s) -> p h s", h=4), ["pO0"], ["kTs"])
                        if npast:
                            S.dma(vsb[:, 0:npast, :], cv[l, b].rearrange("(n p) f -> p n f", p=128), writes=["vsb"], queue="pool")
                        S.dma(kTs[:, :, past:past + NT], sq["kT"].rearrange("(h p) t -> p h t", p=128), writes=["kTs"])
                        if NT >= 128:
                            S.dma(vsb[:, npast:npast + NT // 128, :], sq["vv"].rearrange("(n p) f -> p n f", p=128), writes=["vsb"])
                        else:
                            S.dma(vsb[:NT, npast, :], sq["vv"], writes=["vsb"])
                        for ti in range(NT // N):
                            t0 = ti * N
                            S.dma(qTt[:, :, :N], sq["qT"].rearrange("(h p) t -> p h t", p=128)[:, :, t0:t0 + N], writes=["qTt"])
                            S.dma(gaTt[:, :, :N], sq["gaT"].rearrange("(h p) t -> p h t", p=128)[:, :, t0:t0 + N], writes=["gaTt"])
                            S.dma(catT[:, 4:8, :N], sq["soT"].rearrange("(h p) t -> p h t", p=128)[:, :, t0:t0 + N], writes=["catT"])
                            chk(11)
                            blocks = []
                            if sq["kind"] == "p":
                                for j in reversed(range(nsub)):
                                    blocks.append((t0 + j * 128, 128, (t0 + j * 128) // 128, masks[:, j, :N]))
                                for kb in reversed(range(t0 // 128)):
                                    blocks.append((kb * 128, 128, kb, None))
                            else:
                                blocks.append((past, NT, npast, masks_s[:, :]))
                                for kb in reversed(range(npast)):
                                    blocks.append((kb * 128, 128, kb, None))
                            work = [(h, bi) for h in range(8) for bi in range(len(blocks))]

                            def info(idx):
                                h, bi = work[idx]
                                s0, Rk, vb, mk = blocks[bi]
                                return h, bi, s0, Rk, vb, mk, h // 2, 64 * (h % 2)

                            def sA(idx):
                                h, bi, s0, Rk, vb, mk, hp, base = info(idx)
                                sl = idx % 5
                                mm(pA[sl][:Rk, :N], kTs[base:base + 64, hp, s0:s0 + Rk], qTt[base:base + 64, hp, :N], True, False, ["kTs", "qTt"], ["pA%d" % sl])

                            def sB(idx):
                                h, bi, s0, Rk, vb, mk, hp, base = info(idx)
                                sl = idx % 5
                                act(e_sb[idx % 3][:Rk, :N], pA[sl][:Rk, :N], AF.Exp, ["pA%d" % sl], ["e_sb%d" % (idx % 3)])

                            def sC(idx):
                                h, bi, s0, Rk, vb, mk, hp, base = info(idx)
                                spk = "sp_sb%d" % (idx % 3)
                                act(sp_sb[idx % 3][:Rk, :N], e_sb[idx % 3][:Rk, :N], AF.Ln, ["e_sb%d" % (idx % 3)], [spk], bias=1.0)
                                if mk is not None:
                                    tt("pool", sp_sb[idx % 3][:Rk, :N], sp_sb[idx % 3][:Rk, :N], mk, ALU.mult, [spk, "masks"], [spk])

                            def sD(idx):
                                h, bi, s0, Rk, vb, mk, hp, base = info(idx)
                                sl = idx % 5
                                Ak, spk = "pA%d" % sl, "sp_sb%d" % (idx % 3)
                                spt = sp_sb[idx % 3]
                                first, lastb = (bi == 0), (bi == len(blocks) - 1)
                                mm(pA[sl][:Rk, :N], negU[:Rk, :Rk], spt[:Rk, :N], False, first, [spk, "negU"], [Ak])
                                if not first:
                                    prv = "spsum%d" % ((idx - 1) % 4)
                                    mm(pA[sl][:Rk, :N], negones[:, :Rk], spsum[(idx - 1) % 4][:, :N], False, True, [prv, "negones"], [Ak])
                                if not lastb:
                                    cur, prv = "spsum%d" % (idx % 4), "spsum%d" % ((idx - 1) % 4)
                                    if first:
                                        if Rk < 128:
                                            mset("dve", spsum[idx % 4][:, :N], 0.0, [cur])
                                        cp("dve", spsum[idx % 4][:Rk, :N], spt[:Rk, :N], [spk], [cur])
                                    else:
                                        tt("dve", spsum[idx % 4][:, :N], spsum[(idx - 1) % 4][:, :N], spt[:, :N], ALU.add, [spk, prv], [cur])

                            def sE(idx):
                                h, bi, s0, Rk, vb, mk, hp, base = info(idx)
                                sl = idx % 5
                                wk = "w_sb%d" % (idx % 3)
                                act(w_sb[idx % 3][:Rk, :N], pA[sl][:Rk, :N], AF.Exp, ["pA%d" % sl], [wk])
                                if mk is not None:
                                    tt("pool", w_sb[idx % 3][:Rk, :N], w_sb[idx % 3][:Rk, :N], mk, ALU.mult, [wk, "masks"], [wk])

                            def sF(idx):
                                h, bi, s0, Rk, vb, mk, hp, base = info(idx)
                                wk = "w_sb%d" % (idx % 3)
                                atk = "pAtt_%d" % (h % 2)
                                first, lastb = (bi == 0), (bi == len(blocks) - 1)
                                mm(pAtt[0][base:base + 64, :N], vsb[:Rk, vb, h * 64:(h + 1) * 64], w_sb[idx % 3][:Rk, :N], first, lastb, [wk, "vsb"], [atk])
                                if lastb:
                                    cp("dve", att[base:base + 64, hp, :N], pAtt[0][base:base + 64, :N], [atk], ["att"])

                            nw = len(work)
                            stages = [sA, sB, sC, sD, sE, sF]
                            for it in range(nw + 5):
                                for d, fn in enumerate(stages):
                                    if 0 <= it - d < nw:
                                        fn(it - d)
                            chk(12)
                            for hp in range(4):
                                tt("pool", asq[:, hp, :N], att[:, hp, :N], att[:, hp, :N], ALU.mult, ["att"], ["asq"])
                            for hp in range(4):
                                mm(pO[1][:, :N], onesb[:], asq[:, hp, :N], hp == 0, hp == 3, ["onesb", "asq"], ["pO1"])
                            act(rstd2[:, :N], pO[1][:, :N], AF.Ln, ["pO1"], ["rstd2"], scale=1.0 / 512, bias=EPS)
                            act(rstd2[:, :N], rstd2[:, :N], AF.Exp, ["rstd2"], ["rstd2"], scale=-0.5)
                            for hp in range(4):
                                tt("dve", att[:, hp, :N], att[:, hp, :N], rstd2[:, :N], ALU.mult, ["att", "rstd2"], ["att"])
                                stt(catT[:, hp, :N], att[:, hp, :N], gatt[:, hp:hp + 1], gaTt[:, hp, :N], ALU.mult, ALU.mult, ["att", "gatt", "gaTt"], ["catT"])
                            chk(13)
                            for sub in range(nsub):
                                xi = c2["x"] % 2; c2["x"] += 1
                                xk, nk = "xt2%d" % xi, "xn%d" % xi
                                r0 = t0 + sub * R
                                S.dma(xt2[xi][:R, :], xsrc[r0:r0 + R, :], writes=[xk])
                                for half in range(2):
                                    oi = c2["o"] % 2; c2["o"] += 1
                                    ok = "pO%d" % oi
                                    for kc in range(8):
                                        mm(pO[oi][:R, :], catT[:, kc, sub * R:sub * R + R], woutb[:, kc, half * 512:(half + 1) * 512], kc == 0, kc == 7, ["catT", "woutb"], [ok])
                                    tt("dve", xn[xi][:R, half * 512:(half + 1) * 512], pO[oi][:R, :], xt2[xi][:R, half * 512:(half + 1) * 512], ALU.add, [ok, xk], [nk])
                                if not last:
                                    S.dma(xdst[r0:r0 + R, :], xn[xi][:R, :], reads=[nk], chan="st_" + nk)
                                else:
                                    act(junk2[:R, :], xn[xi][:R, :], AF.Square, [nk], ["junk2"])
                                    S.op("dve", lambda e: e.tensor_reduce(out=ssq2[:R, 0:1], in_=junk2[:R, :], axis=mybir.AxisListType.X, op=ALU.add), ["junk2"], ["ssq2"])
                                    act(ssq2[:R, 1:2], ssq2[:R, 0:1], AF.Ln, ["ssq2"], ["ssq2"], scale=1.0 / D, bias=EPS)
                                    act(ssq2[:R, 2:3], ssq2[:R, 1:2], AF.Exp, ["ssq2"], ["ssq2"], scale=-0.5)
                                    stt(xt2[xi][:R, :], xn[xi][:R, :], ssq2[:R, 2:3], fgbc[:R, :], ALU.mult, ALU.mult, [nk, "ssq2", "fgbc"], [xk])
                                    S.dma(sq["yout"][r0:r0 + R, :], xt2[xi][:R, :], reads=[xk], chan="st_" + xk)
                    S.barrier()
        except StopBuild:
            pass
        S._final = True
        S.barrier()
    return nc


_CACHE = {}


def _run(inputs, T, TS, PL, DEPTH):
    key = (T, TS, PL, DEPTH)
    if key not in _CACHE:
        _CACHE[key] = build(T, TS, PL, DEPTH)
    nc = _CACHE[key]
    f = lambda a: np.ascontiguousarray(np.asarray(a, dtype=np.float32))
    in_maps = []
    for c in range(8):
        sl = slice(2 * c, 2 * c + 2)
        m = {
            "xp": f(inputs["x_prompt"][sl]), "xs": f(inputs["x_sample"][sl]),
            "ck": f(np.asarray(inputs["cache_k"])[:, sl].reshape(DEPTH, 2, PL, 512)),
            "cv": f(np.asarray(inputs["cache_v"])[:, sl].reshape(DEPTH, 2, PL, 512)),
            "sre": f(np.asarray(inputs["state_ssm_re"])[:, sl]), "sim": f(np.asarray(inputs["state_ssm_im"])[:, sl]),
            "ln_g": f(inputs["ln_g"]), "w_in": f(inputs["w_in"]),
            "a_re": f(inputs["ssm_a_re"]), "a_im": f(inputs["ssm_a_im"]), "log_dt": f(inputs["ssm_log_dt"]),
            "b_re": f(inputs["ssm_b_re"]), "b_im": f(inputs["ssm_b_im"]),
            "c_re": f(inputs["ssm_c_re"]), "c_im": f(inputs["ssm_c_im"]),
            "ssm_d": f(inputs["ssm_d"]), "w_glu": f(inputs["w_glu"]), "b_glu": f(inputs["b_glu"]),
            "g_att": f(inputs["g_att"]), "g_ssm": f(inputs["g_ssm"]), "w_out": f(inputs["w_out"]),
            "final_g": f(inputs["final_g"]),
        }
        in_maps.append(m)
    res = run_bass_kernel_spmd(nc, in_maps, core_ids=list(range(8)))
    R = res.results
    cat0 = lambda k: np.concatenate([r[k] for r in R], axis=0)
    cat1 = lambda k: np.concatenate([r[k] for r in R], axis=1)
    y_p = cat0("yp"); y_s = cat0("ys")
    k_p = cat1("kp").reshape(DEPTH, 16, T, 8, 64); v_p = cat1("vp").reshape(DEPTH, 16, T, 8, 64)
    k_s = cat1("ks").reshape(DEPTH, 16, TS, 8, 64); v_s = cat1("vs").reshape(DEPTH, 16, TS, 8, 64)
    return (y_p, y_s, k_p, v_p, cat1("hrp"), cat1("hip"), k_s, v_s, cat1("hrs"), cat1("his"))


def kernel(**inputs):
    T = int(np.shape(inputs["x_prompt"])[1]); TS = int(np.shape(inputs["x_sample"])[1])
    PL = int(np.shape(inputs["cache_k"])[2]); DEPTH = int(np.shape(inputs["w_in"])[0])
    return _run(inputs, T, TS, PL, DEPTH)
```

```python
import contextlib
import math
import numpy as np
import concourse.bass as bass
import concourse.mybir as mybir
from concourse.bass_utils import run_bass_kernel_spmd

F32 = mybir.dt.float32
BF16 = mybir.dt.bfloat16
I32 = mybir.dt.int32
AF = mybir.ActivationFunctionType
ALU = mybir.AluOpType
EPS = 1e-6
TWO_PI = 2.0 * math.pi


class Sched:
    def __init__(self, nc, stack):
        self.nc = nc
        self.stack = stack
        self.eng = {"pe": nc.tensor, "act": nc.scalar, "dve": nc.vector,
                    "pool": nc.gpsimd, "sp": nc.sync}
        self.sem = {}
        self.cnt = {}
        for e in ("pe", "act", "dve", "pool"):
            self.sem[e] = stack.enter_context(nc.semaphore("s_" + e))
            self.cnt[e] = 0
        self.waited = {e: {} for e in self.eng}
        self.lastw = {}
        self.readers = {}
        self.nchan = 0
        self.stopped = False

    def chan(self, name):
        if name not in self.sem:
            self.sem[name] = self.stack.enter_context(self.nc.semaphore("d%d" % self.nchan))
            self.nchan += 1
            self.cnt[name] = 0
        return name

    def _deps(self, engine, reads, writes):
        deps = {}

        def add(d, raw):
            if d is None:
                return
            src, c = d
            if src == engine and not raw:
                return
            if src == "pe" and engine == "pe":
                return
            if deps.get(src, 0) < c:
                deps[src] = c

        for r in reads:
            add(self.lastw.get(r), True)
        for w in writes:
            add(self.lastw.get(w), True)
            for src, c in self.readers.get(w, {}).items():
                add((src, c), False)
        return deps

    def _emit_waits(self, engine, deps):
        e = self.eng[engine]
        wd = self.waited[engine]
        for src, c in deps.items():
            if wd.get(src, 0) >= c:
                continue
            e.wait_ge(self.sem[src], c)
            wd[src] = c

    def op(self, engine, fn, reads=(), writes=()):
        if self.stopped:
            return None
        deps = self._deps(engine, reads, writes)
        self._emit_waits(engine, deps)
        ins = fn(self.eng[engine])
        ins.then_inc(self.sem[engine], 1)
        self.cnt[engine] += 1
        c = self.cnt[engine]
        for r in reads:
            self.readers.setdefault(r, {})[engine] = c
        for w in writes:
            self.lastw[w] = (engine, c)
            self.readers[w] = {}
        return ins

    def dma(self, out, in_, reads=(), writes=(), chan=None, queue="sp", **kw):
        if self.stopped:
            return None
        if chan is None:
            chan = "c_" + str(writes[0] if writes else reads[0])
        self.chan(chan)
        deps = self._deps("__dma__", reads, writes)
        self._emit_waits(queue, deps)
        ins = self.eng[queue].dma_start(out=out, in_=in_, **kw)
        ins.then_inc(self.sem[chan], 16)
        self.cnt[chan] += 16
        c = self.cnt[chan]
        for r in reads:
            self.readers.setdefault(r, {})[chan] = c
        for w in writes:
            self.lastw[w] = (chan, c)
            self.readers[w] = {}
        return ins

    def barrier(self):
        if self.stopped and getattr(self, "_final", False) is False:
            return
        allsrc = [s for s in self.cnt if self.cnt[s] > 0]
        for e in ("pe", "act", "dve", "pool", "sp"):
            self._emit_waits(e, {s: self.cnt[s] for s in allsrc if s != e})


class StopBuild(Exception):
    pass


def build(T, TS, PL, DEPTH):
    import os
    KSTOP = int(os.environ.get("KSTOP", "99"))

    def chk(level):
        if KSTOP <= level:
            S.stopped = True

    nc = bass.Bass("TRN2", target_bir_lowering=False)
    D = 1024

    def din(name, shape):
        return nc.dram_tensor(name, shape, F32, kind="ExternalInput").ap()

    def dout(name, shape):
        return nc.dram_tensor(name, shape, F32, kind="ExternalOutput").ap()

    def dscr(name, shape, dt):
        return nc.dram_tensor(name, shape, dt, kind="Internal").ap()

    xp = din("xp", [2, T, D]); xs = din("xs", [2, TS, D])
    ck = din("ck", [DEPTH, 2, PL, 512]); cv = din("cv", [DEPTH, 2, PL, 512])
    sre = din("sre", [DEPTH, 2, 32, 64]); sim = din("sim", [DEPTH, 2, 32, 64])
    ln_g = din("ln_g", [DEPTH, D]); w_in = din("w_in", [DEPTH, D, 3072])
    a_re = din("a_re", [DEPTH, 32, 64]); a_im = din("a_im", [DEPTH, 32, 64])
    log_dt = din("log_dt", [DEPTH, 32])
    b_re = din("b_re", [DEPTH, 32, 64, 16]); b_im = din("b_im", [DEPTH, 32, 64, 16])
    c_re = din("c_re", [DEPTH, 32, 16, 64]); c_im = din("c_im", [DEPTH, 32, 16, 64])
    ssm_d = din("ssm_d", [DEPTH, 512]); w_glu = din("w_glu", [DEPTH, 512, 512])
    b_glu = din("b_glu", [DEPTH, 512]); g_att = din("g_att", [DEPTH, 512])
    g_ssm = din("g_ssm", [DEPTH, 512]); w_out = din("w_out", [DEPTH, D, D])
    final_g = din("final_g", [D])
    yp = dout("yp", [2, T, D]); ys = dout("ys", [2, TS, D])
    kp = dout("kp", [DEPTH, 2, T, 512]); vp = dout("vp", [DEPTH, 2, T, 512])
    hrp = dout("hrp", [DEPTH, 2, 32, 64]); hip = dout("hip", [DEPTH, 2, 32, 64])
    ks = dout("ks", [DEPTH, 2, TS, 512]); vs = dout("vs", [DEPTH, 2, TS, 512])
    hrs = dout("hrs", [DEPTH, 2, 32, 64]); his = dout("his", [DEPTH, 2, 32, 64])

    seqs = []
    for b in range(2):
        seqs.append(dict(kind="p", b=b, NT=T, xin=xp[b], yout=yp[b], past=0))
    for b in range(2):
        seqs.append(dict(kind="s", b=b, NT=TS, xin=xs[b], yout=ys[b], past=PL))
    for si, sq in enumerate(seqs):
        NT = sq["NT"]
        sq["xa"] = dscr("xa%d" % si, [NT, D], F32)
        sq["xb"] = dscr("xb%d" % si, [NT, D], F32)
        for nm in ("qT", "kT", "gaT", "soT"):
            sq[nm] = dscr("%s%d" % (nm, si), [512, NT], BF16)
        sq["vv"] = dscr("vv%d" % si, [NT, 512], BF16)
        sq["N"] = min(512, NT)

    with contextlib.ExitStack() as top:
        S = Sched(nc, top)

        def sb(stack, name, shape, dt):
            return stack.enter_context(nc.sbuf_tensor(name, shape, dt))

        def mm(out, lhsT, rhs, start, stop, r, w):
            S.op("pe", lambda e: e.matmul(out, lhsT, rhs, start=start, stop=stop), r, w)

        def tp(out, in_, ident, r, w):
            S.op("pe", lambda e: e.transpose(out, in_, ident), r, w)

        def act(out, in_, func, r, w, scale=1.0, bias=0.0, accum=None):
            if accum is None:
                S.op("act", lambda e: e.activation(out=out, in_=in_, func=func, bias=bias, scale=scale), r, w)
            else:
                S.op("act", lambda e: e.activation(out=out, in_=in_, func=func, bias=bias, scale=scale, accum_out=accum), r, w)

        def tt(eng, out, in0, in1, op, r, w):
            S.op(eng, lambda e: e.tensor_tensor(out=out, in0=in0, in1=in1, op=op), r, w)

        def ts(eng, out, in0, s1, op0, r, w, s2=None, op1=None):
            if op1 is None:
                S.op(eng, lambda e: e.tensor_scalar(out=out, in0=in0, scalar1=s1, scalar2=None, op0=op0), r, w)
            else:
                S.op(eng, lambda e: e.tensor_scalar(out=out, in0=in0, scalar1=s1, scalar2=s2, op0=op0, op1=op1), r, w)

        def stt(out, in0, scalar, in1, op0, op1, r, w):
            S.op("dve", lambda e: e.scalar_tensor_tensor(out=out, in0=in0, scalar=scalar, in1=in1, op0=op0, op1=op1), r, w)

        def cp(eng, out, in_, r, w):
            S.op(eng, lambda e: e.tensor_copy(out=out, in_=in_), r, w)

        def recip(out, in_, r, w):
            S.op("dve", lambda e: e.reciprocal(out=out, in_=in_), r, w)

        def mset(eng, ap, val, w):
            S.op(eng, lambda e: e.memset(ap, val), (), w)

        identb = sb(top, "identb", [128, 128], BF16)
        identf = sb(top, "identf", [128, 128], F32)
        negU = sb(top, "negU", [128, 128], BF16)
        negones = sb(top, "negones", [128, 128], BF16)
        onesb = sb(top, "onesb", [128, 128], BF16)
        mask32 = sb(top, "mask32", [128, 128], F32)
        fgbc = sb(top, "fgbc", [128, D], F32)
        NP = seqs[0]["N"]
        ndiag = NP // 128
        masks = sb(top, "masks", [128, ndiag, NP], BF16)
        masks_s = sb(top, "masks_s", [TS, TS], BF16)
        mset("pool", identb[:], 0.0, ["identb"])
        S.op("pool", lambda e: e.affine_select(out=identb[:], in_=identb[:], pattern=[[-1, 128]], compare_op=ALU.not_equal, fill=1.0, base=0, channel_multiplier=1), ["identb"], ["identb"])
        mset("pool", identf[:], 0.0, ["identf"])
        S.op("pool", lambda e: e.affine_select(out=identf[:], in_=identf[:], pattern=[[-1, 128]], compare_op=ALU.not_equal, fill=1.0, base=0, channel_multiplier=1), ["identf"], ["identf"])
        mset("pool", negU[:], -1.0, ["negU"])
        S.op("pool", lambda e: e.affine_select(out=negU[:], in_=negU[:], pattern=[[-1, 128]], compare_op=ALU.is_ge, fill=0.0, base=0, channel_multiplier=1), ["negU"], ["negU"])
        mset("pool", negones[:], -1.0, ["negones"])
        mset("pool", onesb[:], 1.0, ["onesb"])
        mset("pool", mask32[:], 0.0, ["mask32"])
        for j in range(4):
            mset("pool", mask32[32 * j:32 * j + 32, 32 * j:32 * j + 32], 1.0, ["mask32"])
        mset("pool", masks[:], 1.0, ["masks"])
        for j in range(ndiag):
            S.op("pool", lambda e: e.affine_select(out=masks[:, j, :], in_=masks[:, j, :], pattern=[[1, NP]], compare_op=ALU.is_gt, fill=0.0, base=-128 * j, channel_multiplier=-1), ["masks"], ["masks"])
        mset("pool", masks_s[:], 1.0, ["masks_s"])
        S.op("pool", lambda e: e.affine_select(out=masks_s[:], in_=masks_s[:], pattern=[[1, TS]], compare_op=ALU.is_gt, fill=0.0, base=0, channel_multiplier=-1), ["masks_s"], ["masks_s"])
        S.dma(fgbc[:], final_g.partition_broadcast(128), writes=["fgbc"])

        try:
            chk(0)
            for l in range(DEPTH):
                last = (l == DEPTH - 1)
                with contextlib.ExitStack() as L1:
                    winb = sb(L1, "winb%d" % l, [128, 8, 3072], BF16)
                    wglub = sb(L1, "wglub%d" % l, [128, 4, 512], BF16)
                    gbc = sb(L1, "gbc%d" % l, [128, D], F32)
                    dvec = sb(L1, "dvec%d" % l, [128, 4], F32)
                    nbg = sb(L1, "nbg%d" % l, [128, 4], F32)
                    gssm = sb(L1, "gssm%d" % l, [128, 4], F32)
                    PWr = sb(L1, "PWr%d" % l, [128, 16, 16], F32)
                    PWi = sb(L1, "PWi%d" % l, [128, 16, 16], F32)
                    A1 = sb(L1, "A1%d" % l, [128, 2, 16], F32)
                    Aip = sb(L1, "Aip%d" % l, [128, 16], F32)
                    Ain = sb(L1, "Ain%d" % l, [128, 16], F32)
                    AJ1 = sb(L1, "AJ1%d" % l, [128, 8, 2, 16], F32)
                    AJn = sb(L1, "AJn%d" % l, [128, 8, 16], F32)
                    BD = sb(L1, "BD%d" % l, [128, 4, 8, 128], BF16)
                    BL = sb(L1, "BL%d" % l, [128, 4, 8, 2, 128], BF16)
                    CA = sb(L1, "CA%d" % l, [128, 8, 2, 16, 32], BF16)
                    for kc in range(8):
                        S.dma(winb[:, kc, :], w_in[l, kc * 128:(kc + 1) * 128, :], writes=["winb"], queue="pool")
                    S.dma(wglub[:], w_glu[l].rearrange("(kc p) n -> p kc n", p=128), writes=["wglub"], queue="pool")
                    S.dma(gbc[:], ln_g[l].partition_broadcast(128), writes=["gbc"])
                    for tl, src, nm in ((dvec, ssm_d, "dvec%d" % l), (nbg, b_glu, "nbg%d" % l), (gssm, g_ssm, "gssm%d" % l)):
                        S.dma(tl[:], src[l].rearrange("(c p) -> p c", p=128), writes=[nm], allow_slow_non_contiguous=True)
                    ts("dve", nbg[:], nbg[:], -1.0, ALU.mult, ["nbg%d" % l], ["nbg%d" % l])

                    with contextlib.ExitStack() as SU:
                        k_ = "su"
                        aqr = sb(SU, "aqr%d" % l, [128, 16], F32); aqi = sb(SU, "aqi%d" % l, [128, 16], F32)
                        dtq = sb(SU, "dtq%d" % l, [128, 16], F32)
                        adt = sb(SU, "adt%d" % l, [128, 16], F32); ang = sb(SU, "ang%d" % l, [128, 16], F32)
                        mag = sb(SU, "mag%d" % l, [128, 16, 16], F32)
                        ph = sb(SU, "ph%d" % l, [128, 2, 16, 16], F32)
                        pht = sb(SU, "pht%d" % l, [128, 2, 16, 16], F32)
                        phi = sb(SU, "phi%d" % l, [128, 2, 16, 16], I32)
                        sc = sb(SU, "sc%d" % l, [128, 2, 16, 16], F32)
                        t1 = sb(SU, "t1%d" % l, [128, 16], F32); t2 = sb(SU, "t2%d" % l, [128, 16], F32)
                        t3 = sb(SU, "t3%d" % l, [128, 16], F32)
                        fr = sb(SU, "fr%d" % l, [128, 16], F32); fi = sb(SU, "fi%d" % l, [128, 16], F32)
                        Bq0r = sb(SU, "Bq0r%d" % l, [128, 16, 32], F32); Bq0i = sb(SU, "Bq0i%d" % l, [128, 16, 32], F32)
                        Bbr = sb(SU, "Bbr%d" % l, [128, 16, 32], F32); Bbi = sb(SU, "Bbi%d" % l, [128, 16, 32], F32)
                        nBbi = sb(SU, "nBbi%d" % l, [128, 16, 32], F32)
                        u1 = sb(SU, "u1%d" % l, [128, 16, 32], F32); u2 = sb(SU, "u2%d" % l, [128, 16, 32], F32)
                        Yr = sb(SU, "Yr%d" % l, [128, 16, 32], F32); Yi = sb(SU, "Yi%d" % l, [128, 16, 32], F32)
                        Zr = sb(SU, "Zr%d" % l, [32, 16, 128], F32); Zi = sb(SU, "Zi%d" % l, [32, 16, 128], F32)
                        CTr = sb(SU, "CTr%d" % l, [128, 16, 32], F32); CTi = sb(SU, "CTi%d" % l, [128, 16, 32], F32)
                        Xr = sb(SU, "Xr%d" % l, [128, 9, 16, 32], F32); Xi = sb(SU, "Xi%d" % l, [128, 9, 16, 32], F32)
                        pss = SU.enter_context(nc.psum_tensor("pss%d" % l, [128, 4, 128], F32))
                        psc = SU.enter_context(nc.psum_tensor("psc%d" % l, [128, 16, 32], F32))
                        K = [k_]
                        S.dma(aqr[:], a_re[l].rearrange("(pr g2) p -> (g2 p) pr", g2=2), writes=K, allow_slow_non_contiguous=True)
                        S.dma(aqi[:], a_im[l].rearrange("(pr g2) p -> (g2 p) pr", g2=2), writes=K, allow_slow_non_contiguous=True)
                        mset("dve", Bq0r[:], 0.0, K); mset("dve", Bq0i[:], 0.0, K)
                        mset("dve", Zr[:], 0.0, K); mset("dve", Zi[:], 0.0, K)
                        for g2 in range(2):
                            S.dma(dtq[g2 * 64:(g2 + 1) * 64, :], log_dt[l].rearrange("(pr g2) -> g2 pr", g2=2)[g2].partition_broadcast(64), reads=K, writes=K, allow_slow_non_contiguous=True)
                            S.dma(Bq0r[g2 * 64:(g2 + 1) * 64, :, g2 * 16:(g2 + 1) * 16], b_re[l].rearrange("(pr g2) p c -> g2 p pr c", g2=2)[g2], reads=K, writes=K)
                            S.dma(Bq0i[g2 * 64:(g2 + 1) * 64, :, g2 * 16:(g2 + 1) * 16], b_im[l].rearrange("(pr g2) p c -> g2 p pr c", g2=2)[g2], reads=K, writes=K)
                            S.dma(Zr[g2 * 16:(g2 + 1) * 16, :, g2 * 64:(g2 + 1) * 64], c_re[l].rearrange("(pr g2) c p -> g2 c pr p", g2=2)[g2], reads=K, writes=K)
                            S.dma(Zi[g2 * 16:(g2 + 1) * 16, :, g2 * 64:(g2 + 1) * 64], c_im[l].rearrange("(pr g2) c p -> g2 c pr p", g2=2)[g2], reads=K, writes=K)
                        act(dtq[:], dtq[:], AF.Exp, K, K)
                        tt("dve", adt[:], aqr[:], dtq[:], ALU.mult, K, K)
                        tt("dve", ang[:], aqi[:], dtq[:], ALU.mult, K, K)
                        for di, dl in enumerate(list(range(9)) + [16, 24, 32, 40, 48, 56, 64]):
                            act(mag[:, di, :], adt[:], AF.Exp, K, K, scale=float(dl))
                            ts("dve", ph[:, 0, di, :], ang[:], float(dl), ALU.mult, K, K)
                            ts("dve", ph[:, 1, di, :], ang[:], float(dl), ALU.mult, K, K, s2=math.pi / 2, op1=ALU.add)
                        ts("dve", pht[:], ph[:], 1.0 / TWO_PI, ALU.mult, K, K)
                        cp("dve", phi[:], pht[:], K, K)
                        cp("dve", pht[:], phi[:], K, K)
                        stt(ph[:], pht[:], -TWO_PI, ph[:], ALU.mult, ALU.add, K, K)
                        ts("dve", ph[:], ph[:], -math.pi, ALU.max, K, K, s2=math.pi, op1=ALU.min)
                        act(sc[:], ph[:], AF.Sin, K, K)
                        chk(1)
                        tt("dve", PWi[:], mag[:], sc[:, 0], ALU.mult, K, ["PW"])
                        tt("dve", PWr[:], mag[:], sc[:, 1], ALU.mult, K, ["PW"])
                        cp("dve", A1[:, 0, :], PWr[:, 8, :], ["PW"], ["A8"]); cp("dve", A1[:, 1, :], PWr[:, 8, :], ["PW"], ["A8"])
                        cp("dve", Aip[:], PWi[:, 8, :], ["PW"], ["A8"])
                        ts("dve", Ain[:], PWi[:, 8, :], -1.0, ALU.mult, ["PW"], ["A8"])
                        cp("dve", AJ1[:, :, 0, :], PWr[:, 8:16, :], ["PW"], ["A8"]); cp("dve", AJ1[:, :, 1, :], PWr[:, 8:16, :], ["PW"], ["A8"])
                        ts("dve", AJn[:], PWi[:, 8:16, :], -1.0, ALU.mult, ["PW"], ["A8"])
                        tt("dve", t1[:], aqr[:], aqr[:], ALU.mult, K, K)
                        tt("dve", t2[:], aqi[:], aqi[:], ALU.mult, K, K)
                        tt("dve", t1[:], t1[:], t2[:], ALU.add, K, K)
                        recip(t1[:], t1[:], K, K)
                        ts("dve", t2[:], PWr[:, 1, :], -1.0, ALU.add, K + ["PW"], K)
                        tt("dve", fr[:], t2[:], aqr[:], ALU.mult, K, K)
                        tt("dve", t3[:], PWi[:, 1, :], aqi[:], ALU.mult, K + ["PW"], K)
                        tt("dve", fr[:], fr[:], t3[:], ALU.add, K, K)
                        tt("dve", fr[:], fr[:], t1[:], ALU.mult, K, K)
                        tt("dve", fi[:], PWi[:, 1, :], aqr[:], ALU.mult, K + ["PW"], K)
                        tt("dve", t3[:], t2[:], aqi[:], ALU.mult, K, K)
                        tt("dve", fi[:], fi[:], t3[:], ALU.subtract, K, K)
                        tt("dve", fi[:], fi[:], t1[:], ALU.mult, K, K)
                        bc = lambda ap: ap.unsqueeze(2).to_broadcast([128, 16, 32])
                        tt("dve", u1[:], Bq0r[:], bc(fr[:]), ALU.mult, K, K)
                        tt("dve", u2[:], Bq0i[:], bc(fi[:]), ALU.mult, K, K)
                        tt("dve", Bbr[:], u1[:], u2[:], ALU.subtract, K, K)
                        tt("dve", u1[:], Bq0i[:], bc(fr[:]), ALU.mult, K, K)
                        tt("dve", u2[:], Bq0r[:], bc(fi[:]), ALU.mult, K, K)
                        tt("dve", Bbi[:], u1[:], u2[:], ALU.add, K, K)
                        ts("dve", nBbi[:], Bbi[:], -1.0, ALU.mult, K, K)
                        for (Z, CT) in ((Zr, CTr), (Zi, CTi)):
                            for pr in range(16):
                                tp(psc[:, pr, :], Z[:, pr, :], identf[0:32, 0:32], K + ["identf"], ["psc"])
                            cp("dve", CT[:], psc[:], ["psc"], K)
                        for dl in range(9):
                            pr_ = bc(PWr[:, dl, :]); pi_ = bc(PWi[:, dl, :])
                            tt("dve", u1[:], CTr[:], pr_, ALU.mult, K + ["PW"], K)
                            tt("dve", u2[:], CTi[:], pi_, ALU.mult, K + ["PW"], K)
                            tt("dve", Xr[:, dl], u1[:], u2[:], ALU.subtract, K, K)
                            tt("dve", u1[:], CTr[:], pi_, ALU.mult, K + ["PW"], K)
                            tt("dve", u2[:], CTi[:], pr_, ALU.mult, K + ["PW"], K)
                            tt("dve", Xi[:, dl], u1[:], u2[:], ALU.add, K, K)
                        for tau in range(8):
                            cp("dve", CA[:, tau, 0, :, :], Xr[:, tau + 1], K, ["CA"])
                            ts("dve", CA[:, tau, 1, :, :], Xi[:, tau + 1], -1.0, ALU.mult, K, ["CA"])
                        for tau in range(8):
                            pr_ = bc(PWr[:, 7 - tau, :]); pi_ = bc(PWi[:, 7 - tau, :])
                            tt("dve", u1[:], Bbr[:], pr_, ALU.mult, K + ["PW"], K)
                            tt("dve", u2[:], Bbi[:], pi_, ALU.mult, K + ["PW"], K)
                            tt("dve", Yr[:], u1[:], u2[:], ALU.subtract, K, K)
                            tt("dve", u1[:], Bbr[:], pi_, ALU.mult, K + ["PW"], K)
                            tt("dve", u2[:], Bbi[:], pr_, ALU.mult, K + ["PW"], K)
                            tt("dve", Yi[:], u1[:], u2[:], ALU.add, K, K)
                            for ri, Y in ((0, Yr), (1, Yi)):
                                for ft in range(4):
                                    tp(pss[:, ft, :], Y[:, ft * 4:(ft + 1) * 4, :].rearrange("p a b -> p (a b)"), identf[:], K + ["identf"], ["pss"])
                                cp("dve", BL[:, :, tau, ri, :], pss[:], ["pss"], ["BL"])
                        for dl in range(8):
                            for ft in range(4):
                                fl = lambda t_: t_[:, ft * 4:(ft + 1) * 4, :].rearrange("p a b -> p (a b)")
                                mm(pss[:, ft, :], fl(Bbr), Xr[:, dl, ft * 4:(ft + 1) * 4, :].rearrange("p a b -> p (a b)"), True, False, K, ["pss"])
                                mm(pss[:, ft, :], fl(nBbi), Xi[:, dl, ft * 4:(ft + 1) * 4, :].rearrange("p a b -> p (a b)"), False, True, K, ["pss"])
                            tt("dve", BD[:, :, dl, :], pss[:], mask32[:].unsqueeze(1).to_broadcast([128, 4, 128]), ALU.mult, ["pss", "mask32"], ["BD"])
                    chk(2)
                    S.barrier()

                    with contextlib.ExitStack() as W1:
                        xt = [sb(W1, "xt%d_%d" % (l, i), [128, D], F32) for i in range(2)]
                        ssq = sb(W1, "ssq%d" % l, [128, 4], F32)
                        hn = sb(W1, "hn%d" % l, [128, D], BF16)
                        junk = hn
                        hnT = sb(W1, "hnT%d" % l, [128, 8, 512], BF16)
                        stg = [sb(W1, "stg%d_%d" % (l, i), [128, 512], BF16) for i in range(3)]
                        ef = [sb(W1, "ef%d_%d" % (l, i), [128, 512], F32) for i in range(2)]
                        kvst = [sb(W1, "kvst%d_%d" % (l, i), [128, 1024], F32) for i in range(2)]
                        vst = [sb(W1, "vst%d_%d" % (l, i), [128, 512], BF16) for i in range(2)]
                        uTs = [sb(W1, "uT%d_%d" % (l, i), [128, 4, 512], BF16) for i in range(2)]
                        gsTs = [sb(W1, "gsT%d_%d" % (l, i), [128, 4, 512], BF16) for i in range(2)]
                        Hall = sb(W1, "Hall%d" % l, [128, 65, 2, 16], F32)
                        Hbs = [sb(W1, "Hb%d_%d" % (l, i), [128, 2, 16, 64], BF16) for i in range(2)]
                        Hbns = [sb(W1, "Hbn%d_%d" % (l, i), [128, 2, 16, 64], BF16) for i in range(2)]
                        Pg = sb(W1, "Pg%d" % l, [128, 8, 2, 16], F32)
                        Qg = sb(W1, "Qg%d" % l, [128, 8, 2, 16], F32)
                        Pgp = sb(W1, "Pgp%d" % l, [128, 8, 2, 16], F32)
                        Qgp = sb(W1, "Qgp%d" % l, [128, 8, 2, 16], F32)
                        ysb = sb(W1, "ysb%d" % l, [128, 512], F32)
                        ygb = sb(W1, "ygb%d" % l, [128, 4, 512], BF16)
                        sy = sb(W1, "sy%d" % l, [128, 4, 512], F32)
                        sqb = sb(W1, "sqb%d" % l, [128, 4, 512], BF16)
                        rstd = ysb
                        sob = sqb
                        ptp = [W1.enter_context(nc.psum_tensor("ptp%d_%d" % (l, i), [128, 8, 128], BF16)) for i in range(1)]
                        ppj = [W1.enter_context(nc.psum_tensor("ppj%d_%d" % (l, i), [128, 512], F32)) for i in range(2)]
                        pS = [W1.enter_context(nc.psum_tensor("pS%d_%d" % (l, i), [128, 2, 4, 64], F32)) for i in range(4)]
                        py = [W1.enter_context(nc.psum_tensor("py%d_%d" % (l, i), [128, 64, 8], F32)) for i in range(1)]
                        cnt = dict(x=0, pj=0, stg=0, ef=0, kv=0, py=0)

                        def silu_evac(ps, pskey, out, outkey, N):
                            i = cnt["ef"] % 2; cnt["ef"] += 1
                            e_ = ef[i]; ek = "ef%d" % i
                            act(e_[:, :N], ps, AF.Exp, [pskey], [ek], scale=-1.0)
                            act(e_[:, :N], e_[:, :N], AF.Ln, [ek], [ek], bias=1.0)
                            act(e_[:, :N], e_[:, :N], AF.Exp, [ek], [ek], scale=-1.0)
                            tt("dve", out, e_[:, :N], ps, ALU.mult, [ek, pskey], [outkey])

                        for si, sq in enumerate(seqs):
                            NT, N, b = sq["NT"], sq["N"], sq["b"]
                            R = min(128, N); nsub = N // R; nch = N // 8
                            xsrc = sq["xin"] if l == 0 else (sq["xa"] if l % 2 == 1 else sq["xb"])
                            kout, vout = (kp, vp) if sq["kind"] == "p" else (ks, vs)
                            if sq["kind"] == "p":
                                mset("pool", Hall[:, 0], 0.0, ["Hall"])
                            else:
                                S.dma(Hall[:, 0, 0, :], sre[l, b].rearrange("(pr g2) p -> (g2 p) pr", g2=2), writes=["Hall"], chan="hld", allow_slow_non_contiguous=True)
                                S.dma(Hall[:, 0, 1, :], sim[l, b].rearrange("(pr g2) p -> (g2 p) pr", g2=2), writes=["Hall"], chan="hld", allow_slow_non_contiguous=True)
                            def front(ti, bi_):
                                t0 = ti * N
                                uT = uTs[bi_]; gsT = gsTs[bi_]; Hb = Hbs[bi_]; Hbn = Hbns[bi_]
                                uk = 'uT%d' % bi_; gk = 'gsT%d' % bi_; hbk = 'Hb%d' % bi_; hbnk = 'Hbn%d' % bi_
                                for sub in range(nsub):
                                    xi = cnt["x"] % 2; cnt["x"] += 1
                                    xk = "xt%d" % xi
                                    r0 = t0 + sub * R
                                    S.dma(xt[xi][:R, :], xsrc[r0:r0 + R, :], writes=[xk])
                                    act(junk[:R, :], xt[xi][:R, :], AF.Square, [xk], ["hn", "ssq"], accum=ssq[:R, 0:1])
                                    act(ssq[:R, 1:2], ssq[:R, 0:1], AF.Ln, ["ssq"], ["ssq"], scale=1.0 / D, bias=EPS)
                                    act(ssq[:R, 2:3], ssq[:R, 1:2], AF.Exp, ["ssq"], ["ssq"], scale=-0.5)
                                    stt(hn[:R, :], xt[xi][:R, :], ssq[:R, 2:3], gbc[:R, :], ALU.mult, ALU.mult, [xk, "ssq", "gbc"], ["hn"])
                                    pi_ = 0; pk = "ptp%d" % pi_
                                    for kc in range(8):
                                        tp(ptp[pi_][:, kc, :R], hn[:R, kc * 128:(kc + 1) * 128], identb[:R, :R], ["hn", "identb"], [pk])
                                    cp("dve", hnT[:, :, sub * R:sub * R + R], ptp[pi_][:, :, :R], [pk], ["hnT"])
                                chk(3)
                                for mt in list(range(0, 8)) + list(range(12, 24)):
                                    pj = cnt["pj"] % 2; cnt["pj"] += 1
                                    pjk = "ppj%d" % pj
                                    for kc in range(8):
                                        mm(ppj[pj][:, :N], winb[:, kc, mt * 128:(mt + 1) * 128], hnT[:, kc, :N], kc == 0, kc == 7, ["winb", "hnT"], [pjk])
                                    grp, ft = mt // 4, mt % 4
                                    if grp in (0, 1):
                                        sg = cnt["stg"] % 3; cnt["stg"] += 1
                                        sk = "stg%d" % sg
                                        if grp == 0:
                                            act(stg[sg][:, :N], ppj[pj][:, :N], AF.Copy, [pjk], [sk], scale=0.125)
                                        else:
                                            act(stg[sg][:, :N], ppj[pj][:, :N], AF.Copy, [pjk], [sk])
                                        dst = sq["qT"] if grp == 0 else sq["kT"]
                                        S.dma(dst[ft * 128:(ft + 1) * 128, t0:t0 + N], stg[sg][:, :N], reads=[sk], chan="st_" + sk)
                                    elif grp == 3:
                                        sg = cnt["stg"] % 3; cnt["stg"] += 1
                                        sk = "stg%d" % sg
                                        silu_evac(ppj[pj][:, :N], pjk, stg[sg][:, :N], sk, N)
                                        S.dma(sq["gaT"][ft * 128:(ft + 1) * 128, t0:t0 + N], stg[sg][:, :N], reads=[sk], chan="st_" + sk)
                                    elif grp == 4:
                                        act(uT[:, ft, :N], ppj[pj][:, :N], AF.Copy, [pjk], [uk])
                                    else:
                                        silu_evac(ppj[pj][:, :N], pjk, gsT[:, ft, :N], gk, N)
                                chk(4)
                                for sub in range(nsub):
                                    ki = cnt["kv"] % 2; cnt["kv"] += 1
                                    kk = "kvst%d" % ki; vk = "vst%d" % ki
                                    r0 = t0 + sub * R
                                    for half in range(2):
                                        pj = cnt["pj"] % 2; cnt["pj"] += 1
                                        pjk = "ppj%d" % pj
                                        for kc in range(8):
                                            mm(ppj[pj][:R, :], hnT[:, kc, sub * R:sub * R + R], winb[:, kc, 512 + half * 512:1024 + half * 512], kc == 0, kc == 7, ["winb", "hnT"], [pjk])
                                        if os.environ.get("KV", "abcde").find("d") >= 0:
                                            cp("dve", kvst[ki][:R, half * 512:(half + 1) * 512], ppj[pj][:R, :], [pjk], [kk])
                                        if half == 1 and os.environ.get("KV", "abcde").find("e") >= 0:
                                            act(vst[ki][:R, :], kvst[ki][:R, 512:1024], AF.Copy, [kk], [vk])
                                    if os.environ.get("KV", "abc").find("a") >= 0:
                                        S.dma(kout[l, b, r0:r0 + R, :], kvst[ki][:R, 0:512], reads=[kk], chan="st_" + kk)
                                    if os.environ.get("KV", "abc").find("b") >= 0:
                                        S.dma(vout[l, b, r0:r0 + R, :], kvst[ki][:R, 512:1024], reads=[kk], chan="st_" + kk)
                                    if os.environ.get("KV", "abc").find("c") >= 0:
                                        S.dma(sq["vv"][r0:r0 + R, :], vst[ki][:R, :], reads=[vk], chan="st_" + vk)
                                chk(5)
                                uv = uT[:, :, :N].rearrange("p f (k t) -> p f k t", t=8)
                                for j in range(4):
                                    pk = "pS%d" % j
                                    lo = 64 if j == 3 else 32 * j
                                    for ri in range(2):
                                        for ft in range(4):
                                            for tau in range(8):
                                                mm(pS[j][:, ri, ft, :nch], BL[lo:32 * j + 32, ft, tau, ri, :], uv[lo:32 * j + 32, ft, :, tau], tau == 0, tau == 7, ["BL", uk], [pk])
                                    cp("dve", Hall[:, 1:nch + 1, :, j::4].rearrange("p k r f -> p r f k"), pS[j][:, :, :, :nch], [pk], ["Hall"])
                                tt("dve", Hall[:, 1:nch + 1, :, 3::4], Hall[:, 1:nch + 1, :, 3::4], Hall[:, 1:nch + 1, :, 2::4], ALU.subtract, ["Hall"], ["Hall"])
                                chk(6)
                                def cmacc(eng, dst, X, a1, an, ap_, g):
                                    Pg_, Qg_ = (Pg, Qg) if eng == "dve" else (Pgp, Qgp)
                                    pk_, qk_ = ("Pg", "Qg") if eng == "dve" else ("Pgp", "Qgp")
                                    if g:
                                        P_, Q_ = Pg_[:, :g], Qg_[:, :g]
                                        Q0, Q1, X0, X1 = Qg_[:, :g, 0, :], Qg_[:, :g, 1, :], X[:, :, 0, :], X[:, :, 1, :]
                                    else:
                                        P_, Q_ = Pg_[:, 0], Qg_[:, 0]
                                        Q0, Q1, X0, X1 = Qg_[:, 0, 0, :], Qg_[:, 0, 1, :], X[:, 0, :], X[:, 1, :]
                                    tt(eng, P_, X, a1, ALU.mult, ["Hall", "A8"], [pk_])
                                    tt(eng, Q0, X1, an, ALU.mult, ["Hall", "A8"], [qk_])
                                    tt(eng, Q1, X0, ap_, ALU.mult, ["Hall", "A8"], [qk_])
                                    tt(eng, P_, P_, Q_, ALU.add, [pk_, qk_], [pk_])
                                    tt(eng, dst, dst, P_, ALU.add, [pk_, "Hall"], ["Hall"])

                                if nch == 64:
                                    G = 8
                                    Hv = Hall[:, 1:65].rearrange("p (g m) r c -> p g m r c", m=8)
                                    Cv = Hall[:, 0:64].rearrange("p (g m) r c -> p g m r c", m=8)[:, :, 0]
                                    b4 = lambda ap: ap.unsqueeze(1).to_broadcast([128, G, 2, 16])
                                    b3 = lambda ap: ap.unsqueeze(1).to_broadcast([128, G, 16])
                                    for j in range(1, 8):
                                        cmacc("dve", Hv[:, :, j], Hv[:, :, j - 1], b4(A1[:]), b3(Ain[:]), b3(Aip[:]), G)
                                    for g in range(G):
                                        cmacc("dve", Hall[:, 8 * (g + 1)], Hall[:, 8 * g], AJ1[:, 7], AJn[:, 7], PWi[:, 15, :], 0)
                                    for j in range(7):
                                        cmacc("pool", Hv[:, :, j], Cv, b4(AJ1[:, j]), b3(AJn[:, j]), b3(PWi[:, 8 + j, :]), G)
                                else:
                                    for k in range(nch):
                                        cmacc("pool", Hall[:, k + 1], Hall[:, k], A1[:], Ain[:], Aip[:], 0)
                                for ri in range(2):
                                    cp("pool", Hb[:, ri, :, :nch], Hall[:, 0:nch, ri, :].rearrange("p k r -> p r k"), ["Hall"], [hbk])
                                ts("pool", Hbn[:, :, :, :nch], Hb[:, :, :, :nch], -1.0, ALU.mult, [hbk], [hbnk])
                                if ti == NT // N - 1:
                                    ho_r, ho_i = (hrp, hip) if sq["kind"] == "p" else (hrs, his)
                                    S.dma(ho_r[l, b].rearrange("(pr g2) p -> (g2 p) pr", g2=2), Hall[:, nch, 0, :], reads=["Hall"], chan="hst", allow_slow_non_contiguous=True)
                                    S.dma(ho_i[l, b].rearrange("(pr g2) p -> (g2 p) pr", g2=2), Hall[:, nch, 1, :], reads=["Hall"], chan="hst", allow_slow_non_contiguous=True)
                                else:
                                    cp("pool", Hall[:, 0], Hall[:, nch], ["Hall"], ["Hall"])

                            def back(ti, bi_):
                                t0 = ti * N
                                uT = uTs[bi_]; gsT = gsTs[bi_]; Hb = Hbs[bi_]; Hbn = Hbns[bi_]
                                uk = 'uT%d' % bi_; gk = 'gsT%d' % bi_; hbk = 'Hb%d' % bi_; hbnk = 'Hbn%d' % bi_
                                uv = uT[:, :, :N].rearrange("p f (k t) -> p f k t", t=8)
                                chk(7)
                                for ft in range(4):
                                    yi = 0
                                    yk = "py%d" % yi
                                    for dl in range(8):
                                        mm(py[yi][:, :nch, dl:8], BD[:, ft, dl, :], uv[:, ft, :, 0:8 - dl], dl == 0, False, ["BD", uk], [yk])
                                    for tau in range(8):
                                        for j in range(4):
                                            pr = ft * 4 + j
                                            for ri in range(2):
                                                lastm = (tau == 7 and j == 3 and ri == 1)
                                                if j < 3:
                                                    mm(py[yi][32 * j:32 * j + 32, :nch, tau], CA[:, tau, ri, pr, :], Hb[:, ri, pr, :nch], False, False, ["CA", hbk], [yk])
                                                else:
                                                    mm(py[yi][64:128, :nch, tau], CA[:, tau, ri, pr - 1:pr + 1, :].rearrange("p a b -> p (a b)"), Hb[:, ri, pr, :nch], False, False, ["CA", hbk], [yk])
                                                    mm(py[yi][64:96, :nch, tau], CA[:, tau, ri, pr - 1, :], Hbn[:, ri, pr, :nch], False, lastm, ["CA", hbnk], [yk])
                                    yv = py[yi][:, :nch, :]
                                    ysk = "ys%d" % ft
                                    ysf = sy[:, ft, :N]
                                    ysv = ysf.rearrange("p (k t) -> p k t", t=8)
                                    gi = cnt["ef"] % 2; cnt["ef"] += 1
                                    tg_ = ef[gi]; tk = "ef%d" % gi
                                    stt(ysv, uv[:, ft], dvec[:, ft:ft + 1], yv, ALU.mult, ALU.add, [uk, "dvec%d" % l, yk], [ysk])
                                    ts("dve", ysf, ysf, -7.0, ALU.max, [ysk], [ysk])
                                    tt("dve", tg_[:, :N], ysf, ysf, ALU.mult, [ysk], [tk])
                                    ts("dve", tg_[:, :N], tg_[:, :N], 0.044715, ALU.mult, [tk], [tk], s2=1.0, op1=ALU.add)
                                    tt("dve", tg_[:, :N], tg_[:, :N], ysf, ALU.mult, [tk, ysk], [tk])
                                    act(tg_[:, :N], tg_[:, :N], AF.Exp, [tk], [tk], scale=-1.5957691216057308)
                                    act(tg_[:, :N], tg_[:, :N], AF.Ln, [tk], [tk], bias=1.0)
                                    act(tg_[:, :N], tg_[:, :N], AF.Exp, [tk], [tk], scale=-1.0)
                                    tt("dve", ygb[:, ft, :N], ysf, tg_[:, :N], ALU.mult, [tk, ysk], ["ygb"])
                                chk(8)
                                for mt in range(4):
                                    pj = cnt["pj"] % 2; cnt["pj"] += 1
                                    pjk = "ppj%d" % pj
                                    for kc in range(4):
                                        mm(ppj[pj][:, :N], wglub[:, kc, mt * 128:(mt + 1) * 128], ygb[:, kc, :N], kc == 0, kc == 3, ["wglub", "ygb"], [pjk])
                                    gi = cnt["ef"] % 2; cnt["ef"] += 1
                                    tg_ = ef[gi]; tk = "ef%d" % gi
                                    act(tg_[:, :N], ppj[pj][:, :N], AF.Exp, [pjk, "nbg%d" % l], [tk], scale=-1.0, bias=nbg[:, mt:mt + 1])
                                    act(tg_[:, :N], tg_[:, :N], AF.Ln, [tk], [tk], bias=1.0)
                                    act(tg_[:, :N], tg_[:, :N], AF.Exp, [tk], [tk], scale=-1.0)
                                    tt("dve", sy[:, mt, :N], ygb[:, mt, :N], tg_[:, :N], ALU.mult, [tk, "ygb", "ys%d" % mt], ["sy"])
                                    tt("pool", sqb[:, mt, :N], sy[:, mt, :N], sy[:, mt, :N], ALU.mult, ["sy"], ["sqb"])
                                pj = cnt["pj"] % 2; cnt["pj"] += 1
                                pjk = "ppj%d" % pj
                                for kc in range(4):
                                    mm(ppj[pj][:, :N], onesb[:], sqb[:, kc, :N], kc == 0, kc == 3, ["onesb", "sqb"], [pjk])
                                act(rstd[:, :N], ppj[pj][:, :N], AF.Ln, [pjk], ["ysb"], scale=1.0 / 512, bias=EPS)
                                act(rstd[:, :N], rstd[:, :N], AF.Exp, ["ysb"], ["ysb"], scale=-0.5)
                                for ft in range(4):
                                    tt("dve", sy[:, ft, :N], sy[:, ft, :N], rstd[:, :N], ALU.mult, ["sy", "ysb"], ["sy"])
                                    stt(sob[:, ft, :N], sy[:, ft, :N], gssm[:, ft:ft + 1], gsT[:, ft, :N], ALU.mult, ALU.mult, ["sy", "gssm%d" % l, gk], ["sqb"])
                                S.dma(sq["soT"].rearrange("(f p) t -> p f t", p=128)[:, :, t0:t0 + N], sob[:, :, :N], reads=["sqb"], chan="st_sob")

                            ntile = NT // N
                            front(0, 0)
                            for ti in range(ntile):
                                if ti + 1 < ntile:
                                    front(ti + 1, (ti + 1) % 2)
                                back(ti, ti % 2)
                    S.barrier()

                chk(10)
                with contextlib.ExitStack() as L2:
                    woutb = sb(L2, "woutb%d" % l, [128, 8, D], BF16)
                    gatt = sb(L2, "gatt%d" % l, [128, 4], F32)
                    NKmax = max(T, PL + TS)
                    nblk_max = (NKmax + 127) // 128
                    kTs = sb(L2, "kTs%d" % l, [128, 4, nblk_max * 128], BF16)
                    vsb = sb(L2, "vsb%d" % l, [128, nblk_max, 512], BF16)
                    qTt = sb(L2, "qTt%d" % l, [128, 4, 512], BF16)
                    gaTt = sb(L2, "gaTt%d" % l, [128, 4, 512], BF16)
                    catT = sb(L2, "catT%d" % l, [128, 8, 512], BF16)
                    ckst = [sb(L2, "ckst%d_%d" % (l, i), [128, 512], BF16) for i in range(2)]
                    e_sb = [sb(L2, "e_sb%d_%d" % (l, i), [128, 512], F32) for i in range(3)]
                    sp_sb = [sb(L2, "sp_sb%d_%d" % (l, i), [128, 512], BF16) for i in range(3)]
                    w_sb = [sb(L2, "w_sb%d_%d" % (l, i), [128, 512], BF16) for i in range(3)]
                    spsum = [sb(L2, "spsum%d_%d" % (l, i), [128, 512], BF16) for i in range(4)]
                    att = sb(L2, "att%d" % l, [128, 4, 512], F32)
                    asq = sb(L2, "asq%d" % l, [128, 4, 512], BF16)
                    rstd2 = sb(L2, "rstd2%d" % l, [128, 512], F32)
                    xt2 = [sb(L2, "xt2%d_%d" % (l, i), [128, D], F32) for i in range(2)]
                    xn = [sb(L2, "xn%d_%d" % (l, i), [128, D], F32) for i in range(2)]
                    junk2 = sb(L2, "junk2%d" % l, [128, D], BF16)
                    ssq2 = sb(L2, "ssq2%d" % l, [128, 4], F32)
                    pA = [L2.enter_context(nc.psum_tensor("pA%d_%d" % (l, i), [128, 512], F32)) for i in range(5)]
                    pAtt = [L2.enter_context(nc.psum_tensor("pAtt%d_%d" % (l, i), [128, 512], F32)) for i in range(1)]
                    pO = [L2.enter_context(nc.psum_tensor("pO%d_%d" % (l, i), [128, 512], F32)) for i in range(2)]
                    ptk = pO[0][:].bitcast(BF16)
                    for kc in range(8):
                        S.dma(woutb[:, kc, :], w_out[l, kc * 128:(kc + 1) * 128, :], writes=["woutb"], queue="pool")
                    S.dma(gatt[:], g_att[l].rearrange("(c p) -> p c", p=128), writes=["gatt"], allow_slow_non_contiguous=True)
                    c2 = dict(x=0, o=0, ck=0, blk=0)
                    for si, sq in enumerate(seqs):
                        NT, N, b, past = sq["NT"], sq["N"], sq["b"], sq["past"]
                        R = min(128, N); nsub = N // R
                        xsrc = sq["xin"] if l == 0 else (sq["xa"] if l % 2 == 1 else sq["xb"])
                        xdst = sq["xa"] if l % 2 == 0 else sq["xb"]
                        npast = past // 128
                        for pb in range(npast):
                            ci = c2["ck"] % 2; c2["ck"] += 1
                            ckk = "ckst%d" % ci
                            S.dma(ckst[ci][:], ck[l, b, pb * 128:(pb + 1) * 128, :], writes=[ckk], queue="pool")
                            for hp in range(4):
                                tp(ptk[:, hp * 128:(hp + 1) * 128], ckst[ci][:, hp * 128:(hp + 1) * 128], identb[:], [ckk, "identb"], ["pO0"])
                            cp("dve", kTs[:, :, pb * 128:(pb + 1) * 128], ptk[:, 0:512].rearrange("p (h s) -> p h s", h=4), ["pO0"], ["kTs"])
                        if npast:
                            S.dma(vsb[:, 0:npast, :], cv[l, b].rearrange("(n p) f -> p n f", p=128), writes=["vsb"], queue="pool")
                        S.dma(kTs[:, :, past:past + NT], sq["kT"].rearrange("(h p) t -> p h t", p=128), writes=["kTs"])
                        if NT >= 128:
                            S.dma(vsb[:, npast:npast + NT // 128, :], sq["vv"].rearrange("(n p) f -> p n f", p=128), writes=["vsb"])
                        else:
                            S.dma(vsb[:NT, npast, :], sq["vv"], writes=["vsb"])
                        for ti in range(NT // N):
                            t0 = ti * N
                            S.dma(qTt[:, :, :N], sq["qT"].rearrange("(h p) t -> p h t", p=128)[:, :, t0:t0 + N], writes=["qTt"])
                            S.dma(gaTt[:, :, :N], sq["gaT"].rearrange("(h p) t -> p h t", p=128)[:, :, t0:t0 + N], writes=["gaTt"])
                            S.dma(catT[:, 4:8, :N], sq["soT"].rearrange("(h p) t -> p h t", p=128)[:, :, t0:t0 + N], writes=["catT"])
                            chk(11)
                            blocks = []
                            if sq["kind"] == "p":
                                for j in reversed(range(nsub)):
                                    blocks.append((t0 + j * 128, 128, (t0 + j * 128) // 128, masks[:, j, :N]))
                                for kb in reversed(range(t0 // 128)):
                                    blocks.append((kb * 128, 128, kb, None))
                            else:
                                blocks.append((past, NT, npast, masks_s[:, :]))
                                for kb in reversed(range(npast)):
                                    blocks.append((kb * 128, 128, kb, None))
                            work = [(h, bi) for h in range(8) for bi in range(len(blocks))]

                            def info(idx):
                                h, bi = work[idx]
                                s0, Rk, vb, mk = blocks[bi]
                                return h, bi, s0, Rk, vb, mk, h // 2, 64 * (h % 2)

                            def sA(idx):
                                h, bi, s0, Rk, vb, mk, hp, base = info(idx)
                                sl = idx % 5
                                mm(pA[sl][:Rk, :N], kTs[base:base + 64, hp, s0:s0 + Rk], qTt[base:base + 64, hp, :N], True, False, ["kTs", "qTt"], ["pA%d" % sl])

                            def sB(idx):
                                h, bi, s0, Rk, vb, mk, hp, base = info(idx)
                                sl = idx % 5
                                act(e_sb[idx % 3][:Rk, :N], pA[sl][:Rk, :N], AF.Exp, ["pA%d" % sl], ["e_sb%d" % (idx % 3)])

                            def sC(idx):
                                h, bi, s0, Rk, vb, mk, hp, base = info(idx)
                                spk = "sp_sb%d" % (idx % 3)
                                act(sp_sb[idx % 3][:Rk, :N], e_sb[idx % 3][:Rk, :N], AF.Ln, ["e_sb%d" % (idx % 3)], [spk], bias=1.0)
                                if mk is not None:
                                    tt("pool", sp_sb[idx % 3][:Rk, :N], sp_sb[idx % 3][:Rk, :N], mk, ALU.mult, [spk, "masks"], [spk])

                            def sD(idx):
                                h, bi, s0, Rk, vb, mk, hp, base = info(idx)
                                sl = idx % 5
                                Ak, spk = "pA%d" % sl, "sp_sb%d" % (idx % 3)
                                spt = sp_sb[idx % 3]
                                first, lastb = (bi == 0), (bi == len(blocks) - 1)
                                mm(pA[sl][:Rk, :N], negU[:Rk, :Rk], spt[:Rk, :N], False, first, [spk, "negU"], [Ak])
                                if not first:
                                    prv = "spsum%d" % ((idx - 1) % 4)
                                    mm(pA[sl][:Rk, :N], negones[:, :Rk], spsum[(idx - 1) % 4][:, :N], False, True, [prv, "negones"], [Ak])
                                if not lastb:
                                    cur, prv = "spsum%d" % (idx % 4), "spsum%d" % ((idx - 1) % 4)
                                    if first:
                                        if Rk < 128:
                                            mset("dve", spsum[idx % 4][:, :N], 0.0, [cur])
                                        cp("dve", spsum[idx % 4][:Rk, :N], spt[:Rk, :N], [spk], [cur])
                                    else:
                                        tt("dve", spsum[idx % 4][:, :N], spsum[(idx - 1) % 4][:, :N], spt[:, :N], ALU.add, [spk, prv], [cur])

                            def sE(idx):
                                h, bi, s0, Rk, vb, mk, hp, base = info(idx)
                                sl = idx % 5
                                wk = "w_sb%d" % (idx % 3)
                                act(w_sb[idx % 3][:Rk, :N], pA[sl][:Rk, :N], AF.Exp, ["pA%d" % sl], [wk])
                                if mk is not None:
                                    tt("pool", w_sb[idx % 3][:Rk, :N], w_sb[idx % 3][:Rk, :N], mk, ALU.mult, [wk, "masks"], [wk])

                            def sF(idx):
                                h, bi, s0, Rk, vb, mk, hp, base = info(idx)
                                wk = "w_sb%d" % (idx % 3)
                                atk = "pAtt_%d" % (h % 2)
                                first, lastb = (bi == 0), (bi == len(blocks) - 1)
                                mm(pAtt[0][base:base + 64, :N], vsb[:Rk, vb, h * 64:(h + 1) * 64], w_sb[idx % 3][:Rk, :N], first, lastb, [wk, "vsb"], [atk])
                                if lastb:
                                    cp("dve", att[base:base + 64, hp, :N], pAtt[0][base:base + 64, :N], [atk], ["att"])

                            nw = len(work)
                            stages = [sA, sB, sC, sD, sE, sF]
                            for it in range(nw + 5):
                                for d, fn in enumerate(stages):
                                    if 0 <= it - d < nw:
                                        fn(it - d)
                            chk(12)
                            for hp in range(4):
                                tt("pool", asq[:, hp, :N], att[:, hp, :N], att[:, hp, :N], ALU.mult, ["att"], ["asq"])
                            for hp in range(4):
                                mm(pO[1][:, :N], onesb[:], asq[:, hp, :N], hp == 0, hp == 3, ["onesb", "asq"], ["pO1"])
                            act(rstd2[:, :N], pO[1][:, :N], AF.Ln, ["pO1"], ["rstd2"], scale=1.0 / 512, bias=EPS)
                            act(rstd2[:, :N], rstd2[:, :N], AF.Exp, ["rstd2"], ["rstd2"], scale=-0.5)
                            for hp in range(4):
                                tt("dve", att[:, hp, :N], att[:, hp, :N], rstd2[:, :N], ALU.mult, ["att", "rstd2"], ["att"])
                                stt(catT[:, hp, :N], att[:, hp, :N], gatt[:, hp:hp + 1], gaTt[:, hp, :N], ALU.mult, ALU.mult, ["att", "gatt", "gaTt"], ["catT"])
                            chk(13)
                            for sub in range(nsub):
                                xi = c2["x"] % 2; c2["x"] += 1
                                xk, nk = "xt2%d" % xi, "xn%d" % xi
                                r0 = t0 + sub * R
                                S.dma(xt2[xi][:R, :], xsrc[r0:r0 + R, :], writes=[xk])
                                for half in range(2):
                                    oi = c2["o"] % 2; c2["o"] += 1
                                    ok = "pO%d" % oi
                                    for kc in range(8):
                                        mm(pO[oi][:R, :], catT[:, kc, sub * R:sub * R + R], woutb[:, kc, half * 512:(half + 1) * 512], kc == 0, kc == 7, ["catT", "woutb"], [ok])
                                    tt("dve", xn[xi][:R, half * 512:(half + 1) * 512], pO[oi][:R, :], xt2[xi][:R, half * 512:(half + 1) * 512], ALU.add, [ok, xk], [nk])
                                if not last:
                                    S.dma(xdst[r0:r0 + R, :], xn[xi][:R, :], reads=[nk], chan="st_" + nk)
                                else:
                                    act(junk2[:R, :], xn[xi][:R, :], AF.Square, [nk], ["junk2"])
                                    S.op("dve", lambda e: e.tensor_reduce(out=ssq2[:R, 0:1], in_=junk2[:R, :], axis=mybir.AxisListType.X, op=ALU.add), ["junk2"], ["ssq2"])
                                    act(ssq2[:R, 1:2], ssq2[:R, 0:1], AF.Ln, ["ssq2"], ["ssq2"], scale=1.0 / D, bias=EPS)
                                    act(ssq2[:R, 2:3], ssq2[:R, 1:2], AF.Exp, ["ssq2"], ["ssq2"], scale=-0.5)
                                    stt(xt2[xi][:R, :], xn[xi][:R, :], ssq2[:R, 2:3], fgbc[:R, :], ALU.mult, ALU.mult, [nk, "ssq2", "fgbc"], [xk])
                                    S.dma(sq["yout"][r0:r0 + R, :], xt2[xi][:R, :], reads=[xk], chan="st_" + xk)
                    S.barrier()
        except StopBuild:
            pass
        S._final = True
        S.barrier()
    return nc


_CACHE = {}


def _run(inputs, T, TS, PL, DEPTH):
    key = (T, TS, PL, DEPTH)
    if key not in _CACHE:
        _CACHE[key] = build(T, TS, PL, DEPTH)
    nc = _CACHE[key]
    f = lambda a: np.ascontiguousarray(np.asarray(a, dtype=np.float32))
    in_maps = []
    for c in range(8):
        sl = slice(2 * c, 2 * c + 2)
        m = {
            "xp": f(inputs["x_prompt"][sl]), "xs": f(inputs["x_sample"][sl]),
            "ck": f(np.asarray(inputs["cache_k"])[:, sl].reshape(DEPTH, 2, PL, 512)),
            "cv": f(np.asarray(inputs["cache_v"])[:, sl].reshape(DEPTH, 2, PL, 512)),
            "sre": f(np.asarray(inputs["state_ssm_re"])[:, sl]), "sim": f(np.asarray(inputs["state_ssm_im"])[:, sl]),
            "ln_g": f(inputs["ln_g"]), "w_in": f(inputs["w_in"]),
            "a_re": f(inputs["ssm_a_re"]), "a_im": f(inputs["ssm_a_im"]), "log_dt": f(inputs["ssm_log_dt"]),
            "b_re": f(inputs["ssm_b_re"]), "b_im": f(inputs["ssm_b_im"]),
            "c_re": f(inputs["ssm_c_re"]), "c_im": f(inputs["ssm_c_im"]),
            "ssm_d": f(inputs["ssm_d"]), "w_glu": f(inputs["w_glu"]), "b_glu": f(inputs["b_glu"]),
            "g_att": f(inputs["g_att"]), "g_ssm": f(inputs["g_ssm"]), "w_out": f(inputs["w_out"]),
            "final_g": f(inputs["final_g"]),
        }
        in_maps.append(m)
    res = run_bass_kernel_spmd(nc, in_maps, core_ids=list(range(8)))
    R = res.results
    cat0 = lambda k: np.concatenate([r[k] for r in R], axis=0)
    cat1 = lambda k: np.concatenate([r[k] for r in R], axis=1)
    y_p = cat0("yp"); y_s = cat0("ys")
    k_p = cat1("kp").reshape(DEPTH, 16, T, 8, 64); v_p = cat1("vp").reshape(DEPTH, 16, T, 8, 64)
    k_s = cat1("ks").reshape(DEPTH, 16, TS, 8, 64); v_s = cat1("vs").reshape(DEPTH, 16, TS, 8, 64)
    return (y_p, y_s, k_p, v_p, cat1("hrp"), cat1("hip"), k_s, v_s, cat1("hrs"), cat1("his"))


def kernel(**inputs):
    T = int(np.shape(inputs["x_prompt"])[1]); TS = int(np.shape(inputs["x_sample"])[1])
    PL = int(np.shape(inputs["cache_k"])[2]); DEPTH = int(np.shape(inputs["w_in"])[0])
    return _run(inputs, T, TS, PL, DEPTH)
```

```python
import contextlib
import math
import numpy as np
import concourse.bass as bass
import concourse.mybir as mybir
from concourse.bass_utils import run_bass_kernel_spmd

F32 = mybir.dt.float32
BF16 = mybir.dt.bfloat16
I32 = mybir.dt.int32
AF = mybir.ActivationFunctionType
ALU = mybir.AluOpType
EPS = 1e-6
TWO_PI = 2.0 * math.pi


class Sched:
    def __init__(self, nc, stack):
        self.nc = nc
        self.stack = stack
        self.eng = {"pe": nc.tensor, "act": nc.scalar, "dve": nc.vector,
                    "pool": nc.gpsimd, "sp": nc.sync}
        self.sem = {}
        self.cnt = {}
        for e in ("pe", "act", "dve", "pool"):
            self.sem[e] = stack.enter_context(nc.semaphore("s_" + e))
            self.cnt[e] = 0
        self.waited = {e: {} for e in self.eng}
        self.lastw = {}
        self.readers = {}
        self.nchan = 0
        self.stopped = False

    def chan(self, name):
        if name not in self.sem:
            self.sem[name] = self.stack.enter_context(self.nc.semaphore("d%d" % self.nchan))
            self.nchan += 1
            self.cnt[name] = 0
        return name

    def _deps(self, engine, reads, writes):
        deps = {}

        def add(d, raw):
            if d is None:
                return
            src, c = d
            if src == engine and not raw:
                return
            if src == "pe" and engine == "pe":
                return
            if deps.get(src, 0) < c:
                deps[src] = c

        for r in reads:
            add(self.lastw.get(r), True)
        for w in writes:
            add(self.lastw.get(w), True)
            for src, c in self.readers.get(w, {}).items():
                add((src, c), False)
        return deps

    def _emit_waits(self, engine, deps):
        e = self.eng[engine]
        wd = self.waited[engine]
        for src, c in deps.items():
            if wd.get(src, 0) >= c:
                continue
            e.wait_ge(self.sem[src], c)
            wd[src] = c

    def op(self, engine, fn, reads=(), writes=()):
        if self.stopped:
            return None
        deps = self._deps(engine, reads, writes)
        self._emit_waits(engine, deps)
        ins = fn(self.eng[engine])
        ins.then_inc(self.sem[engine], 1)
        self.cnt[engine] += 1
        c = self.cnt[engine]
        for r in reads:
            self.readers.setdefault(r, {})[engine] = c
        for w in writes:
            self.lastw[w] = (engine, c)
            self.readers[w] = {}
        return ins

    def dma(self, out, in_, reads=(), writes=(), chan=None, queue="sp", **kw):
        if self.stopped:
            return None
        if chan is None:
            chan = "c_" + str(writes[0] if writes else reads[0])
        self.chan(chan)
        deps = self._deps("__dma__", reads, writes)
        self._emit_waits(queue, deps)
        ins = self.eng[queue].dma_start(out=out, in_=in_, **kw)
        ins.then_inc(self.sem[chan], 16)
        self.cnt[chan] += 16
        c = self.cnt[chan]
        for r in reads:
            self.readers.setdefault(r, {})[chan] = c
        for w in writes:
            self.lastw[w] = (chan, c)
            self.readers[w] = {}
        return ins

    def barrier(self):
        if self.stopped and getattr(self, "_final", False) is False:
            return
        allsrc = [s for s in self.cnt if self.cnt[s] > 0]
        for e in ("pe", "act", "dve", "pool", "sp"):
            self._emit_waits(e, {s: self.cnt[s] for s in allsrc if s != e})


class StopBuild(Exception):
    pass


def build(T, TS, PL, DEPTH):
    import os
    KSTOP = int(os.environ.get("KSTOP", "99"))

    def chk(level):
        if KSTOP <= level:
            S.stopped = True

    nc = bass.Bass("TRN2", target_bir_lowering=False)
    D = 1024

    def din(name, shape):
        return nc.dram_tensor(name, shape, F32, kind="ExternalInput").ap()

    def dout(name, shape):
        return nc.dram_tensor(name, shape, F32, kind="ExternalOutput").ap()

    def dscr(name, shape, dt):
        return nc.dram_tensor(name, shape, dt, kind="Internal").ap()

    xp = din("xp", [2, T, D]); xs = din("xs", [2, TS, D])
    ck = din("ck", [DEPTH, 2, PL, 512]); cv = din("cv", [DEPTH, 2, PL, 512])
    sre = din("sre", [DEPTH, 2, 32, 64]); sim = din("sim", [DEPTH, 2, 32, 64])
    ln_g = din("ln_g", [DEPTH, D]); w_in = din("w_in", [DEPTH, D, 3072])
    a_re = din("a_re", [DEPTH, 32, 64]); a_im = din("a_im", [DEPTH, 32, 64])
    log_dt = din("log_dt", [DEPTH, 32])
    b_re = din("b_re", [DEPTH, 32, 64, 16]); b_im = din("b_im", [DEPTH, 32, 64, 16])
    c_re = din("c_re", [DEPTH, 32, 16, 64]); c_im = din("c_im", [DEPTH, 32, 16, 64])
    ssm_d = din("ssm_d", [DEPTH, 512]); w_glu = din("w_glu", [DEPTH, 512, 512])
    b_glu = din("b_glu", [DEPTH, 512]); g_att = din("g_att", [DEPTH, 512])
    g_ssm = din("g_ssm", [DEPTH, 512]); w_out = din("w_out", [DEPTH, D, D])
    final_g = din("final_g", [D])
    yp = dout("yp", [2, T, D]); ys = dout("ys", [2, TS, D])
    kp = dout("kp", [DEPTH, 2, T, 512]); vp = dout("vp", [DEPTH, 2, T, 512])
    hrp = dout("hrp", [DEPTH, 2, 32, 64]); hip = dout("hip", [DEPTH, 2, 32, 64])
    ks = dout("ks", [DEPTH, 2, TS, 512]); vs = dout("vs", [DEPTH, 2, TS, 512])
    hrs = dout("hrs", [DEPTH, 2, 32, 64]); his = dout("his", [DEPTH, 2, 32, 64])

    seqs = []
    for b in range(2):
        seqs.append(dict(kind="p", b=b, NT=T, xin=xp[b], yout=yp[b], past=0))
    for b in range(2):
        seqs.append(dict(kind="s", b=b, NT=TS, xin=xs[b], yout=ys[b], past=PL))
    for si, sq in enumerate(seqs):
        NT = sq["NT"]
        sq["xa"] = dscr("xa%d" % si, [NT, D], F32)
        sq["xb"] = dscr("xb%d" % si, [NT, D], F32)
        for nm in ("qT", "kT", "gaT", "soT"):
            sq[nm] = dscr("%s%d" % (nm, si), [512, NT], BF16)
        sq["vv"] = dscr("vv%d" % si, [NT, 512], BF16)
        sq["N"] = min(512, NT)

    with contextlib.ExitStack() as top:
        S = Sched(nc, top)

        def sb(stack, name, shape, dt):
            return stack.enter_context(nc.sbuf_tensor(name, shape, dt))

        def mm(out, lhsT, rhs, start, stop, r, w):
            S.op("pe", lambda e: e.matmul(out, lhsT, rhs, start=start, stop=stop), r, w)

        def tp(out, in_, ident, r, w):
            S.op("pe", lambda e: e.transpose(out, in_, ident), r, w)

        def act(out, in_, func, r, w, scale=1.0, bias=0.0, accum=None):
            if accum is None:
                S.op("act", lambda e: e.activation(out=out, in_=in_, func=func, bias=bias, scale=scale), r, w)
            else:
                S.op("act", lambda e: e.activation(out=out, in_=in_, func=func, bias=bias, scale=scale, accum_out=accum), r, w)

        def tt(eng, out, in0, in1, op, r, w):
            S.op(eng, lambda e: e.tensor_tensor(out=out, in0=in0, in1=in1, op=op), r, w)

        def ts(eng, out, in0, s1, op0, r, w, s2=None, op1=None):
            if op1 is None:
                S.op(eng, lambda e: e.tensor_scalar(out=out, in0=in0, scalar1=s1, scalar2=None, op0=op0), r, w)
            else:
                S.op(eng, lambda e: e.tensor_scalar(out=out, in0=in0, scalar1=s1, scalar2=s2, op0=op0, op1=op1), r, w)

        def stt(out, in0, scalar, in1, op0, op1, r, w):
            S.op("dve", lambda e: e.scalar_tensor_tensor(out=out, in0=in0, scalar=scalar, in1=in1, op0=op0, op1=op1), r, w)

        def cp(eng, out, in_, r, w):
            S.op(eng, lambda e: e.tensor_copy(out=out, in_=in_), r, w)

        def recip(out, in_, r, w):
            S.op("dve", lambda e: e.reciprocal(out=out, in_=in_), r, w)

        def mset(eng, ap, val, w):
            S.op(eng, lambda e: e.memset(ap, val), (), w)

        identb = sb(top, "identb", [128, 128], BF16)
        identf = sb(top, "identf", [128, 128], F32)
        negU = sb(top, "negU", [128, 128], BF16)
        negones = sb(top, "negones", [128, 128], BF16)
        onesb = sb(top, "onesb", [128, 128], BF16)
        mask32 = sb(top, "mask32", [128, 128], F32)
        fgbc = sb(top, "fgbc", [128, D], F32)
        NP = seqs[0]["N"]
        ndiag = NP // 128
        masks = sb(top, "masks", [128, ndiag, NP], BF16)
        masks_s = sb(top, "masks_s", [TS, TS], BF16)
        mset("pool", identb[:], 0.0, ["identb"])
        S.op("pool", lambda e: e.affine_select(out=identb[:], in_=identb[:], pattern=[[-1, 128]], compare_op=ALU.not_equal, fill=1.0, base=0, channel_multiplier=1), ["identb"], ["identb"])
        mset("pool", identf[:], 0.0, ["identf"])
        S.op("pool", lambda e: e.affine_select(out=identf[:], in_=identf[:], pattern=[[-1, 128]], compare_op=ALU.not_equal, fill=1.0, base=0, channel_multiplier=1), ["identf"], ["identf"])
        mset("pool", negU[:], -1.0, ["negU"])
        S.op("pool", lambda e: e.affine_select(out=negU[:], in_=negU[:], pattern=[[-1, 128]], compare_op=ALU.is_ge, fill=0.0, base=0, channel_multiplier=1), ["negU"], ["negU"])
        mset("pool", negones[:], -1.0, ["negones"])
        mset("pool", onesb[:], 1.0, ["onesb"])
        mset("pool", mask32[:], 0.0, ["mask32"])
        for j in range(4):
            mset("pool", mask32[32 * j:32 * j + 32, 32 * j:32 * j + 32], 1.0, ["mask32"])
        mset("pool", masks[:], 1.0, ["masks"])
        for j in range(ndiag):
            S.op("pool", lambda e: e.affine_select(out=masks[:, j, :], in_=masks[:, j, :], pattern=[[1, NP]], compare_op=ALU.is_gt, fill=0.0, base=-128 * j, channel_multiplier=-1), ["masks"], ["masks"])
        mset("pool", masks_s[:], 1.0, ["masks_s"])
        S.op("pool", lambda e: e.affine_select(out=masks_s[:], in_=masks_s[:], pattern=[[1, TS]], compare_op=ALU.is_gt, fill=0.0, base=0, channel_multiplier=-1), ["masks_s"], ["masks_s"])
        S.dma(fgbc[:], final_g.partition_broadcast(128), writes=["fgbc"])

        try:
            chk(0)
            for l in range(DEPTH):
                last = (l == DEPTH - 1)
                with contextlib.ExitStack() as L1:
                    winb = sb(L1, "winb%d" % l, [128, 8, 3072], BF16)
                    wglub = sb(L1, "wglub%d" % l, [128, 4, 512], BF16)
                    gbc = sb(L1, "gbc%d" % l, [128, D], F32)
                    dvec = sb(L1, "dvec%d" % l, [128, 4], F32)
                    nbg = sb(L1, "nbg%d" % l, [128, 4], F32)
                    gssm = sb(L1, "gssm%d" % l, [128, 4], F32)
                    PWr = sb(L1, "PWr%d" % l, [128, 16, 16], F32)
                    PWi = sb(L1, "PWi%d" % l, [128, 16, 16], F32)
                    A1 = sb(L1, "A1%d" % l, [128, 2, 16], F32)
                    Aip = sb(L1, "Aip%d" % l, [128, 16], F32)
                    Ain = sb(L1, "Ain%d" % l, [128, 16], F32)
                    AJ1 = sb(L1, "AJ1%d" % l, [128, 8, 2, 16], F32)
                    AJn = sb(L1, "AJn%d" % l, [128, 8, 16], F32)
                    BD = sb(L1, "BD%d" % l, [128, 4, 8, 128], BF16)
                    BL = sb(L1, "BL%d" % l, [128, 4, 8, 2, 128], BF16)
                    CA = sb(L1, "CA%d" % l, [128, 8, 2, 16, 32], BF16)
                    for kc in range(8):
                        S.dma(winb[:, kc, :], w_in[l, kc * 128:(kc + 1) * 128, :], writes=["winb"], queue="pool")
                    S.dma(wglub[:], w_glu[l].rearrange("(kc p) n -> p kc n", p=128), writes=["wglub"], queue="pool")
                    S.dma(gbc[:], ln_g[l].partition_broadcast(128), writes=["gbc"])
                    for tl, src, nm in ((dvec, ssm_d, "dvec%d" % l), (nbg, b_glu, "nbg%d" % l), (gssm, g_ssm, "gssm%d" % l)):
                        S.dma(tl[:], src[l].rearrange("(c p) -> p c", p=128), writes=[nm], allow_slow_non_contiguous=True)
                    ts("dve", nbg[:], nbg[:], -1.0, ALU.mult, ["nbg%d" % l], ["nbg%d" % l])

                    with contextlib.ExitStack() as SU:
                        k_ = "su"
                        aqr = sb(SU, "aqr%d" % l, [128, 16], F32); aqi = sb(SU, "aqi%d" % l, [128, 16], F32)
                        dtq = sb(SU, "dtq%d" % l, [128, 16], F32)
                        adt = sb(SU, "adt%d" % l, [128, 16], F32); ang = sb(SU, "ang%d" % l, [128, 16], F32)
                        mag = sb(SU, "mag%d" % l, [128, 16, 16], F32)
                        ph = sb(SU, "ph%d" % l, [128, 2, 16, 16], F32)
                        pht = sb(SU, "pht%d" % l, [128, 2, 16, 16], F32)
                        phi = sb(SU, "phi%d" % l, [128, 2, 16, 16], I32)
                        sc = sb(SU, "sc%d" % l, [128, 2, 16, 16], F32)
                        t1 = sb(SU, "t1%d" % l, [128, 16], F32); t2 = sb(SU, "t2%d" % l, [128, 16], F32)
                        t3 = sb(SU, "t3%d" % l, [128, 16], F32)
                        fr = sb(SU, "fr%d" % l, [128, 16], F32); fi = sb(SU, "fi%d" % l, [128, 16], F32)
                        Bq0r = sb(SU, "Bq0r%d" % l, [128, 16, 32], F32); Bq0i = sb(SU, "Bq0i%d" % l, [128, 16, 32], F32)
                        Bbr = sb(SU, "Bbr%d" % l, [128, 16, 32], F32); Bbi = sb(SU, "Bbi%d" % l, [128, 16, 32], F32)
                        nBbi = sb(SU, "nBbi%d" % l, [128, 16, 32], F32)
                        u1 = sb(SU, "u1%d" % l, [128, 16, 32], F32); u2 = sb(SU, "u2%d" % l, [128, 16, 32], F32)
                        Yr = sb(SU, "Yr%d" % l, [128, 16, 32], F32); Yi = sb(SU, "Yi%d" % l, [128, 16, 32], F32)
                        Zr = sb(SU, "Zr%d" % l, [32, 16, 128], F32); Zi = sb(SU, "Zi%d" % l, [32, 16, 128], F32)
                        CTr = sb(SU, "CTr%d" % l, [128, 16, 32], F32); CTi = sb(SU, "CTi%d" % l, [128, 16, 32], F32)
                        Xr = sb(SU, "Xr%d" % l, [128, 9, 16, 32], F32); Xi = sb(SU, "Xi%d" % l, [128, 9, 16, 32], F32)
                        pss = SU.enter_context(nc.psum_tensor("pss%d" % l, [128, 4, 128], F32))
                        psc = SU.enter_context(nc.psum_tensor("psc%d" % l, [128, 16, 32], F32))
                        K = [k_]
                        S.dma(aqr[:], a_re[l].rearrange("(pr g2) p -> (g2 p) pr", g2=2), writes=K, allow_slow_non_contiguous=True)
                        S.dma(aqi[:], a_im[l].rearrange("(pr g2) p -> (g2 p) pr", g2=2), writes=K, allow_slow_non_contiguous=True)
                        mset("dve", Bq0r[:], 0.0, K); mset("dve", Bq0i[:], 0.0, K)
                        mset("dve", Zr[:], 0.0, K); mset("dve", Zi[:], 0.0, K)
                        for g2 in range(2):
                            S.dma(dtq[g2 * 64:(g2 + 1) * 64, :], log_dt[l].rearrange("(pr g2) -> g2 pr", g2=2)[g2].partition_broadcast(64), reads=K, writes=K, allow_slow_non_contiguous=True)
                            S.dma(Bq0r[g2 * 64:(g2 + 1) * 64, :, g2 * 16:(g2 + 1) * 16], b_re[l].rearrange("(pr g2) p c -> g2 p pr c", g2=2)[g2], reads=K, writes=K)
                            S.dma(Bq0i[g2 * 64:(g2 + 1) * 64, :, g2 * 16:(g2 + 1) * 16], b_im[l].rearrange("(pr g2) p c -> g2 p pr c", g2=2)[g2], reads=K, writes=K)
                            S.dma(Zr[g2 * 16:(g2 + 1) * 16, :, g2 * 64:(g2 + 1) * 64], c_re[l].rearrange("(pr g2) c p -> g2 c pr p", g2=2)[g2], reads=K, writes=K)
                            S.dma(Zi[g2 * 16:(g2 + 1) * 16, :, g2 * 64:(g2 + 1) * 64], c_im[l].rearrange("(pr g2) c p -> g2 c pr p", g2=2)[g2], reads=K, writes=K)
                        act(dtq[:], dtq[:], AF.Exp, K, K)
                        tt("dve", adt[:], aqr[:], dtq[:], ALU.mult, K, K)
                        tt("dve", ang[:], aqi[:], dtq[:], ALU.mult, K, K)
                        for di, dl in enumerate(list(range(9)) + [16, 24, 32, 40, 48, 56, 64]):
                            act(mag[:, di, :], adt[:], AF.Exp, K, K, scale=float(dl))
                            ts("dve", ph[:, 0, di, :], ang[:], float(dl), ALU.mult, K, K)
                            ts("dve", ph[:, 1, di, :], ang[:], float(dl), ALU.mult, K, K, s2=math.pi / 2, op1=ALU.add)
                        ts("dve", pht[:], ph[:], 1.0 / TWO_PI, ALU.mult, K, K)
                        cp("dve", phi[:], pht[:], K, K)
                        cp("dve", pht[:], phi[:], K, K)
                        stt(ph[:], pht[:], -TWO_PI, ph[:], ALU.mult, ALU.add, K, K)
                        ts("dve", ph[:], ph[:], -math.pi, ALU.max, K, K, s2=math.pi, op1=ALU.min)
                        act(sc[:], ph[:], AF.Sin, K, K)
                        chk(1)
                        tt("dve", PWi[:], mag[:], sc[:, 0], ALU.mult, K, ["PW"])
                        tt("dve", PWr[:], mag[:], sc[:, 1], ALU.mult, K, ["PW"])
                        cp("dve", A1[:, 0, :], PWr[:, 8, :], ["PW"], ["A8"]); cp("dve", A1[:, 1, :], PWr[:, 8, :], ["PW"], ["A8"])
                        cp("dve", Aip[:], PWi[:, 8, :], ["PW"], ["A8"])
                        ts("dve", Ain[:], PWi[:, 8, :], -1.0, ALU.mult, ["PW"], ["A8"])
                        cp("dve", AJ1[:, :, 0, :], PWr[:, 8:16, :], ["PW"], ["A8"]); cp("dve", AJ1[:, :, 1, :], PWr[:, 8:16, :], ["PW"], ["A8"])
                        ts("dve", AJn[:], PWi[:, 8:16, :], -1.0, ALU.mult, ["PW"], ["A8"])
                        tt("dve", t1[:], aqr[:], aqr[:], ALU.mult, K, K)
                        tt("dve", t2[:], aqi[:], aqi[:], ALU.mult, K, K)
                        tt("dve", t1[:], t1[:], t2[:], ALU.add, K, K)
                        recip(t1[:], t1[:], K, K)
                        ts("dve", t2[:], PWr[:, 1, :], -1.0, ALU.add, K + ["PW"], K)
                        tt("dve", fr[:], t2[:], aqr[:], ALU.mult, K, K)
                        tt("dve", t3[:], PWi[:, 1, :], aqi[:], ALU.mult, K + ["PW"], K)
                        tt("dve", fr[:], fr[:], t3[:], ALU.add, K, K)
                        tt("dve", fr[:], fr[:], t1[:], ALU.mult, K, K)
                        tt("dve", fi[:], PWi[:, 1, :], aqr[:], ALU.mult, K + ["PW"], K)
                        tt("dve", t3[:], t2[:], aqi[:], ALU.mult, K, K)
                        tt("dve", fi[:], fi[:], t3[:], ALU.subtract, K, K)
                        tt("dve", fi[:], fi[:], t1[:], ALU.mult, K, K)
                        bc = lambda ap: ap.unsqueeze(2).to_broadcast([128, 16, 32])
                        tt("dve", u1[:], Bq0r[:], bc(fr[:]), ALU.mult, K, K)
                        tt("dve", u2[:], Bq0i[:], bc(fi[:]), ALU.mult, K, K)
                        tt("dve", Bbr[:], u1[:], u2[:], ALU.subtract, K, K)
                        tt("dve", u1[:], Bq0i[:], bc(fr[:]), ALU.mult, K, K)
                        tt("dve", u2[:], Bq0r[:], bc(fi[:]), ALU.mult, K, K)
                        tt("dve", Bbi[:], u1[:], u2[:], ALU.add, K, K)
                        ts("dve", nBbi[:], Bbi[:], -1.0, ALU.mult, K, K)
                        for (Z, CT) in ((Zr, CTr), (Zi, CTi)):
                            for pr in range(16):
                                tp(psc[:, pr, :], Z[:, pr, :], identf[0:32, 0:32], K + ["identf"], ["psc"])
                            cp("dve", CT[:], psc[:], ["psc"], K)
                        for dl in range(9):
                            pr_ = bc(PWr[:, dl, :]); pi_ = bc(PWi[:, dl, :])
                            tt("dve", u1[:], CTr[:], pr_, ALU.mult, K + ["PW"], K)
                            tt("dve", u2[:], CTi[:], pi_, ALU.mult, K + ["PW"], K)
                            tt("dve", Xr[:, dl], u1[:], u2[:], ALU.subtract, K, K)
                            tt("dve", u1[:], CTr[:], pi_, ALU.mult, K + ["PW"], K)
                            tt("dve", u2[:], CTi[:], pr_, ALU.mult, K + ["PW"], K)
                            tt("dve", Xi[:, dl], u1[:], u2[:], ALU.add, K, K)
                        for tau in range(8):
                            cp("dve", CA[:, tau, 0, :, :], Xr[:, tau + 1], K, ["CA"])
                            ts("dve", CA[:, tau, 1, :, :], Xi[:, tau + 1], -1.0, ALU.mult, K, ["CA"])
                        for tau in range(8):
                            pr_ = bc(PWr[:, 7 - tau, :]); pi_ = bc(PWi[:, 7 - tau, :])
                            tt("dve", u1[:], Bbr[:], pr_, ALU.mult, K + ["PW"], K)
                            tt("dve", u2[:], Bbi[:], pi_, ALU.mult, K + ["PW"], K)
                            tt("dve", Yr[:], u1[:], u2[:], ALU.subtract, K, K)
                            tt("dve", u1[:], Bbr[:], pi_, ALU.mult, K + ["PW"], K)
                            tt("dve", u2[:], Bbi[:], pr_, ALU.mult, K + ["PW"], K)
                            tt("dve", Yi[:], u1[:], u2[:], ALU.add, K, K)
                            for ri, Y in ((0, Yr), (1, Yi)):
                                for ft in range(4):
                                    tp(pss[:, ft, :], Y[:, ft * 4:(ft + 1) * 4, :].rearrange("p a b -> p (a b)"), identf[:], K + ["identf"], ["pss"])
                                cp("dve", BL[:, :, tau, ri, :], pss[:], ["pss"], ["BL"])
                        for dl in range(8):
                            for ft in range(4):
                                fl = lambda t_: t_[:, ft * 4:(ft + 1) * 4, :].rearrange("p a b -> p (a b)")
                                mm(pss[:, ft, :], fl(Bbr), Xr[:, dl, ft * 4:(ft + 1) * 4, :].rearrange("p a b -> p (a b)"), True, False, K, ["pss"])
                                mm(pss[:, ft, :], fl(nBbi), Xi[:, dl, ft * 4:(ft + 1) * 4, :].rearrange("p a b -> p (a b)"), False, True, K, ["pss"])
                            tt("dve", BD[:, :, dl, :], pss[:], mask32[:].unsqueeze(1).to_broadcast([128, 4, 128]), ALU.mult, ["pss", "mask32"], ["BD"])
                    chk(2)
                    S.barrier()

                    with contextlib.ExitStack() as W1:
                        xt = [sb(W1, "xt%d_%d" % (l, i), [128, D], F32) for i in range(2)]
                        ssq = sb(W1, "ssq%d" % l, [128, 4], F32)
                        hn = sb(W1, "hn%d" % l, [128, D], BF16)
                        junk = hn
                        hnT = sb(W1, "hnT%d" % l, [128, 8, 512], BF16)
                        stg = [sb(W1, "stg%d_%d" % (l, i), [128, 512], BF16) for i in range(3)]
                        ef = [sb(W1, "ef%d_%d" % (l, i), [128, 512], F32) for i in range(2)]
                        kvst = [sb(W1, "kvst%d_%d" % (l, i), [128, 1024], F32) for i in range(2)]
                        vst = [sb(W1, "vst%d_%d" % (l, i), [128, 512], BF16) for i in range(2)]
                        uTs = [sb(W1, "uT%d_%d" % (l, i), [128, 4, 512], BF16) for i in range(2)]
                        gsTs = [sb(W1, "gsT%d_%d" % (l, i), [128, 4, 512], BF16) for i in range(2)]
                        Hall = sb(W1, "Hall%d" % l, [128, 65, 2, 16], F32)
                        Hbs = [sb(W1, "Hb%d_%d" % (l, i), [128, 2, 16, 64], BF16) for i in range(2)]
                        Hbns = [sb(W1, "Hbn%d_%d" % (l, i), [128, 2, 16, 64], BF16) for i in range(2)]
                        Pg = sb(W1, "Pg%d" % l, [128, 8, 2, 16], F32)
                        Qg = sb(W1, "Qg%d" % l, [128, 8, 2, 16], F32)
                        ysb = sb(W1, "ysb%d" % l, [128, 512], F32)
                        ygb = sb(W1, "ygb%d" % l, [128, 4, 512], BF16)
                        sy = sb(W1, "sy%d" % l, [128, 4, 512], F32)
                        sqb = sb(W1, "sqb%d" % l, [128, 4, 512], BF16)
                        rstd = ysb
                        sob = sqb
                        ptp = [W1.enter_context(nc.psum_tensor("ptp%d_%d" % (l, i), [128, 8, 128], BF16)) for i in range(1)]
                        ppj = [W1.enter_context(nc.psum_tensor("ppj%d_%d" % (l, i), [128, 512], F32)) for i in range(2)]
                        pS = [W1.enter_context(nc.psum_tensor("pS%d_%d" % (l, i), [128, 2, 4, 64], F32)) for i in range(4)]
                        py = [W1.enter_context(nc.psum_tensor("py%d_%d" % (l, i), [128, 64, 8], F32)) for i in range(1)]
                        cnt = dict(x=0, pj=0, stg=0, ef=0, kv=0, py=0)

                        def silu_evac(ps, pskey, out, outkey, N):
                            i = cnt["ef"] % 2; cnt["ef"] += 1
                            e_ = ef[i]; ek = "ef%d" % i
                            act(e_[:, :N], ps, AF.Exp, [pskey], [ek], scale=-1.0)
                            act(e_[:, :N], e_[:, :N], AF.Ln, [ek], [ek], bias=1.0)
                            act(e_[:, :N], e_[:, :N], AF.Exp, [ek], [ek], scale=-1.0)
                            tt("dve", out, e_[:, :N], ps, ALU.mult, [ek, pskey], [outkey])

                        for si, sq in enumerate(seqs):
                            NT, N, b = sq["NT"], sq["N"], sq["b"]
                            R = min(128, N); nsub = N // R; nch = N // 8
                            xsrc = sq["xin"] if l == 0 else (sq["xa"] if l % 2 == 1 else sq["xb"])
                            kout, vout = (kp, vp) if sq["kind"] == "p" else (ks, vs)
                            if sq["kind"] == "p":
                                mset("pool", Hall[:, 0], 0.0, ["Hall"])
                            else:
                                S.dma(Hall[:, 0, 0, :], sre[l, b].rearrange("(pr g2) p -> (g2 p) pr", g2=2), writes=["Hall"], chan="hld", allow_slow_non_contiguous=True)
                                S.dma(Hall[:, 0, 1, :], sim[l, b].rearrange("(pr g2) p -> (g2 p) pr", g2=2), writes=["Hall"], chan="hld", allow_slow_non_contiguous=True)
                            pref = set()

                            def front(ti, bi_):
                                t0 = ti * N
                                uT = uTs[bi_]; gsT = gsTs[bi_]; Hb = Hbs[bi_]; Hbn = Hbns[bi_]
                                uk = 'uT%d' % bi_; gk = 'gsT%d' % bi_; hbk = 'Hb%d' % bi_; hbnk = 'Hbn%d' % bi_
                                for sub in range(nsub):
                                    xi = cnt["x"] % 2; cnt["x"] += 1
                                    xk = "xt%d" % xi
                                    r0 = t0 + sub * R
                                    if (ti, sub) not in pref:
                                        S.dma(xt[xi][:R, :], xsrc[r0:r0 + R, :], writes=[xk])
                                    act(junk[:R, :], xt[xi][:R, :], AF.Square, [xk], ["hn", "ssq"], accum=ssq[:R, 0:1])
                                    act(ssq[:R, 1:2], ssq[:R, 0:1], AF.Ln, ["ssq"], ["ssq"], scale=1.0 / D, bias=EPS)
                                    act(ssq[:R, 2:3], ssq[:R, 1:2], AF.Exp, ["ssq"], ["ssq"], scale=-0.5)
                                    stt(hn[:R, :], xt[xi][:R, :], ssq[:R, 2:3], gbc[:R, :], ALU.mult, ALU.mult, [xk, "ssq", "gbc"], ["hn"])
                                    pi_ = 0; pk = "ptp%d" % pi_
                                    for kc in range(8):
                                        tp(ptp[pi_][:, kc, :R], hn[:R, kc * 128:(kc + 1) * 128], identb[:R, :R], ["hn", "identb"], [pk])
                                    cp("dve", hnT[:, :, sub * R:sub * R + R], ptp[pi_][:, :, :R], [pk], ["hnT"])
                                if nsub == 4 and ti + 1 < NT // N:
                                    for s2 in range(2):
                                        x2 = (cnt["x"] + s2) % 2
                                        S.dma(xt[x2][:R, :], xsrc[t0 + N + s2 * R:t0 + N + s2 * R + R, :], writes=["xt%d" % x2])
                                        pref.add((ti + 1, s2))
                                chk(3)
                                for mt in list(range(0, 8)) + list(range(12, 24)):
                                    pj = cnt["pj"] % 2; cnt["pj"] += 1
                                    pjk = "ppj%d" % pj
                                    for kc in range(8):
                                        mm(ppj[pj][:, :N], winb[:, kc, mt * 128:(mt + 1) * 128], hnT[:, kc, :N], kc == 0, kc == 7, ["winb", "hnT"], [pjk])
                                    grp, ft = mt // 4, mt % 4
                                    if grp in (0, 1):
                                        sg = cnt["stg"] % 3; cnt["stg"] += 1
                                        sk = "stg%d" % sg
                                        if grp == 0:
                                            act(stg[sg][:, :N], ppj[pj][:, :N], AF.Copy, [pjk], [sk], scale=0.125)
                                        else:
                                            act(stg[sg][:, :N], ppj[pj][:, :N], AF.Copy, [pjk], [sk])
                                        dst = sq["qT"] if grp == 0 else sq["kT"]
                                        S.dma(dst[ft * 128:(ft + 1) * 128, t0:t0 + N], stg[sg][:, :N], reads=[sk], chan="st_" + sk)
                                    elif grp == 3:
                                        sg = cnt["stg"] % 3; cnt["stg"] += 1
                                        sk = "stg%d" % sg
                                        silu_evac(ppj[pj][:, :N], pjk, stg[sg][:, :N], sk, N)
                                        S.dma(sq["gaT"][ft * 128:(ft + 1) * 128, t0:t0 + N], stg[sg][:, :N], reads=[sk], chan="st_" + sk)
                                    elif grp == 4:
                                        act(uT[:, ft, :N], ppj[pj][:, :N], AF.Copy, [pjk], [uk])
                                    else:
                                        silu_evac(ppj[pj][:, :N], pjk, gsT[:, ft, :N], gk, N)
                                chk(4)
                                for sub in range(nsub):
                                    ki = cnt["kv"] % 2; cnt["kv"] += 1
                                    kk = "kvst%d" % ki; vk = "vst%d" % ki
                                    r0 = t0 + sub * R
                                    for half in range(2):
                                        pj = cnt["pj"] % 2; cnt["pj"] += 1
                                        pjk = "ppj%d" % pj
                                        for kc in range(8):
                                            mm(ppj[pj][:R, :], hnT[:, kc, sub * R:sub * R + R], winb[:, kc, 512 + half * 512:1024 + half * 512], kc == 0, kc == 7, ["winb", "hnT"], [pjk])
                                        if os.environ.get("KV", "abcde").find("d") >= 0:
                                            cp("dve", kvst[ki][:R, half * 512:(half + 1) * 512], ppj[pj][:R, :], [pjk], [kk])
                                        if half == 1 and os.environ.get("KV", "abcde").find("e") >= 0:
                                            act(vst[ki][:R, :], kvst[ki][:R, 512:1024], AF.Copy, [kk], [vk])
                                    if os.environ.get("KV", "abc").find("a") >= 0:
                                        S.dma(kout[l, b, r0:r0 + R, :], kvst[ki][:R, 0:512], reads=[kk], chan="st_" + kk)
                                    if os.environ.get("KV", "abc").find("b") >= 0:
                                        S.dma(vout[l, b, r0:r0 + R, :], kvst[ki][:R, 512:1024], reads=[kk], chan="st_" + kk)
                                    if os.environ.get("KV", "abc").find("c") >= 0:
                                        S.dma(sq["vv"][r0:r0 + R, :], vst[ki][:R, :], reads=[vk], chan="st_" + vk)
                                chk(5)
                                uv = uT[:, :, :N].rearrange("p f (k t) -> p f k t", t=8)
                                for j in range(4):
                                    pk = "pS%d" % j
                                    lo = 64 if j == 3 else 32 * j
                                    for ri in range(2):
                                        for ft in range(4):
                                            for tau in range(8):
                                                mm(pS[j][:, ri, ft, :nch], BL[lo:32 * j + 32, ft, tau, ri, :], uv[lo:32 * j + 32, ft, :, tau], tau == 0, tau == 7, ["BL", uk], [pk])
                                    cp("dve", Hall[:, 1:nch + 1, :, j::4].rearrange("p k r f -> p r f k"), pS[j][:, :, :, :nch], [pk], ["Hall"])
                                tt("pool", Hall[:, 1:nch + 1, :, 3::4], Hall[:, 1:nch + 1, :, 3::4], Hall[:, 1:nch + 1, :, 2::4], ALU.subtract, ["Hall"], ["Hall"])
                                chk(6)
                                def cmacc(eng, dst, X, a1, an, ap_, g):
                                    if g:
                                        P_, Q_ = Pg[:, :g], Qg[:, :g]
                                        Q0, Q1, X0, X1 = Qg[:, :g, 0, :], Qg[:, :g, 1, :], X[:, :, 0, :], X[:, :, 1, :]
                                    else:
                                        P_, Q_ = Pg[:, 0], Qg[:, 0]
                                        Q0, Q1, X0, X1 = Qg[:, 0, 0, :], Qg[:, 0, 1, :], X[:, 0, :], X[:, 1, :]
                                    tt(eng, P_, X, a1, ALU.mult, ["Hall", "A8"], ["Pg"])
                                    tt(eng, Q0, X1, an, ALU.mult, ["Hall", "A8"], ["Qg"])
                                    tt(eng, Q1, X0, ap_, ALU.mult, ["Hall", "A8"], ["Qg"])
                                    tt(eng, P_, P_, Q_, ALU.add, ["Pg", "Qg"], ["Pg"])
                                    tt(eng, dst, dst, P_, ALU.add, ["Pg", "Hall"], ["Hall"])

                                if nch == 64:
                                    G = 8
                                    Hv = Hall[:, 1:65].rearrange("p (g m) r c -> p g m r c", m=8)
                                    Cv = Hall[:, 0:64].rearrange("p (g m) r c -> p g m r c", m=8)[:, :, 0]
                                    b4 = lambda ap: ap.unsqueeze(1).to_broadcast([128, G, 2, 16])
                                    b3 = lambda ap: ap.unsqueeze(1).to_broadcast([128, G, 16])
                                    for j in range(1, 8):
                                        cmacc("pool", Hv[:, :, j], Hv[:, :, j - 1], b4(A1[:]), b3(Ain[:]), b3(Aip[:]), G)
                                    for g in range(G):
                                        cmacc("pool", Hall[:, 8 * (g + 1)], Hall[:, 8 * g], AJ1[:, 7], AJn[:, 7], PWi[:, 15, :], 0)
                                    for j in range(7):
                                        cmacc("pool", Hv[:, :, j], Cv, b4(AJ1[:, j]), b3(AJn[:, j]), b3(PWi[:, 8 + j, :]), G)
                                else:
                                    for k in range(nch):
                                        cmacc("pool", Hall[:, k + 1], Hall[:, k], A1[:], Ain[:], Aip[:], 0)
                                for ri in range(2):
                                    cp("pool", Hb[:, ri, :, :nch], Hall[:, 0:nch, ri, :].rearrange("p k r -> p r k"), ["Hall"], [hbk])
                                ts("pool", Hbn[:, :, :, :nch], Hb[:, :, :, :nch], -1.0, ALU.mult, [hbk], [hbnk])
                                if ti == NT // N - 1:
                                    ho_r, ho_i = (hrp, hip) if sq["kind"] == "p" else (hrs, his)
                                    S.dma(ho_r[l, b].rearrange("(pr g2) p -> (g2 p) pr", g2=2), Hall[:, nch, 0, :], reads=["Hall"], chan="hst", allow_slow_non_contiguous=True)
                                    S.dma(ho_i[l, b].rearrange("(pr g2) p -> (g2 p) pr", g2=2), Hall[:, nch, 1, :], reads=["Hall"], chan="hst", allow_slow_non_contiguous=True)
                                else:
                                    cp("pool", Hall[:, 0], Hall[:, nch], ["Hall"], ["Hall"])

                            def back(ti, bi_):
                                t0 = ti * N
                                uT = uTs[bi_]; gsT = gsTs[bi_]; Hb = Hbs[bi_]; Hbn = Hbns[bi_]
                                uk = 'uT%d' % bi_; gk = 'gsT%d' % bi_; hbk = 'Hb%d' % bi_; hbnk = 'Hbn%d' % bi_
                                uv = uT[:, :, :N].rearrange("p f (k t) -> p f k t", t=8)
                                chk(7)
                                for ft in range(4):
                                    yi = 0
                                    yk = "py%d" % yi
                                    for dl in range(8):
                                        mm(py[yi][:, :nch, dl:8], BD[:, ft, dl, :], uv[:, ft, :, 0:8 - dl], dl == 0, False, ["BD", uk], [yk])
                                    for tau in range(8):
                                        for j in range(4):
                                            pr = ft * 4 + j
                                            for ri in range(2):
                                                lastm = (tau == 7 and j == 3 and ri == 1)
                                                if j < 3:
                                                    mm(py[yi][32 * j:32 * j + 32, :nch, tau], CA[:, tau, ri, pr, :], Hb[:, ri, pr, :nch], False, False, ["CA", hbk], [yk])
                                                else:
                                                    mm(py[yi][64:128, :nch, tau], CA[:, tau, ri, pr - 1:pr + 1, :].rearrange("p a b -> p (a b)"), Hb[:, ri, pr, :nch], False, False, ["CA", hbk], [yk])
                                                    mm(py[yi][64:96, :nch, tau], CA[:, tau, ri, pr - 1, :], Hbn[:, ri, pr, :nch], False, lastm, ["CA", hbnk], [yk])
                                    yv = py[yi][:, :nch, :]
                                    ysk = "ys%d" % ft
                                    ysf = sy[:, ft, :N]
                                    ysv = ysf.rearrange("p (k t) -> p k t", t=8)
                                    gi = cnt["ef"] % 2; cnt["ef"] += 1
                                    tg_ = ef[gi]; tk = "ef%d" % gi
                                    stt(ysv, uv[:, ft], dvec[:, ft:ft + 1], yv, ALU.mult, ALU.add, [uk, "dvec%d" % l, yk], [ysk])
                                    ts("dve", ysf, ysf, -7.0, ALU.max, [ysk], [ysk])
                                    tt("dve", tg_[:, :N], ysf, ysf, ALU.mult, [ysk], [tk])
                                    ts("dve", tg_[:, :N], tg_[:, :N], 0.044715, ALU.mult, [tk], [tk], s2=1.0, op1=ALU.add)
                                    tt("dve", tg_[:, :N], tg_[:, :N], ysf, ALU.mult, [tk, ysk], [tk])
                                    act(tg_[:, :N], tg_[:, :N], AF.Exp, [tk], [tk], scale=-1.5957691216057308)
                                    act(tg_[:, :N], tg_[:, :N], AF.Ln, [tk], [tk], bias=1.0)
                                    act(tg_[:, :N], tg_[:, :N], AF.Exp, [tk], [tk], scale=-1.0)
                                    tt("dve", ygb[:, ft, :N], ysf, tg_[:, :N], ALU.mult, [tk, ysk], ["ygb"])
                                chk(8)
                                for mt in range(4):
                                    pj = cnt["pj"] % 2; cnt["pj"] += 1
                                    pjk = "ppj%d" % pj
                                    for kc in range(4):
                                        mm(ppj[pj][:, :N], wglub[:, kc, mt * 128:(mt + 1) * 128], ygb[:, kc, :N], kc == 0, kc == 3, ["wglub", "ygb"], [pjk])
                                    gi = cnt["ef"] % 2; cnt["ef"] += 1
                                    tg_ = ef[gi]; tk = "ef%d" % gi
                                    act(tg_[:, :N], ppj[pj][:, :N], AF.Exp, [pjk, "nbg%d" % l], [tk], scale=-1.0, bias=nbg[:, mt:mt + 1])
                                    act(tg_[:, :N], tg_[:, :N], AF.Ln, [tk], [tk], bias=1.0)
                                    act(tg_[:, :N], tg_[:, :N], AF.Exp, [tk], [tk], scale=-1.0)
                                    tt("dve", sy[:, mt, :N], ygb[:, mt, :N], tg_[:, :N], ALU.mult, [tk, "ygb", "ys%d" % mt], ["sy"])
                                    tt("pool", sqb[:, mt, :N], sy[:, mt, :N], sy[:, mt, :N], ALU.mult, ["sy"], ["sqb"])
                                pj = cnt["pj"] % 2; cnt["pj"] += 1
                                pjk = "ppj%d" % pj
                                for kc in range(4):
                                    mm(ppj[pj][:, :N], onesb[:], sqb[:, kc, :N], kc == 0, kc == 3, ["onesb", "sqb"], [pjk])
                                act(rstd[:, :N], ppj[pj][:, :N], AF.Ln, [pjk], ["ysb"], scale=1.0 / 512, bias=EPS)
                                act(rstd[:, :N], rstd[:, :N], AF.Exp, ["ysb"], ["ysb"], scale=-0.5)
                                for ft in range(4):
                                    tt("dve", sy[:, ft, :N], sy[:, ft, :N], rstd[:, :N], ALU.mult, ["sy", "ysb"], ["sy"])
                                    stt(sob[:, ft, :N], sy[:, ft, :N], gssm[:, ft:ft + 1], gsT[:, ft, :N], ALU.mult, ALU.mult, ["sy", "gssm%d" % l, gk], ["sqb"])
                                S.dma(sq["soT"].rearrange("(f p) t -> p f t", p=128)[:, :, t0:t0 + N], sob[:, :, :N], reads=["sqb"], chan="st_sob")

                            ntile = NT // N
                            front(0, 0)
                            for ti in range(ntile):
                                if ti + 1 < ntile:
                                    front(ti + 1, (ti + 1) % 2)
                                back(ti, ti % 2)
                    S.barrier()

                chk(10)
                with contextlib.ExitStack() as L2:
                    woutb = sb(L2, "woutb%d" % l, [128, 8, D], BF16)
                    gatt = sb(L2, "gatt%d" % l, [128, 4], F32)
                    NKmax = max(T, PL + TS)
                    nblk_max = (NKmax + 127) // 128
                    kTs = sb(L2, "kTs%d" % l, [128, 4, nblk_max * 128], BF16)
                    vsb = sb(L2, "vsb%d" % l, [128, nblk_max, 512], BF16)
                    qTt = sb(L2, "qTt%d" % l, [128, 4, 512], BF16)
                    gaTt = sb(L2, "gaTt%d" % l, [128, 4, 512], BF16)
                    catT = sb(L2, "catT%d" % l, [128, 8, 512], BF16)
                    ckst = [sb(L2, "ckst%d_%d" % (l, i), [128, 512], BF16) for i in range(2)]
                    e_sb = [sb(L2, "e_sb%d_%d" % (l, i), [128, 512], F32) for i in range(3)]
                    sp_sb = [sb(L2, "sp_sb%d_%d" % (l, i), [128, 512], BF16) for i in range(3)]
                    w_sb = [sb(L2, "w_sb%d_%d" % (l, i), [128, 512], BF16) for i in range(3)]
                    spsum = [sb(L2, "spsum%d_%d" % (l, i), [128, 512], BF16) for i in range(4)]
                    att = sb(L2, "att%d" % l, [128, 4, 512], F32)
                    asq = sb(L2, "asq%d" % l, [128, 4, 512], BF16)
                    rstd2 = sb(L2, "rstd2%d" % l, [128, 512], F32)
                    xt2 = [sb(L2, "xt2%d_%d" % (l, i), [128, D], F32) for i in range(2)]
                    xn = [sb(L2, "xn%d_%d" % (l, i), [128, D], F32) for i in range(2)]
                    junk2 = sb(L2, "junk2%d" % l, [128, D], BF16)
                    ssq2 = sb(L2, "ssq2%d" % l, [128, 4], F32)
                    pA = [L2.enter_context(nc.psum_tensor("pA%d_%d" % (l, i), [128, 512], F32)) for i in range(5)]
                    pAtt = [L2.enter_context(nc.psum_tensor("pAtt%d_%d" % (l, i), [128, 512], F32)) for i in range(1)]
                    pO = [L2.enter_context(nc.psum_tensor("pO%d_%d" % (l, i), [128, 512], F32)) for i in range(2)]
                    ptk = pO[0][:].bitcast(BF16)
                    for kc in range(8):
                        S.dma(woutb[:, kc, :], w_out[l, kc * 128:(kc + 1) * 128, :], writes=["woutb"], queue="pool")
                    S.dma(gatt[:], g_att[l].rearrange("(c p) -> p c", p=128), writes=["gatt"], allow_slow_non_contiguous=True)
                    c2 = dict(x=0, o=0, ck=0, blk=0)
                    for si, sq in enumerate(seqs):
                        NT, N, b, past = sq["NT"], sq["N"], sq["b"], sq["past"]
                        R = min(128, N); nsub = N // R
                        xsrc = sq["xin"] if l == 0 else (sq["xa"] if l % 2 == 1 else sq["xb"])
                        xdst = sq["xa"] if l % 2 == 0 else sq["xb"]
                        npast = past // 128
                        for pb in range(npast):
                            ci = c2["ck"] % 2; c2["ck"] += 1
                            ckk = "ckst%d" % ci
                            S.dma(ckst[ci][:], ck[l, b, pb * 128:(pb + 1) * 128, :], writes=[ckk], queue="pool")
                            for hp in range(4):
                                tp(ptk[:, hp * 128:(hp + 1) * 128], ckst[ci][:, hp * 128:(hp + 1) * 128], identb[:], [ckk, "identb"], ["pO0"])
                            cp("dve", kTs[:, :, pb * 128:(pb + 1) * 128], ptk[:, 0:512].rearrange("p (h s) -> p h s", h=4), ["pO0"], ["kTs"])
                        if npast:
                            S.dma(vsb[:, 0:npast, :], cv[l, b].rearrange("(n p) f -> p n f", p=128), writes=["vsb"], queue="pool")
                        S.dma(kTs[:, :, past:past + NT], sq["kT"].rearrange("(h p) t -> p h t", p=128), writes=["kTs"])
                        if NT >= 128:
                            S.dma(vsb[:, npast:npast + NT // 128, :], sq["vv"].rearrange("(n p) f -> p n f", p=128), writes=["vsb"])
                        else:
                            S.dma(vsb[:NT, npast, :], sq["vv"], writes=["vsb"])
                        for ti in range(NT // N):
                            t0 = ti * N
                            S.dma(qTt[:, :, :N], sq["qT"].rearrange("(h p) t -> p h t", p=128)[:, :, t0:t0 + N], writes=["qTt"])
                            S.dma(gaTt[:, :, :N], sq["gaT"].rearrange("(h p) t -> p h t", p=128)[:, :, t0:t0 + N], writes=["gaTt"])
                            S.dma(catT[:, 4:8, :N], sq["soT"].rearrange("(h p) t -> p h t", p=128)[:, :, t0:t0 + N], writes=["catT"])
                            chk(11)
                            blocks = []
                            if sq["kind"] == "p":
                                for j in reversed(range(nsub)):
                                    blocks.append((t0 + j * 128, 128, (t0 + j * 128) // 128, masks[:, j, :N]))
                                for kb in reversed(range(t0 // 128)):
                                    blocks.append((kb * 128, 128, kb, None))
                            else:
                                blocks.append((past, NT, npast, masks_s[:, :]))
                                for kb in reversed(range(npast)):
                                    blocks.append((kb * 128, 128, kb, None))
                            work = [(h, bi) for h in range(8) for bi in range(len(blocks))]

                            def info(idx):
                                h, bi = work[idx]
                                s0, Rk, vb, mk = blocks[bi]
                                return h, bi, s0, Rk, vb, mk, h // 2, 64 * (h % 2)

                            def sA(idx):
                                h, bi, s0, Rk, vb, mk, hp, base = info(idx)
                                sl = idx % 5
                                mm(pA[sl][:Rk, :N], kTs[base:base + 64, hp, s0:s0 + Rk], qTt[base:base + 64, hp, :N], True, False, ["kTs", "qTt"], ["pA%d" % sl])

                            def sB(idx):
                                h, bi, s0, Rk, vb, mk, hp, base = info(idx)
                                sl = idx % 5
                                act(e_sb[idx % 3][:Rk, :N], pA[sl][:Rk, :N], AF.Exp, ["pA%d" % sl], ["e_sb%d" % (idx % 3)])

                            def sC(idx):
                                h, bi, s0, Rk, vb, mk, hp, base = info(idx)
                                spk = "sp_sb%d" % (idx % 3)
                                act(sp_sb[idx % 3][:Rk, :N], e_sb[idx % 3][:Rk, :N], AF.Ln, ["e_sb%d" % (idx % 3)], [spk], bias=1.0)
                                if mk is not None:
                                    tt("pool", sp_sb[idx % 3][:Rk, :N], sp_sb[idx % 3][:Rk, :N], mk, ALU.mult, [spk, "masks"], [spk])

                            def sD(idx):
                                h, bi, s0, Rk, vb, mk, hp, base = info(idx)
                                sl = idx % 5
                                Ak, spk = "pA%d" % sl, "sp_sb%d" % (idx % 3)
                                spt = sp_sb[idx % 3]
                                first, lastb = (bi == 0), (bi == len(blocks) - 1)
                                mm(pA[sl][:Rk, :N], negU[:Rk, :Rk], spt[:Rk, :N], False, first, [spk, "negU"], [Ak])
                                if not first:
                                    prv = "spsum%d" % ((idx - 1) % 4)
                                    mm(pA[sl][:Rk, :N], negones[:, :Rk], spsum[(idx - 1) % 4][:, :N], False, True, [prv, "negones"], [Ak])
                                if not lastb:
                                    cur, prv = "spsum%d" % (idx % 4), "spsum%d" % ((idx - 1) % 4)
                                    if first:
                                        if Rk < 128:
                                            mset("dve", spsum[idx % 4][:, :N], 0.0, [cur])
                                        cp("dve", spsum[idx % 4][:Rk, :N], spt[:Rk, :N], [spk], [cur])
                                    else:
                                        tt("dve", spsum[idx % 4][:, :N], spsum[(idx - 1) % 4][:, :N], spt[:, :N], ALU.add, [spk, prv], [cur])

                            def sE(idx):
                                h, bi, s0, Rk, vb, mk, hp, base = info(idx)
                                sl = idx % 5
                                wk = "w_sb%d" % (idx % 3)
                                act(w_sb[idx % 3][:Rk, :N], pA[sl][:Rk, :N], AF.Exp, ["pA%d" % sl], [wk])
                                if mk is not None:
                                    tt("pool", w_sb[idx % 3][:Rk, :N], w_sb[idx % 3][:Rk, :N], mk, ALU.mult, [wk, "masks"], [wk])

                            def sF(idx):
                                h, bi, s0, Rk, vb, mk, hp, base = info(idx)
                                wk = "w_sb%d" % (idx % 3)
                                atk = "pAtt_%d" % (h % 2)
                                first, lastb = (bi == 0), (bi == len(blocks) - 1)
                                mm(pAtt[0][base:base + 64, :N], vsb[:Rk, vb, h * 64:(h + 1) * 64], w_sb[idx % 3][:Rk, :N], first, lastb, [wk, "vsb"], [atk])
                                if lastb:
                                    cp("dve", att[base:base + 64, hp, :N], pAtt[0][base:base + 64, :N], [atk], ["att"])

                            nw = len(work)
                            stages = [sA, sB, sC, sD, sE, sF]
                            for it in range(nw + 5):
                                for d, fn in enumerate(stages):
                                    if 0 <= it - d < nw:
                                        fn(it - d)
                            chk(12)
                            for hp in range(4):
                                tt("pool", asq[:, hp, :N], att[:, hp, :N], att[:, hp, :N], ALU.mult, ["att"], ["asq"])
                            for hp in range(4):
                                mm(pO[1][:, :N], onesb[:], asq[:, hp, :N], hp == 0, hp == 3, ["onesb", "asq"], ["pO1"])
                            act(rstd2[:, :N], pO[1][:, :N], AF.Ln, ["pO1"], ["rstd2"], scale=1.0 / 512, bias=EPS)
                            act(rstd2[:, :N], rstd2[:, :N], AF.Exp, ["rstd2"], ["rstd2"], scale=-0.5)
                            for hp in range(4):
                                tt("dve", att[:, hp, :N], att[:, hp, :N], rstd2[:, :N], ALU.mult, ["att", "rstd2"], ["att"])
                                stt(catT[:, hp, :N], att[:, hp, :N], gatt[:, hp:hp + 1], gaTt[:, hp, :N], ALU.mult, ALU.mult, ["att", "gatt", "gaTt"], ["catT"])
                            chk(13)
                            for sub in range(nsub):
                                xi = c2["x"] % 2; c2["x"] += 1
                                xk, nk = "xt2%d" % xi, "xn%d" % xi
                                r0 = t0 + sub * R
                                S.dma(xt2[xi][:R, :], xsrc[r0:r0 + R, :], writes=[xk])
                                for half in range(2):
                                    oi = c2["o"] % 2; c2["o"] += 1
                                    ok = "pO%d" % oi
                                    for kc in range(8):
                                        mm(pO[oi][:R, :], catT[:, kc, sub * R:sub * R + R], woutb[:, kc, half * 512:(half + 1) * 512], kc == 0, kc == 7, ["catT", "woutb"], [ok])
                                    tt("dve", xn[xi][:R, half * 512:(half + 1) * 512], pO[oi][:R, :], xt2[xi][:R, half * 512:(half + 1) * 512], ALU.add, [ok, xk], [nk])
                                if not last:
                                    S.dma(xdst[r0:r0 + R, :], xn[xi][:R, :], reads=[nk], chan="st_" + nk)
                                else:
                                    act(junk2[:R, :], xn[xi][:R, :], AF.Square, [nk], ["junk2"])
                                    S.op("dve", lambda e: e.tensor_reduce(out=ssq2[:R, 0:1], in_=junk2[:R, :], axis=mybir.AxisListType.X, op=ALU.add), ["junk2"], ["ssq2"])
                                    act(ssq2[:R, 1:2], ssq2[:R, 0:1], AF.Ln, ["ssq2"], ["ssq2"], scale=1.0 / D, bias=EPS)
                                    act(ssq2[:R, 2:3], ssq2[:R, 1:2], AF.Exp, ["ssq2"], ["ssq2"], scale=-0.5)
                                    stt(xt2[xi][:R, :], xn[xi][:R, :], ssq2[:R, 2:3], fgbc[:R, :], ALU.mult, ALU.mult, [nk, "ssq2", "fgbc"], [xk])
                                    S.dma(sq["yout"][r0:r0 + R, :], xt2[xi][:R, :], reads=[xk], chan="st_" + xk)
                    S.barrier()
        except StopBuild:
            pass
        S._final = True
        S.barrier()
    return nc


_CACHE = {}


def _run(inputs, T, TS, PL, DEPTH):
    key = (T, TS, PL, DEPTH)
    if key not in _CACHE:
        _CACHE[key] = build(T, TS, PL, DEPTH)
    nc = _CACHE[key]
    f = lambda a: np.ascontiguousarray(np.asarray(a, dtype=np.float32))
    in_maps = []
    for c in range(8):
        sl = slice(2 * c, 2 * c + 2)
        m = {
            "xp": f(inputs["x_prompt"][sl]), "xs": f(inputs["x_sample"][sl]),
            "ck": f(np.asarray(inputs["cache_k"])[:, sl].reshape(DEPTH, 2, PL, 512)),
            "cv": f(np.asarray(inputs["cache_v"])[:, sl].reshape(DEPTH, 2, PL, 512)),
            "sre": f(np.asarray(inputs["state_ssm_re"])[:, sl]), "sim": f(np.asarray(inputs["state_ssm_im"])[:, sl]),
            "ln_g": f(inputs["ln_g"]), "w_in": f(inputs["w_in"]),
            "a_re": f(inputs["ssm_a_re"]), "a_im": f(inputs["ssm_a_im"]), "log_dt": f(inputs["ssm_log_dt"]),
            "b_re": f(inputs["ssm_b_re"]), "b_im": f(inputs["ssm_b_im"]),
            "c_re": f(inputs["ssm_c_re"]), "c_im": f(inputs["ssm_c_im"]),
            "ssm_d": f(inputs["ssm_d"]), "w_glu": f(inputs["w_glu"]), "b_glu": f(inputs["b_glu"]),
            "g_att": f(inputs["g_att"]), "g_ssm": f(inputs["g_ssm"]), "w_out": f(inputs["w_out"]),
            "final_g": f(inputs["final_g"]),
        }
        in_maps.append(m)
    res = run_bass_kernel_spmd(nc, in_maps, core_ids=list(range(8)))
    R = res.results
    cat0 = lambda k: np.concatenate([r[k] for r in R], axis=0)
    cat1 = lambda k: np.concatenate([r[k] for r in R], axis=1)
    y_p = cat0("yp"); y_s = cat0("ys")
    k_p = cat1("kp").reshape(DEPTH, 16, T, 8, 64); v_p = cat1("vp").reshape(DEPTH, 16, T, 8, 64)
    k_s = cat1("ks").reshape(DEPTH, 16, TS, 8, 64); v_s = cat1("vs").reshape(DEPTH, 16, TS, 8, 64)
    return (y_p, y_s, k_p, v_p, cat1("hrp"), cat1("hip"), k_s, v_s, cat1("hrs"), cat1("his"))


def kernel(**inputs):
    T = int(np.shape(inputs["x_prompt"])[1]); TS = int(np.shape(inputs["x_sample"])[1])
    PL = int(np.shape(inputs["cache_k"])[2]); DEPTH = int(np.shape(inputs["w_in"])[0])
    return _run(inputs, T, TS, PL, DEPTH)
```
